# Optimizing a Trainium2 kernel written in Bass

```python
import math
import jax
import jax.numpy as jnp
from jax import lax
import numpy as np

D_MODEL = 1024
BATCH = 2
SEQ = 8192
DEPTH = 2

GRID_W = 64
CTX_LEN = 256
FOUR_GROUPS = 4
FOUR_GROUP_W = D_MODEL // 16
FOUR_WIDTH = FOUR_GROUPS * FOUR_GROUP_W
RET_HEADS = 4
RET_HEAD_DIM = (3 * D_MODEL // 8) // 4
RET_WIDTH = RET_HEADS * RET_HEAD_DIM
NA_HEADS = 6
NA_HEAD_DIM = (3 * D_MODEL // 8) // 6
NA_WIDTH = NA_HEADS * NA_HEAD_DIM
PROJ_WIDTH = FOUR_WIDTH + 4 * RET_WIDTH + 3 * NA_WIDTH
SPLIT_IDX = (FOUR_WIDTH,
             FOUR_WIDTH + RET_WIDTH,
             FOUR_WIDTH + 2 * RET_WIDTH,
             FOUR_WIDTH + 3 * RET_WIDTH,
             FOUR_WIDTH + 4 * RET_WIDTH,
             FOUR_WIDTH + 4 * RET_WIDTH + NA_WIDTH,
             FOUR_WIDTH + 4 * RET_WIDTH + 2 * NA_WIDTH)
RET_CHUNK = 128
NA_WIN_ROWS = 8
NA_WIN_COLS = 16
NA_QBLOCK_COLS = 16
NA_KBLOCK_COLS = NA_QBLOCK_COLS + NA_WIN_COLS
D_FF = ((8 * D_MODEL // 3 + 255) // 256) * 256
ROPE_BASE = 10000.0
NORM_EPS = 1e-6
NEG_INF = -1e30

kernel_name = "hymba_fnet_retnet_natten_dit"


def rmsnorm(x, g):
    x32 = x.astype(jnp.float32)
    y = x32 * lax.rsqrt(jnp.mean(x32 * x32, axis=-1, keepdims=True) + NORM_EPS)
    return (y * g.astype(jnp.float32)).astype(x.dtype)


def ada_params(cvec, w, b):
    return jnp.split(jax.nn.silu(cvec) @ w + b, 6, axis=-1)


def swiglu(h, w1, w3, w2):
    return (jax.nn.silu(h @ w1) * (h @ w3)) @ w2


def rope_1d(x, pos):
    half = x.shape[-1] // 2
    inv = ROPE_BASE ** (-jnp.arange(half, dtype=jnp.float32) / half)
    ang = pos.astype(jnp.float32)[:, None] * inv[None, :]
    cos, sin = jnp.cos(ang), jnp.sin(ang)
    x1, x2 = x[..., :half], x[..., half:]
    return jnp.concatenate([x1 * cos - x2 * sin, x1 * sin + x2 * cos], axis=-1).astype(x.dtype)


def rope_2d(x, rows, cols):
    d = x.shape[-1] // 2
    return jnp.concatenate([rope_1d(x[..., :d], rows), rope_1d(x[..., d:], cols)], axis=-1)


def fourier_mix(u, w_four):
    B, L, _ = u.shape
    g = u.astype(jnp.float32).reshape(B, L, FOUR_GROUPS, FOUR_GROUP_W)
    f = jnp.fft.fft2(g, axes=(1, 3), norm="ortho").real
    return f.reshape(B, L, FOUR_WIDTH).astype(u.dtype) @ w_four


def retention_final_state(k, v, log_gamma):
    L = k.shape[2]
    w = jnp.exp((L - 1 - jnp.arange(L, dtype=jnp.float32))[None, :] * log_gamma[:, None])
    return jnp.einsum('bhld,bhle->bhde', k * w[None, :, :, None], v)


def retention_chunkwise(q, k, v, log_gamma, state0):
    B, H, L, dk = q.shape
    dv = v.shape[-1]
    C = RET_CHUNK
    N = L // C
    qc = q.reshape(B, H, N, C, dk)
    kc = k.reshape(B, H, N, C, dk)
    vc = v.reshape(B, H, N, C, dv)
    idx = jnp.arange(C, dtype=jnp.float32)
    diff = idx[:, None] - idx[None, :]
    intra_decay = jnp.where(diff >= 0, jnp.exp(jnp.maximum(diff, 0.0)[None] * log_gamma[:, None, None]), 0.0)
    s = jnp.einsum('bhnid,bhnjd->bhnij', qc, kc) * intra_decay[None, :, None]
    intra = jnp.einsum('bhnij,bhnje->bhnie', s, vc)
    q_decay = jnp.exp((idx + 1.0)[None, :] * log_gamma[:, None])
    k_decay = jnp.exp((C - 1.0 - idx)[None, :] * log_gamma[:, None])
    kv = jnp.einsum('bhnjd,bhnje->nbhde', kc * k_decay[None, :, None, :, None], vc)
    chunk_decay = jnp.exp(C * log_gamma)[None, :, None, None]

    def step(state, kv_n):
        return chunk_decay * state + kv_n, state

    _, s_prev = lax.scan(step, state0, kv)
    cross = jnp.einsum('bhnid,nbhde->bhnie', qc, s_prev) * q_decay[None, :, None, :, None]
    return (intra + cross).reshape(B, H, L, dv)


def bidirectional_retention(q_lat, k_lat, v_lat, q_ctx, k_ctx, v_ctx, log_gamma, with_ctx_out):
    flip = lambda t: t[:, :, ::-1]
    s_fwd = retention_final_state(k_ctx, v_ctx, log_gamma[0])
    s_bwd = retention_final_state(flip(k_ctx), flip(v_ctx), log_gamma[1])
    o_lat = (retention_chunkwise(q_lat, k_lat, v_lat, log_gamma[0], s_fwd)
             + flip(retention_chunkwise(flip(q_lat), flip(k_lat), flip(v_lat), log_gamma[1], s_bwd)))
    if not with_ctx_out:
        return o_lat, None
    zero = jnp.zeros_like(s_fwd)
    o_ctx = (retention_chunkwise(q_ctx, k_ctx, v_ctx, log_gamma[0], zero)
             + flip(retention_chunkwise(flip(q_ctx), flip(k_ctx), flip(v_ctx), log_gamma[1], zero)))
    return o_lat, o_ctx


def retention_readout(o, gate, gain):
    B, H, L, dv = o.shape
    o = o * lax.rsqrt(jnp.mean(o * o, axis=-1, keepdims=True) + NORM_EPS)
    o = o.transpose(0, 2, 1, 3).reshape(B, L, H * dv) * gain.astype(jnp.float32)
    return (o * jax.nn.silu(gate.astype(jnp.float32))).astype(gate.dtype)


def neighborhood_attention(q, k, v, k_ctx, v_ctx, rpb):
    B, L, H, d = q.shape
    rows = L // GRID_W
    wr = min(NA_WIN_ROWS, rows)
    nj = GRID_W // NA_QBLOCK_COLS
    r = jnp.arange(rows)
    ridx = jnp.clip(r - wr // 2, 0, rows - wr)[:, None] + jnp.arange(wr)[None, :]
    j = jnp.arange(nj)
    cidx = jnp.clip(j * NA_QBLOCK_COLS - NA_WIN_COLS // 2, 0, GRID_W - NA_KBLOCK_COLS)[:, None] \
        + jnp.arange(NA_KBLOCK_COLS)[None, :]
    tok = (ridx[:, None, :, None] * GRID_W + cidx[None, :, None, :]).reshape(rows, nj, wr * NA_KBLOCK_COLS)
    kg = jnp.take(k, tok, axis=1)
    vg = jnp.take(v, tok, axis=1)
    qb = q.reshape(B, rows, nj, NA_QBLOCK_COLS, H, d)
    qcol = j[:, None] * NA_QBLOCK_COLS + jnp.arange(NA_QBLOCK_COLS)[None, :]
    win_start = jnp.clip(qcol - NA_WIN_COLS // 2, 0, GRID_W - NA_WIN_COLS)
    kcol = cidx[:, None, :]
    col_ok = (kcol >= win_start[..., None]) & (kcol < win_start[..., None] + NA_WIN_COLS)
    dr = ridx - r[:, None] + NA_WIN_ROWS - 1
    dc = jnp.clip(kcol - qcol[..., None] + NA_WIN_COLS - 1, 0, 2 * NA_WIN_COLS - 2)
    bias = rpb[:, dr[:, None, None, :, None], dc[None, :, :, None, :]]
    nk = wr * NA_KBLOCK_COLS
    bias = bias.reshape(H, rows, nj, NA_QBLOCK_COLS, nk).astype(jnp.float32)
    mask = jnp.broadcast_to(col_ok[:, :, None, :], (nj, NA_QBLOCK_COLS, wr, NA_KBLOCK_COLS)).reshape(nj, NA_QBLOCK_COLS, nk)
    bias = jnp.where(mask[None, None], bias, NEG_INF)
    scale = d ** -0.5
    s_lat = jnp.einsum('brjqhd,brjkhd->bhrjqk', qb, kg).astype(jnp.float32) * scale + bias[None]
    s_ctx = jnp.einsum('brjqhd,bchd->bhrjqc', qb, k_ctx).astype(jnp.float32) * scale
    p = jax.nn.softmax(jnp.concatenate([s_lat, s_ctx], axis=-1), axis=-1)
    p_lat, p_ctx = p[..., :nk].astype(v.dtype), p[..., nk:].astype(v.dtype)
    out = (jnp.einsum('bhrjqk,brjkhd->brjqhd', p_lat, vg)
           + jnp.einsum('bhrjqc,bchd->brjqhd', p_ctx, v_ctx))
    return out.reshape(B, L, H * d)


def context_attention(q, k, v):
    B, Lc, H, d = q.shape
    s = jnp.einsum('bqhd,bkhd->bhqk', q, k).astype(jnp.float32) * d ** -0.5
    p = jax.nn.softmax(s, axis=-1).astype(v.dtype)
    return jnp.einsum('bhqk,bkhd->bqhd', p, v).reshape(B, Lc, H * d)


def token_mixer(h_lat, h_ctx, w_in, decay_logit, ret_g, w_four, rpb, w_out, with_ctx_out):
    B, L, _ = h_lat.shape
    Lc = h_ctx.shape[1]
    f_l, rq_l, rk_l, rv_l, rg_l, nq_l, nk_l, nv_l = jnp.split(h_lat @ w_in, SPLIT_IDX, axis=-1)
    f_c, rq_c, rk_c, rv_c, rg_c, nq_c, nk_c, nv_c = jnp.split(h_ctx @ w_in, SPLIT_IDX, axis=-1)
    pos = jnp.arange(L)
    prow, pcol = pos // GRID_W, pos % GRID_W

    def ret_heads(t, n):
        return t.reshape(B, n, RET_HEADS, RET_HEAD_DIM).transpose(0, 2, 1, 3).astype(jnp.float32)

    k_scale = RET_HEAD_DIM ** -0.5
    rq = rope_2d(ret_heads(rq_l, L), prow, pcol)
    rk = rope_2d(ret_heads(rk_l, L), prow, pcol) * k_scale
    rv = ret_heads(rv_l, L)
    rq_ctx = ret_heads(rq_c, Lc) if with_ctx_out else None
    rk_ctx = ret_heads(rk_c, Lc) * k_scale
    rv_ctx = ret_heads(rv_c, Lc)
    log_gamma = jax.nn.log_sigmoid(decay_logit.astype(jnp.float32))
    o_lat, o_ctx = bidirectional_retention(rq, rk, rv, rq_ctx, rk_ctx, rv_ctx, log_gamma, with_ctx_out)

    na_heads = lambda t, n: t.reshape(B, n, NA_HEADS, NA_HEAD_DIM)
    nk_ctx, nv_ctx = na_heads(nk_c, Lc), na_heads(nv_c, Lc)
    y_lat = jnp.concatenate([
        fourier_mix(f_l, w_four),
        retention_readout(o_lat, rg_l, ret_g),
        neighborhood_attention(na_heads(nq_l, L), na_heads(nk_l, L), na_heads(nv_l, L), nk_ctx, nv_ctx, rpb),
    ], axis=-1) @ w_out
    if not with_ctx_out:
        return y_lat, None
    y_ctx = jnp.concatenate([
        fourier_mix(f_c, w_four),
        retention_readout(o_ctx, rg_c, ret_g),
        context_attention(na_heads(nq_c, Lc), nk_ctx, nv_ctx),
    ], axis=-1) @ w_out
    return y_lat, y_ctx


def setup_inputs(seed: int = 0) -> dict:
    key = jax.random.key(seed)
    ks = jax.random.split(key, 20)
    nrm = lambda k, shape, s: jax.random.normal(k, shape, jnp.float32) * s
    gamma0 = 1.0 - 2.0 ** (-5.0 - np.arange(RET_HEADS))
    logit0 = jnp.asarray(np.log(gamma0 / (1.0 - gamma0)).astype(np.float32))
    return {
        "x": nrm(ks[0], (BATCH, SEQ, D_MODEL), 1.0),
        "c": nrm(ks[1], (BATCH, D_MODEL), 1.0),
        "ctx": nrm(ks[2], (BATCH, CTX_LEN, D_MODEL), 1.0),
        "c_ctx": nrm(ks[3], (D_MODEL,), 1.0),
        "w_ada": nrm(ks[4], (DEPTH, D_MODEL, 6 * D_MODEL), 0.5 * D_MODEL ** -0.5),
        "b_ada": nrm(ks[5], (DEPTH, 6 * D_MODEL), 0.01),
        "g_mix": 1.0 + nrm(ks[6], (DEPTH, D_MODEL), 0.05),
        "w_in": nrm(ks[7], (DEPTH, D_MODEL, PROJ_WIDTH), D_MODEL ** -0.5),
        "ret_decay_logit": logit0[None, None, :] + nrm(ks[8], (DEPTH, 2, RET_HEADS), 0.1),
        "ret_norm_g": 1.0 + nrm(ks[9], (DEPTH, RET_WIDTH), 0.05),
        "w_four": nrm(ks[10], (DEPTH, FOUR_WIDTH, FOUR_WIDTH), FOUR_WIDTH ** -0.5),
        "na_rpb": nrm(ks[11], (DEPTH, NA_HEADS, 2 * NA_WIN_ROWS - 1, 2 * NA_WIN_COLS - 1), 0.1),
        "w_out": nrm(ks[12], (DEPTH, D_MODEL, D_MODEL), D_MODEL ** -0.5),
        "g_ffn": 1.0 + nrm(ks[13], (DEPTH, D_MODEL), 0.05),
        "w1": nrm(ks[14], (DEPTH, D_MODEL, D_FF), D_MODEL ** -0.5),
        "w3": nrm(ks[15], (DEPTH, D_MODEL, D_FF), D_MODEL ** -0.5),
        "w2": nrm(ks[16], (DEPTH, D_FF, D_MODEL), D_FF ** -0.5),
        "g_final": 1.0 + nrm(ks[17], (D_MODEL,), 0.05),
    }


def reference(x, c, ctx, c_ctx, w_ada, b_ada, g_mix, w_in, ret_decay_logit, ret_norm_g, w_four,
              na_rpb, w_out, g_ffn, w1, w3, w2, g_final):
    for i in range(DEPTH):
        last = i == DEPTH - 1
        sh_a, sc_a, ga_a, sh_f, sc_f, ga_f = [m[:, None, :] for m in ada_params(c, w_ada[i], b_ada[i])]
        csh_a, csc_a, cga_a, csh_f, csc_f, cga_f = ada_params(c_ctx, w_ada[i], b_ada[i])
        h = rmsnorm(x, g_mix[i]) * (1.0 + sc_a) + sh_a
        hc = rmsnorm(ctx, g_mix[i]) * (1.0 + csc_a) + csh_a
        y, yc = token_mixer(h, hc, w_in[i], ret_decay_logit[i], ret_norm_g[i], w_four[i], na_rpb[i],
                            w_out[i], not last)
        x = x + ga_a * y
        x = x + ga_f * swiglu(rmsnorm(x, g_ffn[i]) * (1.0 + sc_f) + sh_f, w1[i], w3[i], w2[i])
        if not last:
            ctx = ctx + cga_a * yc
            ctx = ctx + cga_f * swiglu(rmsnorm(ctx, g_ffn[i]) * (1.0 + csc_f) + csh_f, w1[i], w3[i], w2[i])
    return rmsnorm(x, g_final)
```

```python
import math
from contextlib import ExitStack
import numpy as np
import ml_dtypes
import concourse.bass as bass
import concourse.mybir as mybir
from concourse.bass_utils import run_bass_kernel_spmd

F32 = mybir.dt.float32
BF16 = mybir.dt.bfloat16
AF = mybir.ActivationFunctionType
ALU = mybir.AluOpType
NPBF = ml_dtypes.bfloat16

D = 1024; L = 8192; LC = 256; DEPTH = 2; PW = 2944; DFF = 2816
TOK = 2048; TT = 2304; NCH = 18
BLOCKS = [(0, 512), (512, 512), (1024, 512), (1536, 512), (2048, 256)]
C_F, C_RQ, C_RK, C_RV, C_RG, C_NQ, C_NK, C_NV = 0, 256, 640, 1024, 1408, 1792, 2176, 2560
EPS = 1e-6
GROUPS = [[0, 1, 2, 3], [4, 5, 6, 7]]


class Buf:
    __slots__ = ("name", "w", "r")

    def __init__(self, name=""):
        self.name = name
        self.w = None
        self.r = {}


class K:
    ENGS = ("pe", "dve", "act", "pool", "sp")

    def __init__(self, nc, st):
        self.nc = nc
        self.prog = {e: [] for e in self.ENGS}
        self.cnt = {}
        self.seen = {e: {} for e in self.ENGS}
        self.sems = {}
        names = ["c_pe", "c_dve", "c_act", "c_pool", "cc"]
        self.ndma = 8
        self.dma_rr = {}
        for e in ("sp", "act", "pool"):
            self.dma_rr[e] = 0
            names += [f"d_{e}{i}" for i in range(self.ndma)]
        for sn in names:
            self.cnt[sn] = 0
            self.sems[sn] = st.enter_context(nc.semaphore(sn))
        self.nblk = 0
        self.stage = "init"

    def _need(self, eng, ev, waits):
        if ev is None:
            return
        sn, val = ev
        if self.seen[eng].get(sn, 0) >= val:
            return
        waits[sn] = max(waits.get(sn, 0), val)

    def _deps(self, eng, reads, writes, pe_accum=False):
        waits = {}
        for b in reads:
            self._need(eng, b.w, waits)
        own = "c_" + eng
        for b in writes:
            if not (b.w is not None and b.w[0] == own and (pe_accum or eng != "pe")):
                self._need(eng, b.w, waits)
            for sn, val in b.r.items():
                self._need(eng, (sn, val), waits)
        for sn, val in waits.items():
            self.seen[eng][sn] = val
            self.prog[eng].append(("wait", sn, val))

    def _mark(self, ev, reads, writes):
        sn, val = ev
        for b in reads:
            b.r[sn] = max(b.r.get(sn, 0), val)
        for b in writes:
            b.w = ev
            b.r = {}

    def op(self, eng, fn, reads=(), writes=(), inc=True, pe_accum=False):
        self._deps(eng, reads, writes, pe_accum)
        sn = "c_" + eng
        val = self.cnt[sn] + 1
        self._mark((sn, val), reads, writes)
        if inc:
            self.cnt[sn] = val
            self.prog[eng].append(("op", fn, sn, 1))
        else:
            self.prog[eng].append(("op", fn, None, 0))

    def dma(self, eng, out, in_, reads=(), writes=(), **kw):
        self._deps(eng, reads, writes)
        i = self.dma_rr[eng]
        self.dma_rr[eng] = (i + 1) % self.ndma
        sn = f"d_{eng}{i}"
        prev = self.cnt[sn]
        if prev > 0 and self.seen[eng].get(sn, 0) < prev:
            self.seen[eng][sn] = prev
            self.prog[eng].append(("wait", sn, prev))
        val = prev + 16
        self.cnt[sn] = val
        self._mark((sn, val), reads, writes)
        self.prog[eng].append(("dma", out, in_, sn, kw))

    def cc(self, fn, reads=(), writes=()):
        eng = "pool"
        self._deps(eng, reads, writes)
        sn = "cc"
        val = self.cnt[sn] + 1
        self.cnt[sn] = val
        self._mark((sn, val), reads, writes)
        self.prog[eng].append(("op", fn, sn, None))

    def barrier(self, final=False):
        snap = dict(self.cnt)
        for e in self.ENGS:
            for sn, val in snap.items():
                if sn == "cc" and not final:
                    continue
                if val > 0 and self.seen[e].get(sn, 0) < val:
                    self.seen[e][sn] = val
                    self.prog[e].append(("wait", sn, val))

    def flush(self, final=False):
        self.barrier(final)
        nc = self.nc
        sems = self.sems
        prog = self.prog
        self.prog = {e: [] for e in self.ENGS}
        self.nblk += 1

        def run(e, items):
            for it in items:
                if it[0] == "wait":
                    e.wait_ge(sems[it[1]], it[2])
                elif it[0] == "op":
                    ins = it[1](e)
                    if it[2] is not None:
                        if it[3] is None:
                            ins.then_inc(sems[it[2]])
                        else:
                            ins.then_inc(sems[it[2]], it[3])
                else:
                    e.dma_start(out=it[1], in_=it[2], **it[4]).then_inc(sems[it[3]], 16)

        with nc.named_scope(f"{self.stage}_{self.nblk}"), nc.Block() as block:
            @block.tensor
            def _(e):
                run(e, prog["pe"])

            @block.vector
            def _(e):
                run(e, prog["dve"])

            @block.scalar
            def _(e):
                run(e, prog["act"])

            @block.gpsimd
            def _(e):
                run(e, prog["pool"])

            @block.sync
            def _(e):
                run(e, prog["sp"])


_UC = [0]


def U(name):
    _UC[0] += 1
    return f"{name}_{_UC[0]}"


class Rot:
    def __init__(self, nc, st, name, shape, dtype, n, psum=False):
        self.items = []
        for i in range(n):
            if psum:
                t = st.enter_context(nc.psum_tensor(U(f"ps_{name}{i}"), shape, dtype))
            else:
                t = st.enter_context(nc.sbuf_tensor(U(f"sb_{name}{i}"), shape, dtype))
            self.items.append((t, Buf(f"{name}{i}")))
        self.i = 0

    def next(self):
        it = self.items[self.i]
        self.i = (self.i + 1) % len(self.items)
        return it


def _bf(a):
    return np.ascontiguousarray(np.asarray(a, np.float32).astype(NPBF))


def _consts_common():
    c = {}
    c["ident_f"] = np.eye(128, dtype=np.float32)
    c["ident_b"] = _bf(np.eye(128))
    c["ones_b"] = _bf(np.ones((128, 128)))
    j = np.arange(128)[:, None]; i = np.arange(128)[None, :]
    dm = np.zeros((128, 4, 128), np.float32)
    dm[:, 0] = np.maximum(i - j, 0); dm[:, 1] = (i >= j)
    dm[:, 2] = np.maximum(j - i, 0); dm[:, 3] = (j >= i)
    c["dmask"] = dm
    eq = np.zeros((128, 2, 128), np.float32)
    eq[:, 0, :] = np.arange(128)[None, :] + 1.0
    eq[:, 1, :] = 128.0 - np.arange(128)[None, :]
    c["eq"] = eq
    ek = np.zeros((128, 2), np.float32)
    ek[:, 0] = 127.0 - np.arange(128); ek[:, 1] = np.arange(128)
    c["ek"] = ek
    ec = np.zeros((128, 2, 16), np.float32)
    ec[:, 0, :] = 128.0 * np.arange(16)[None, :]
    ec[:, 1, :] = 128.0 * (15 - np.arange(16))[None, :]
    c["ec"] = ec
    r = np.arange(128)[:, None].astype(np.float64); l1 = np.arange(128)[None, :].astype(np.float64)
    nrm = 1.0 / math.sqrt(L * 64.0)
    ang = 2 * np.pi * r * l1 / 128.0
    c["d128"] = _bf(np.concatenate([np.cos(ang) * nrm, -np.sin(ang) * nrm], 1))
    cc = np.arange(64)[:, None].astype(np.float64); cp = np.arange(64)[None, :].astype(np.float64)
    a64 = 2 * np.pi * cc * cp / 64.0
    c64 = np.zeros((128, 128)); s64 = np.zeros((128, 128))
    for g in range(2):
        c64[64 * g:64 * g + 64, 64 * g:64 * g + 64] = np.cos(a64)
        s64[64 * g:64 * g + 64, 64 * g:64 * g + 64] = -np.sin(a64)
    c["c64"] = _bf(c64); c["s64n"] = _bf(s64)
    pos = np.arange(256).astype(np.float64)[:, None]; lp = np.arange(256).astype(np.float64)[None, :]
    a256 = 2 * np.pi * pos * lp / 256.0
    nc_ = 1.0 / math.sqrt(256 * 64.0)
    dc = np.concatenate([np.cos(a256) * nc_, np.sin(a256) * nc_], 1)
    c["dc256"] = _bf(dc.reshape(2, 128, 512).transpose(1, 0, 2))
    return c


def _consts_core(q):
    c = {}
    t = np.arange(TOK) + TOK * q
    prow = (t // 64).astype(np.float32); pcol = (t % 64).astype(np.float32)
    inv = (10000.0 ** (-np.arange(24, dtype=np.float32) / 24)).astype(np.float32)
    cos = np.zeros((96, TOK), np.float32); sin = np.zeros((96, TOK), np.float32)
    for i in range(96):
        part, rr = i // 48, i % 48
        m = rr % 24
        ang = ((prow if part == 0 else pcol) * inv[m]).astype(np.float32)
        cos[i] = np.cos(ang)
        sin[i] = -np.sin(ang) if rr < 24 else np.sin(ang)
    ks = np.float32(96 ** -0.5)
    c["rope"] = np.stack([cos, sin, cos * ks, sin * ks], 1)
    ex = np.zeros((128, 5, 2), np.float32); mx = np.zeros((128, 5, 2), np.float32)
    for j in range(4):
        if j < q:
            ex[:, j, 0] = 2048.0 * (q - 1 - j); mx[:, j, 0] = 1
        if j > q:
            ex[:, j, 1] = 2048.0 * (j - q - 1); mx[:, j, 1] = 1
    ex[:, 4, 0] = 2048.0 * q; mx[:, 4, 0] = 1
    ex[:, 4, 1] = 2048.0 * (3 - q); mx[:, 4, 1] = 1
    c["exmx"] = np.stack([ex, mx], 1)
    mh = np.zeros((128, 8), np.float32)
    if q > 0:
        mh[:, q - 1] = 1
    if q < 3:
        mh[:, 4 + q + 1] = 1
    c["mh"] = mh
    w = np.arange(64, dtype=np.float64)[:, None, None]
    l1 = np.arange(128, dtype=np.float64)[None, :, None]
    l2 = (16 * q + np.arange(16, dtype=np.float64))[None, None, :]
    th = 2 * np.pi * w * (l1 + 128 * l2) / 8192.0
    k2 = np.zeros((64, 128, 2, 32))
    k2[:, :, 0, 0:16] = np.cos(th); k2[:, :, 0, 16:32] = np.sin(th)
    k2[:, :, 1, 0:16] = np.sin(th); k2[:, :, 1, 16:32] = -np.cos(th)
    c["k2s"] = _bf(k2.transpose(2, 0, 1, 3).reshape(128, 128, 32))
    return c


def _na_bias(rpb, q):
    NEG = np.float32(-1e30)

    def table(t, kb0, nkt):
        qi = np.arange(128)
        lr = 2 * t + qi // 64
        r = 32 * q + lr
        col = qi % 64
        kb = kb0 + np.arange(nkt * 128)
        br = kb // 64; kc = kb % 64
        kr = 32 * q + (br - 4)
        r0 = np.clip(r - 4, 0, 120)
        rok = (kr[:, None] >= r0[None, :]) & (kr[:, None] < r0[None, :] + 8) & (kr[:, None] >= 0) & (kr[:, None] < 128)
        ws = np.clip(col - 8, 0, 48)
        cok = (kc[:, None] >= ws[None, :]) & (kc[:, None] < ws[None, :] + 16)
        ok = rok & cok
        dr = np.clip(kr[:, None] - r[None, :] + 7, 0, 14)
        dcc = np.clip(kc[:, None] - col[None, :] + 15, 0, 30)
        out = np.empty((6, nkt * 128, 128), np.float32)
        for h in range(6):
            out[h] = np.where(ok, rpb[h][dr, dcc], NEG)
        return out.reshape(6, nkt, 128, 128).transpose(2, 0, 1, 3)

    inter = table(5, 128 * 5, 5)
    edge = np.full((4, 128, 6, 6, 128), NEG, np.float32)
    edge[0] = table(0, 0, 6)
    edge[1][:, :, :5] = table(1, 128, 5)
    edge[2][:, :, :5] = table(14, 128 * 14, 5)
    edge[3] = table(15, 1792, 6)
    return np.ascontiguousarray(inter), np.ascontiguousarray(edge)


def _col8(v):
    return np.ascontiguousarray(np.asarray(v, np.float32).reshape(8, 128).T)


def make_in_maps(inp):
    cm = _consts_common()
    maps = []
    for core in range(8):
        b, q = core // 4, core % 4
        m = dict(cm)
        m.update(_consts_core(q))
        m["x_in"] = np.ascontiguousarray(inp["x"][b, TOK * q:TOK * (q + 1)])
        m["ctx_in"] = np.ascontiguousarray(inp["ctx"][b])
        m["cvec"] = np.ascontiguousarray(np.stack([_col8(inp["c"][b]), _col8(inp["c_ctx"])], 2))
        m["b_ada"] = np.ascontiguousarray(np.stack(
            [np.asarray(inp["b_ada"][l], np.float32).reshape(48, 128).T for l in range(DEPTH)], 0))
        m["g_mix"] = np.stack([_col8(inp["g_mix"][l]) for l in range(DEPTH)], 0)
        m["g_ffn"] = np.stack([_col8(inp["g_ffn"][l]) for l in range(DEPTH)], 0)
        m["g_final"] = _col8(inp["g_final"])
        rg = np.asarray(inp["ret_norm_g"], np.float32).reshape(DEPTH, 4, 96).transpose(0, 2, 1)
        m["ret_gain"] = np.ascontiguousarray(np.broadcast_to(rg[:, :, :, None], (DEPTH, 96, 4, 128)))
        lg = np.asarray(inp["ret_decay_logit"], np.float32).reshape(DEPTH, 1, 8)
        m["logit"] = np.ascontiguousarray(np.broadcast_to(lg, (DEPTH, 128, 8)))
        bi, be = [], []
        for l in range(DEPTH):
            a, e = _na_bias(np.asarray(inp["na_rpb"][l], np.float32), q)
            bi.append(a); be.append(e)
        m["bias_int"] = np.stack(bi, 0)
        m["bias_edge"] = np.stack(be, 0)
        for nme in ("w_ada", "w_in", "w_four", "w_out", "w1", "w3", "w2"):
            m[nme] = np.ascontiguousarray(np.asarray(inp[nme], np.float32))
        maps.append(m)
    return maps


IN_SPECS = {
    "x_in": ([TOK, D], F32), "ctx_in": ([LC, D], F32), "cvec": ([128, 8, 2], F32),
    "b_ada": ([DEPTH, 128, 48], F32), "g_mix": ([DEPTH, 128, 8], F32), "g_ffn": ([DEPTH, 128, 8], F32),
    "g_final": ([128, 8], F32), "ret_gain": ([DEPTH, 96, 4, 128], F32), "logit": ([DEPTH, 128, 8], F32),
    "bias_int": ([DEPTH, 128, 6, 5, 128], F32), "bias_edge": ([DEPTH, 4, 128, 6, 6, 128], F32),
    "w_ada": ([DEPTH, D, 6 * D], F32), "w_in": ([DEPTH, D, PW], F32), "w_four": ([DEPTH, 256, 256], F32),
    "w_out": ([DEPTH, D, D], F32), "w1": ([DEPTH, D, DFF], F32), "w3": ([DEPTH, D, DFF], F32),
    "w2": ([DEPTH, DFF, D], F32),
    "ident_f": ([128, 128], F32), "ident_b": ([128, 128], BF16), "ones_b": ([128, 128], BF16),
    "dmask": ([128, 4, 128], F32), "eq": ([128, 2, 128], F32), "ek": ([128, 2], F32), "ec": ([128, 2, 16], F32),
    "d128": ([128, 256], BF16), "c64": ([128, 128], BF16), "s64n": ([128, 128], BF16), "dc256": ([128, 2, 512], BF16),
    "rope": ([96, 4, TOK], F32), "exmx": ([128, 2, 5, 2], F32), "mh": ([128, 8], F32), "k2s": ([128, 128, 32], BF16),
}


def MM(out, lhsT, rhs, start, stop):
    return lambda e: e.matmul(out, lhsT, rhs, start=start, stop=stop)


def TR(out, in_, ident):
    return lambda e: e.transpose(out=out, in_=in_, identity=ident)


def ACT(out, in_, func, **kw):
    return lambda e: e.activation(out=out, in_=in_, func=func, **kw)


def TTo(out, in0, in1, op):
    return lambda e: e.tensor_tensor(out=out, in0=in0, in1=in1, op=op)


def TS(out, in0, s1, op0, s2=None, op1=None):
    if op1 is None:
        return lambda e: e.tensor_scalar(out=out, in0=in0, scalar1=s1, scalar2=None, op0=op0)
    return lambda e: e.tensor_scalar(out=out, in0=in0, scalar1=s1, scalar2=s2, op0=op0, op1=op1)


def STT(out, in0, scalar, in1, op0, op1):
    return lambda e: e.scalar_tensor_tensor(out=out, in0=in0, scalar=scalar, in1=in1, op0=op0, op1=op1)


def CP(out, in_):
    return lambda e: e.tensor_copy(out=out, in_=in_)


def MS(ap, v):
    return lambda e: e.memset(ap, v)


def RCP(out, in_):
    return lambda e: e.reciprocal(out=out, in_=in_)


def build(debug=False, nlayers=DEPTH, stop=None, only=None):
    nc = bass.Bass("TRN2", target_bir_lowering=False)
    I = {n: nc.dram_tensor(n, shp, dt, kind="ExternalInput").ap() for n, (shp, dt) in IN_SPECS.items()}
    out_d = nc.dram_tensor("out", [TOK, D], F32, kind="ExternalOutput").ap()
    dbgk = "ExternalOutput" if debug else None

    def scr(name, shape, dt=BF16, dbg=True):
        if debug and dbg:
            return nc.dram_tensor(name, shape, dt, kind="ExternalOutput").ap()
        return nc.dram_tensor(name, shape, dt).ap()

    s_q = scr("s_q", [4, 96, TT]); s_k = scr("s_k", [4, 96, TT]); s_g = scr("s_g", [4, 96, TT])
    s_v = scr("s_v", [TT, 384]); s_nq = scr("s_nq", [3, 128, TT]); s_nk = scr("s_nk", [3, 128, TT])
    s_nv = scr("s_nv", [TT, 390]); s_u = scr("s_u", [22, 128, TT], dbg=False)
    f_c = scr("f_c", [TOK, 256], dbg=False); f_g = scr("f_g", [4 * TOK, 256], dbg=False)
    st_c = scr("st_c", [96, 768], F32, dbg=False); st_g = scr("st_g", [4 * 96, 768], F32, dbg=False)
    ke_c = scr("ke_c", [384, 448], dbg=False); ke_g = scr("ke_g", [4 * 384, 448], dbg=False)
    ve_c = scr("ve_c", [448, 390], dbg=False); ve_g = scr("ve_g", [4 * 448, 390], dbg=False)
    if debug:
        d_xT = nc.dram_tensor("d_xT", [128, 8, TT], F32, kind="ExternalOutput").ap()
        d_hy = nc.dram_tensor("d_hy", [128, 9, TT], BF16, kind="ExternalOutput").ap()
        d_mod = nc.dram_tensor("d_mod", [128, 48, 2], F32, kind="ExternalOutput").ap()
    B = {n: Buf(n) for n in ("s_q s_k s_g s_v s_nq s_nk s_nv s_u f_c f_g st_c st_g ke_c ke_g ve_c ve_g out "
                             "const mod gs fctx").split()}

    with ExitStack() as st:
        k = K(nc, st)
        xT = st.enter_context(nc.sbuf_tensor(U("sb_xT"), [128, 8, TT], F32))
        hy = st.enter_context(nc.sbuf_tensor(U("sb_hy"), [128, 9, TT], BF16))
        bx = [Buf(f"x{i}") for i in range(5)]
        bhy = [Buf(f"hy{i}") for i in range(5)]
        identf = st.enter_context(nc.sbuf_tensor(U("sb_identf"), [128, 128], F32))
        identb = st.enter_context(nc.sbuf_tensor(U("sb_identb"), [128, 128], BF16))
        onesb = st.enter_context(nc.sbuf_tensor(U("sb_onesb"), [128, 128], BF16))
        epsc = st.enter_context(nc.sbuf_tensor(U("sb_epsc"), [128, 1], F32))
        cvec = st.enter_context(nc.sbuf_tensor(U("sb_cvec"), [128, 8, 2], F32))
        scv = st.enter_context(nc.sbuf_tensor(U("sb_scv"), [128, 8, 2], F32))
        scvb = st.enter_context(nc.sbuf_tensor(U("sb_scvb"), [128, 8, 2], BF16))
        mod = st.enter_context(nc.sbuf_tensor(U("sb_mod"), [128, 48, 2], F32))
        gs = st.enter_context(nc.sbuf_tensor(U("sb_gs"), [128, 2, 8, 2], F32))
        bad = st.enter_context(nc.sbuf_tensor(U("sb_bad"), [128, 48], F32))
        gvec = st.enter_context(nc.sbuf_tensor(U("sb_gvec"), [128, 3, 8], F32))
        fctx = st.enter_context(nc.sbuf_tensor(U("sb_fctx"), [128, 2, 256], BF16))
        psf = Rot(nc, st, "psf", [128, 512], F32, 4, psum=True)
        psl = Rot(nc, st, "psl", [128, 512], F32, 2, psum=True)
        psb = Rot(nc, st, "psb", [128, 1024], BF16, 2, psum=True)
        bc = B["const"]
        k.dma("sp", identf[:], I["ident_f"][:, :], writes=[bc])
        k.dma("sp", identb[:], I["ident_b"][:, :], writes=[bc])
        k.dma("sp", onesb[:], I["ones_b"][:, :], writes=[bc])
        k.dma("sp", cvec[:], I["cvec"][:, :, :], writes=[bc])
        k.op("dve", MS(epsc[:], EPS), writes=[bc])
        k.op("act", ACT(scv[:], cvec[:], AF.Silu), reads=[bc], writes=[bc])
        k.op("act", ACT(scvb[:], cvec[:], AF.Silu), reads=[bc], writes=[bc])
        k.dma("sp", gvec[:, 2, :], I["g_final"][:, :], writes=[bc])

        def blk_of(c0):
            return min(c0 // 512, 4)

        def load_x(s2):
            xtok = Rot(nc, s2, "xtok", [128, D], F32, 2)
            for t in range(18):
                src = I["x_in"][t * 128:(t + 1) * 128, :] if t < 16 else I["ctx_in"][(t - 16) * 128:(t - 15) * 128, :]
                xt, bxt = xtok.next()
                k.dma("sp", xt[:], src, writes=[bxt])
                bi = blk_of(t * 128)
                for half in range(2):
                    ps, bps = psf.next()
                    for j in range(4):
                        kt = half * 4 + j
                        k.op("pe", TR(ps[:, j * 128:(j + 1) * 128], xt[:, kt * 128:(kt + 1) * 128], identf[:]),
                             reads=[bxt, bc], writes=[bps], inc=(j == 3), pe_accum=True)
                    if half == 0:
                        k.op("act", ACT(xT[:, half * 4:half * 4 + 4, t * 128:(t + 1) * 128], ps[:].rearrange("p (j c) -> p j c", j=4), AF.Copy),
                             reads=[bps], writes=[bx[bi]])
                    else:
                        k.op("dve", CP(xT[:, half * 4:half * 4 + 4, t * 128:(t + 1) * 128], ps[:].rearrange("p (j c) -> p j c", j=4)),
                             reads=[bps], writes=[bx[bi]])

        def ada_steps(l, wrot):
            psA, bpsA = psl.next()
            steps = []
            rotbox = [wrot]
            for cb in range(24):
                def step(cb=cb):
                    w, bw = rotbox[0].next()
                    k.dma("pool", w[:], I["w_ada"][l][:, cb * 256:(cb + 1) * 256].rearrange("(kt p) c -> p kt c", p=128), writes=[bw])
                    for ct in range(2):
                        col = cb * 2 + ct
                        for kt in range(8):
                            k.op("pe", MM(psA[:, col * 2:col * 2 + 2], w[:, kt, ct * 128:(ct + 1) * 128], scvb[:, kt, :], kt == 0, kt == 7),
                                 reads=[bw, bc], writes=[bpsA], inc=(kt == 7 and ct == 1), pe_accum=True)
                steps.append(step)

            def fin(part=None):
                pv = psA[:, 0:96].rearrange("p (t j) -> p t j", j=2)
                if part in (None, "a"):
                    k.dma("sp", bad[:], I["b_ada"][l], writes=[B["mod"]])
                    k.dma("sp", gvec[:, 0, :], I["g_mix"][l], writes=[B["mod"]])
                    k.dma("sp", gvec[:, 1, :], I["g_ffn"][l], writes=[B["mod"]])
                lo, hi = {None: (0, 48), "a": (0, 16), "b": (16, 48)}[part]
                for j in range(2):
                    k.op("dve", TTo(mod[:, lo:hi, j], pv[:, lo:hi, j], bad[:, lo:hi], ALU.add), reads=[bpsA, B["mod"], B["gs"]], writes=[B["mod"]])
                for j in range(2):
                    if part in (None, "a"):
                        k.op("dve", STT(gs[:, 0, :, j], mod[:, 8:16, j], 1.0, gvec[:, 0, :], ALU.add, ALU.mult),
                             reads=[B["mod"]], writes=[B["gs"]])
                    if part in (None, "b"):
                        k.op("dve", STT(gs[:, 1, :, j], mod[:, 32:40, j], 1.0, gvec[:, 1, :], ALU.add, ALU.mult),
                             reads=[B["mod"]], writes=[B["gs"]])
            fin.rotbox = rotbox
            return steps, fin

        def ada(l):
            k.stage = "ada"
            with ExitStack() as s2:
                wrot = Rot(nc, s2, "wada", [128, 8, 256], BF16, 3)
                steps, fin = ada_steps(l, wrot)
                for st_ in steps[:4]:
                    st_()
                if l == 0:
                    load_x(s2)
                for st_ in steps[4:8]:
                    st_()
                fin("a")
                k.flush()
            return steps[8:], fin

        def norm(a, blocks, final=False, scope=None):
            if scope is None:
                k.stage = "norm"
            blocks = list(blocks)
            with ExitStack() as s2_own:
                s2 = s2_own if scope is None else scope
                sqra = Rot(nc, s2, "sqra", [128, 4, 512], BF16, 2)
                sqrb = Rot(nc, s2, "sqrb", [128, 4, 512], BF16, 2)
                rsr = Rot(nc, s2, "rsr", [128, 512], F32, 2)
                tmr = Rot(nc, s2, "tmr", [128, 512], F32, 3)

                def ph1(bi):
                    c0, wd = BLOCKS[bi]
                    sqa, bsqa = sqra.next()
                    sqb, bsqb = sqrb.next()
                    k.op("pool", TTo(sqa[:, :, :wd], xT[:, 0:4, c0:c0 + wd], xT[:, 0:4, c0:c0 + wd], ALU.mult), reads=[bx[bi]], writes=[bsqa])
                    k.op("act", ACT(sqb[:, :, :wd], xT[:, 4:8, c0:c0 + wd], AF.Square), reads=[bx[bi]], writes=[bsqb])
                    ps, bps = psf.next()
                    for kt in range(8):
                        sq_, bsq_ = (sqa, bsqa) if kt < 4 else (sqb, bsqb)
                        k.op("pe", MM(ps[:, :wd], onesb[:], sq_[:, kt % 4, :wd], kt == 0, kt == 7), reads=[bsq_, bc],
                             writes=[bps], inc=(kt == 7), pe_accum=True)
                    rs, brs = rsr.next()
                    k.op("act", ACT(rs[:, :wd], ps[:, :wd], AF.Ln, scale=1.0 / D, bias=epsc[:]), reads=[bps, bc], writes=[brs])
                    k.op("act", ACT(rs[:, :wd], rs[:, :wd], AF.Exp, scale=-0.5), reads=[brs], writes=[brs])
                    return rs, brs

                def ph2(bi, rs, brs):
                    c0, wd = BLOCKS[bi]
                    j = 0 if bi < 4 else 1
                    for kt in range(8):
                        tm, btm = tmr.next()
                        k.op("dve", TTo(tm[:, :wd], xT[:, kt, c0:c0 + wd], rs[:, :wd], ALU.mult), reads=[bx[bi], brs],
                             writes=[btm])
                        if final:
                            k.op("dve", TS(xT[:, kt, c0:c0 + wd], tm[:, :wd], gvec[:, 2, kt:kt + 1], ALU.mult),
                                 reads=[btm, bc], writes=[bx[bi]])
                        else:
                            k.op("act", ACT(hy[:, kt, c0:c0 + wd], tm[:, :wd], AF.Identity, scale=gs[:, a, kt, j:j + 1],
                                            bias=mod[:, 24 * a + kt, j:j + 1]),
                                 reads=[btm, B["gs"], B["mod"]], writes=[bhy[bi]])

                pend = ph1(blocks[0])
                for i_, bi in enumerate(blocks):
                    nxt = ph1(blocks[i_ + 1]) if i_ + 1 < len(blocks) else None
                    ph2(bi, *pend)
                    pend = nxt
                if scope is None:
                    k.flush()

        def AG(src, dst, bsrc, bdst):
            k.cc(lambda e: e.collective_compute("AllGather", ALU.bypass, replica_groups=GROUPS,
                                                ins=[src], outs=[dst]), reads=[bsrc], writes=[bdst])

        def na_halo_exchange():
            k.dma("sp", ke_c.rearrange("(h p) t -> h p t", p=128)[:, :, 0:192], s_nk[:, :, 0:192], reads=[B["s_nk"]], writes=[B["ke_c"]])
            k.dma("sp", ke_c.rearrange("(h p) t -> h p t", p=128)[:, :, 192:448], s_nk[:, :, 1792:2048], reads=[B["s_nk"]], writes=[B["ke_c"]])
            k.dma("sp", ve_c[0:192, :], s_nv[0:192, :], reads=[B["s_nv"]], writes=[B["ve_c"]])
            k.dma("sp", ve_c[192:448, :], s_nv[1792:2048, :], reads=[B["s_nv"]], writes=[B["ve_c"]])
            AG(ke_c[:, :], ke_g[:, :], B["ke_c"], B["ke_g"])
            AG(ve_c[:, :], ve_g[:, :], B["ve_c"], B["ve_g"])

        def load_w(rot, src2d, ncols, krows=8):
            w, bw = rot.next()
            k.dma("pool", w[:, :krows, :ncols], src2d.rearrange("(kt p) c -> p kt c", p=128), writes=[bw])
            return w, bw

        def project(l, last, deferred=None):
            k.stage = "project"
            wl = I["w_in"][l]
            with ExitStack() as s2:
                dq = []
                if deferred is not None:
                    dsteps, dfin = deferred
                    dfin.rotbox[0] = Rot(nc, s2, "wada2", [128, 8, 256], BF16, 3)
                    dq = list(dsteps)

                def ada_tick():
                    if dq:
                        dq.pop(0)()

                wbr = Rot(nc, s2, "wbr", [128, 8, 512], BF16, 2)
                wpr = Rot(nc, s2, "wpr", [128, 8, 384], BF16, 1)
                stg = Rot(nc, s2, "stg", [128, 512], BF16, 6)
                ropr = Rot(nc, s2, "ropr", [96, 4, 512], F32, 2)
                tA = Rot(nc, s2, "tA", [96, 512], F32, 2)
                tB = Rot(nc, s2, "tB", [96, 512], F32, 2)
                nvs = Rot(nc, s2, "nvs", [128, 6, 65], BF16, 2)
                for tle, bb in nvs.items:
                    k.op("pool", MS(tle[:], 1.0), writes=[bb])
                norm(0, range(5), scope=s2)

                def fm_group(w, bw, m0, M, evac):
                    for bi, (c0, wd) in enumerate(BLOCKS):
                        ps, bps = psf.next()
                        for kt in range(8):
                            k.op("pe", MM(ps[:M, :wd], w[:, kt, m0:m0 + M], hy[:, kt, c0:c0 + wd], kt == 0, kt == 7),
                                 reads=[bw, bhy[bi]], writes=[bps], inc=(kt == 7), pe_accum=True)
                        evac(ps, bps, bi, c0, wd)

                def tm_group(w, bw, ncols, evac, tiles=range(18)):
                    for t in tiles:
                        bi = blk_of(t * 128)
                        ps, bps = psf.next()
                        for kt in range(8):
                            k.op("pe", MM(ps[:, :ncols], hy[:, kt, t * 128:(t + 1) * 128], w[:, kt, :ncols], kt == 0, kt == 7),
                                 reads=[bw, bhy[bi]], writes=[bps], inc=(kt == 7), pe_accum=True)
                        evac(ps, bps, t)

                w, bw = load_w(wbr, wl[:, C_F:C_F + 256], 256)

                def ev_f(ps, bps, t):
                    if t < 16:
                        s, bs = stg.next()
                        k.op("act", ACT(s[:, :256], ps[:, :256], AF.Copy), reads=[bps], writes=[bs])
                        k.dma("sp", f_c[t * 128:(t + 1) * 128, :], s[:, :256], reads=[bs], writes=[B["f_c"]])
                    else:
                        k.op("act", ACT(fctx[:, t - 16, :], ps[:, :256], AF.Copy), reads=[bps], writes=[B["fctx"]])
                tm_group(w, bw, 256, ev_f)
                ada_tick()

                for which, c_off, dst, bdst in ((0, C_RQ, s_q, B["s_q"]), (1, C_RK, s_k, B["s_k"])):
                    w, bw = load_w(wbr, wl[:, c_off:c_off + 384], 384)
                    wp, bwp = wpr.next()
                    w5 = w[:, :, 0:384].rearrange("p k (a b c) -> p k a b c", a=8, b=2, c=24)
                    p5 = wp[:, :, 0:384].rearrange("p k (a b c) -> p k a b c", a=8, b=2, c=24)
                    for b_ in range(2):
                        k.op("pool", CP(p5[:, :, :, b_, :], w5[:, :, :, 1 - b_, :]), reads=[bw], writes=[bwp])
                    for bi, (c0, wd) in enumerate(BLOCKS):
                        ada_tick()
                        if bi < 4:
                            rp, brp = ropr.next()
                            k.dma("sp", rp[:], I["rope"][:, :, c0:c0 + wd], writes=[brp])
                        for h in range(4):
                            ps, bps = psf.next()
                            for kt in range(8):
                                k.op("pe", MM(ps[:96, :wd], w[:, kt, h * 96:(h + 1) * 96], hy[:, kt, c0:c0 + wd], kt == 0, kt == 7),
                                     reads=[bw, bhy[bi]], writes=[bps], inc=(kt == 7), pe_accum=True)
                            s, bs = stg.next()
                            if bi < 4:
                                ps2, bps2 = psf.next()
                                for kt in range(8):
                                    k.op("pe", MM(ps2[:96, :wd], wp[:, kt, h * 96:(h + 1) * 96], hy[:, kt, c0:c0 + wd], kt == 0, kt == 7),
                                         reads=[bwp, bhy[bi]], writes=[bps2], inc=(kt == 7), pe_accum=True)
                                a_, ba_ = tA.next()
                                b2, bb2 = tB.next()
                                k.op("dve", TTo(a_[:, :wd], ps[:96, :wd], rp[:, 2 * which, :wd], ALU.mult), reads=[bps, brp], writes=[ba_])
                                k.op("dve", TTo(b2[:, :wd], ps2[:96, :wd], rp[:, 2 * which + 1, :wd], ALU.mult), reads=[bps2, brp], writes=[bb2])
                                k.op("pool", TTo(s[:96, :wd], a_[:, :wd], b2[:, :wd], ALU.add), reads=[ba_, bb2], writes=[bs])
                            else:
                                k.op("act", ACT(s[:96, :wd], ps[:96, :wd], AF.Copy, scale=(1.0 if which == 0 else 96 ** -0.5)),
                                     reads=[bps], writes=[bs])
                            k.dma("sp", dst[h][:, c0:c0 + wd], s[:96, :wd], reads=[bs], writes=[bdst])

                w, bw = load_w(wbr, wl[:, C_RV:C_RV + 384], 384)

                def ev_v(ps, bps, t):
                    s, bs = stg.next()
                    k.op("act", ACT(s[:, :384], ps[:, :384], AF.Copy), reads=[bps], writes=[bs])
                    k.dma("sp", s_v[t * 128:(t + 1) * 128, :], s[:, :384], reads=[bs], writes=[B["s_v"]])
                tm_group(w, bw, 384, ev_v)
                ada_tick()

                w, bw = load_w(wbr, wl[:, C_RG:C_RG + 384], 384)
                for h in range(4):
                    def ev_g(ps, bps, bi, c0, wd, h=h):
                        s, bs = stg.next()
                        k.op("act", ACT(s[:96, :wd], ps[:96, :wd], AF.Silu), reads=[bps], writes=[bs])
                        k.dma("sp", s_g[h][:, c0:c0 + wd], s[:96, :wd], reads=[bs], writes=[B["s_g"]])
                    fm_group(w, bw, h * 96, 96, ev_g)
                    ada_tick()

                for c_off, dst, bdst, scl in ((C_NQ, s_nq, B["s_nq"], 0.125), (C_NK, s_nk, B["s_nk"], 1.0)):
                    w, bw = load_w(wbr, wl[:, c_off:c_off + 384], 384)
                    for hp in range(3):
                        def ev_n(ps, bps, bi, c0, wd, hp=hp, dst=dst, bdst=bdst, scl=scl):
                            s, bs = stg.next()
                            k.op("act", ACT(s[:, :wd], ps[:, :wd], AF.Copy, scale=scl), reads=[bps], writes=[bs])
                            k.dma("sp", dst[hp][:, c0:c0 + wd], s[:, :wd], reads=[bs], writes=[bdst])
                        fm_group(w, bw, hp * 128, 128, ev_n)

                w, bw = load_w(wbr, wl[:, C_NV:C_NV + 384], 384)

                def ev_nv(ps, bps, t):
                    s, bs = nvs.next()
                    k.op("act", ACT(s[:, :, 0:64], ps[:, :384].rearrange("p (h c) -> p h c", h=6), AF.Copy), reads=[bps], writes=[bs])
                    k.dma("sp", s_nv[t * 128:(t + 1) * 128, :], s[:].rearrange("p h c -> p (h c)"), reads=[bs], writes=[B["s_nv"]])
                tm_group(w, bw, 384, ev_nv)
                while dq:
                    ada_tick()
                if deferred is not None:
                    dfin("b")
                AG(f_c[:, :], f_g[:, :], B["f_c"], B["f_g"])
                na_halo_exchange()
                k.flush()

        def fourier(l, last):
            k.stage = "fourier"
            with ExitStack() as s2:
                sb = lambda n, shp, dt: s2.enter_context(nc.sbuf_tensor(U("sb_" + n), shp, dt))
                d128 = sb("d128", [128, 2, 128], BF16); c64 = sb("c64", [128, 128], BF16); s64n = sb("s64n", [128, 128], BF16)
                dc256 = sb("dc256", [128, 2, 512], BF16)
                wfb = sb("wfb", [128, 2, 256], BF16)
                wcs = sb("wcs", [128, 2, 2, 256], BF16)
                AB = sb("AB", [128, 2, 2, TT], BF16)
                bAB = Buf("AB"); bfc = Buf("fconst"); bwcs = Buf("wcs")
                Xr = Rot(nc, s2, "Xr", [128, 64, 64], BF16, 2)
                Yr = Rot(nc, s2, "Yr", [128, 64, 64], BF16, 2)
                k2r = Rot(nc, s2, "k2r", [128, 64, 32], BF16, 2)
                k.dma("sp", d128[:], I["d128"].rearrange("p (a c) -> p a c", a=2), writes=[bfc])
                k.dma("sp", c64[:], I["c64"][:, :], writes=[bfc])
                k.dma("sp", s64n[:], I["s64n"][:, :], writes=[bfc])
                k.dma("sp", dc256[:], I["dc256"][:, :, :], writes=[bfc])
                wff = sb("wff", [128, 2, 256], F32)
                bwff = Buf("wff")
                k.dma("sp", wff[:], I["w_four"][l].rearrange("(g p) c -> p g c", p=128), writes=[bwff])
                k.op("act", ACT(wfb[:], wff[:], AF.Copy), reads=[bwff], writes=[bfc])
                for gp in range(2):
                    for ab, cm in ((0, c64), (1, s64n)):
                        ps, bps = psf.next()
                        k.op("pe", MM(ps[:, :256], cm[:], wfb[:, gp, :], True, True), reads=[bfc], writes=[bps])
                        k.op("act", ACT(wcs[:, gp, ab, :], ps[:, :256], AF.Copy), reads=[bps], writes=[bwcs])
                if not last:
                    for g in range(4):
                        gp, r0 = g // 2, (g % 2) * 64
                        ps, bps = psf.next()
                        for pt in range(2):
                            k.op("pe", MM(ps[r0:r0 + 64, :], fctx[:, pt, g * 64:(g + 1) * 64], dc256[:, pt, :], pt == 0, pt == 1),
                                 reads=[B["fctx"], bfc], writes=[bps], inc=(pt == 1), pe_accum=True)
                        k.op("act", ACT(AB[r0:r0 + 64, gp, :, TOK:TT], ps[r0:r0 + 64, :].rearrange("p (a t) -> p a t", a=2), AF.Copy),
                             reads=[bps], writes=[bAB])
                ei = 0
                for g in range(4):
                    gp, r0 = g // 2, (g % 2) * 64
                    X, bX = Xr.next()
                    k.dma("sp", X[:], f_g.rearrange("(r w) c -> r w c", w=64)[:, :, g * 64:(g + 1) * 64], reads=[B["f_g"]], writes=[bX])
                    for lq in range(2):
                        Y, bY = Yr.next()
                        for c8 in range(8):
                            ps, bps = psf.next()
                            for ci in range(8):
                                c = c8 * 8 + ci
                                k.op("pe", MM(ps[0:64, ci * 64:(ci + 1) * 64], X[:, :, c], d128[:, 0, lq * 64:(lq + 1) * 64], True, True),
                                     reads=[bX, bfc], writes=[bps], inc=False, pe_accum=True)
                                k.op("pe", MM(ps[64:128, ci * 64:(ci + 1) * 64], X[:, :, c], d128[:, 1, lq * 64:(lq + 1) * 64], True, True),
                                     reads=[bX, bfc], writes=[bps], inc=(ci == 7), pe_accum=True)
                            src = ps[:, :].rearrange("p (c l) -> p c l", c=8)
                            dstY = Y[:, c8 * 8:(c8 + 1) * 8, :]
                            ei += 1
                            if ei % 2:
                                k.op("act", ACT(dstY, src, AF.Copy), reads=[bps], writes=[bY])
                            else:
                                k.op("dve", CP(dstY, src), reads=[bps], writes=[bY])
                        k2, bk2 = k2r.next()
                        k.dma("sp", k2[:], I["k2s"][:, lq * 64:(lq + 1) * 64, :], writes=[bk2])
                        for hb in range(4):
                            ps, bps = psf.next()
                            for li in range(16):
                                l1i = hb * 16 + li
                                k.op("pe", MM(ps[r0:r0 + 64, li * 32:(li + 1) * 32], Y[:, :, l1i], k2[:, l1i, :], True, True),
                                     reads=[bY, bk2], writes=[bps], inc=(li == 15), pe_accum=True)
                            l1s = lq * 64 + hb * 16
                            src = ps[r0:r0 + 64, :].rearrange("p (l a m) -> p a m l", l=16, a=2)
                            dstA = AB[r0:r0 + 64, gp, :, 0:TOK].rearrange("p a (m l) -> p a m l", l=128)[:, :, :, l1s:l1s + 16]
                            k.op("dve", CP(dstA, src), reads=[bps], writes=[bAB])
                for bi, (c0, wd) in enumerate(BLOCKS):
                    if last and bi == 4:
                        continue
                    for mt_ in range(2):
                        ps, bps = psf.next()
                        n_ = 0
                        for gp in range(2):
                            for ab in range(2):
                                k.op("pe", MM(ps[:, :wd], wcs[:, gp, ab, mt_ * 128:(mt_ + 1) * 128], AB[:, gp, ab, c0:c0 + wd], n_ == 0, n_ == 3),
                                     reads=[bwcs, bAB], writes=[bps], inc=(n_ == 3), pe_accum=True)
                                n_ += 1
                        k.op("act", ACT(hy[:, mt_, c0:c0 + wd], ps[:, :wd], AF.Copy), reads=[bps], writes=[bhy[bi]])
                k.flush()

        def retention(l, last):
            k.stage = "retention"
            with ExitStack() as s2:
                sb = lambda n, shp, dt: s2.enter_context(nc.sbuf_tensor(U("sb_" + n), shp, dt))
                lg = sb("lg", [128, 8], F32); kd = sb("kd", [128, 8], F32); g128 = sb("g128", [96, 8], F32)
                cn = sb("cn", [96, 16, 8], F32); cx = sb("cx", [96, 5, 8], F32); qd = sb("qd", [96, 8, 128], F32)
                mt = sb("mt", [128, 4, 128], BF16); gain = sb("gain", [96, 4, 128], F32)
                dmask = sb("dmask", [128, 4, 128], F32); eq = sb("eq", [128, 2, 128], F32); ek = sb("ek", [128, 2], F32)
                ec = sb("ec", [128, 2, 16], F32); exmx = sb("exmx", [128, 2, 5, 2], F32)
                t1 = sb("t1", [128, 128], F32); t2 = sb("t2", [128, 128], F32)
                kdt = sb("kdt", [128, 8, 96], F32)
                tent = sb("tent", [96, 18, 8, 96], BF16)
                s3 = ExitStack()
                Tst = sb("Tst", [96, 8, 96], F32); Sin = sb("Sin", [96, 8, 96], F32); Sctx = sb("Sctx", [96, 8, 96], F32)
                Tst2 = sb("Tst2", [96, 8, 96], F32)
                TT2 = [Tst, Tst2]
                bTT = [[Buf(f"T{i_}_{d_}") for d_ in range(8)] for i_ in range(2)]
                par = [0, 0]
                kchr = Rot(nc, s2, "kch", [96, 4, 128], BF16, 2); qchr = Rot(nc, s2, "qch", [96, 4, 128], BF16, 2)
                gchr = Rot(nc, s2, "gch", [96, 4, 128], BF16, 3); vchr = Rot(nc, s2, "vch", [128, 384], BF16, 3)
                kvb = s3.enter_context(nc.sbuf_tensor(U("sb_kvb"), [96, 18, 4, 96], BF16))
                Sg = s3.enter_context(nc.sbuf_tensor(U("sb_Sg"), [96, 4, 768], F32))
                kfr = Rot(nc, s3, "kf", [128, 4, 96], BF16, 2); kbr = Rot(nc, s3, "kb", [128, 4, 96], BF16, 2)
                bT = Buf("Tst"); btent = Buf("tent"); bkvb = Buf("kvb"); bSin = Buf("Sin"); bSctx = Buf("Sctx"); bSg = Buf("Sg")
                b_cst = Buf("cst"); b_lg = Buf("lg"); b_kd = Buf("kd"); b_g = Buf("g128"); b_cn = Buf("cn"); b_cx = Buf("cx")
                b_qd = Buf("qd"); b_mt = Buf("mt"); b_t1 = Buf("t1"); b_t2 = Buf("t2"); b_kdt = Buf("kdt"); b_gt = Buf("gt"); b_gain = Buf("gain")
                RT = [b_lg, b_kd, b_g, b_cn, b_cx, b_qd, b_mt, b_kdt, b_gain]
                for dst_, nme in ((dmask, "dmask"), (eq, "eq"), (ek, "ek"), (ec, "ec"), (exmx, "exmx")):
                    k.dma("sp", dst_[:], I[nme], writes=[b_cst])
                k.dma("sp", lg[:], I["logit"][l], writes=[b_lg])
                k.dma("sp", gain[:], I["ret_gain"][l], writes=[b_gain])
                k.op("act", ACT(lg[:], lg[:], AF.Exp, scale=-1.0), reads=[b_lg], writes=[b_lg])
                k.op("dve", TS(lg[:], lg[:], 1.0, ALU.add), reads=[b_lg], writes=[b_lg])
                k.op("act", ACT(lg[:], lg[:], AF.Ln), reads=[b_lg], writes=[b_lg])
                k.op("dve", TS(lg[:], lg[:], -1.0, ALU.mult), reads=[b_lg], writes=[b_lg])
                k.op("act", ACT(g128[:], lg[:96, :], AF.Exp, scale=128.0), reads=[b_lg], writes=[b_g])
                k.op("dve", MS(kdt[:], 1.0), writes=[b_kdt])
                for dh in range(8):
                    d_ = dh // 4
                    sc_ = lg[:, dh:dh + 1]
                    k.op("act", ACT(kd[:, dh:dh + 1], ek[:, d_:d_ + 1], AF.Exp, scale=sc_), reads=[b_lg, b_cst], writes=[b_kd])
                    k.op("act", ACT(cn[:, :, dh], ec[:96, d_, :], AF.Exp, scale=lg[:96, dh:dh + 1]), reads=[b_lg, b_cst], writes=[b_cn])
                    k.op("act", ACT(cx[:, :, dh], exmx[:96, 0, :, d_], AF.Exp, scale=lg[:96, dh:dh + 1]), reads=[b_lg, b_cst], writes=[b_cx])
                    k.op("act", ACT(qd[:, dh, :], eq[:96, d_, :], AF.Exp, scale=lg[:96, dh:dh + 1]), reads=[b_lg, b_cst], writes=[b_qd])
                for dh in range(8):
                    d_ = dh // 4
                    k.op("dve", TTo(cx[:, :, dh], cx[:, :, dh], exmx[:96, 1, :, d_], ALU.mult), reads=[b_cx, b_cst], writes=[b_cx])
                    k.op("dve", TS(kdt[:, dh, :], kdt[:, dh, :], kd[:, dh:dh + 1], ALU.mult), reads=[b_kd, b_kdt], writes=[b_kdt])
                for h in range(4):
                    k.op("act", ACT(t1[:], dmask[:, 0, :], AF.Exp, scale=lg[:, h:h + 1]), reads=[b_lg, b_cst], writes=[b_t1])
                    k.op("act", ACT(t2[:], dmask[:, 2, :], AF.Exp, scale=lg[:, 4 + h:5 + h]), reads=[b_lg, b_cst], writes=[b_t2])
                    k.op("dve", TTo(t1[:], t1[:], dmask[:, 1, :], ALU.mult), reads=[b_t1, b_cst], writes=[b_t1])
                    k.op("dve", TTo(t2[:], t2[:], dmask[:, 3, :], ALU.mult), reads=[b_t2, b_cst], writes=[b_t2])
                    k.op("dve", TTo(mt[:, h, :], t1[:], t2[:], ALU.add), reads=[b_t1, b_t2], writes=[b_mt])
                k.op("dve", MS(Tst[:], 0.0), writes=[bT] + bTT[0])
                k.op("dve", MS(Tst2[:], 0.0), writes=bTT[1])
                if stop == "ret_tab":
                    s3.close()
                    k.flush()
                    return

                def tok0(n):
                    return n * 128

                def load_k(n):
                    kc, bkc = kchr.next()
                    k.dma("sp", kc[:], s_k[:, :, tok0(n):tok0(n) + 128].rearrange("h d t -> d h t"), reads=[B["s_k"]], writes=[bkc])
                    return kc, bkc

                def load_v(n):
                    vc, bvc = vchr.next()
                    k.dma("sp", vc[:], s_v[tok0(n):tok0(n) + 128, :], reads=[B["s_v"]], writes=[bvc])
                    return vc, bvc

                def preA(n):
                    kc, bkc = load_k(n)
                    vc, bvc = load_v(n)
                    pb, bpb = psb.next()
                    for h in range(4):
                        k.op("pe", TR(pb[:, h * 96:(h + 1) * 96], kc[:, h, :], identb[:96, :96]), reads=[bkc, bc], writes=[bpb],
                             inc=(h == 3), pe_accum=True)
                    kf, bkf = kfr.next(); kb, bkb = kbr.next()
                    pv_ = pb[:, 0:384].rearrange("p (h d) -> p h d", h=4)
                    k.op("dve", TTo(kf[:], pv_, kdt[:, 0:4, :], ALU.mult), reads=[bpb, *RT], writes=[bkf])
                    k.op("dve", TTo(kb[:], pv_, kdt[:, 4:8, :], ALU.mult), reads=[bpb, *RT], writes=[bkb])
                    return (vc, bvc, kf, bkf, kb, bkb)

                def preB(n, c_):
                    vc, bvc, kf, bkf, kb, bkb = c_
                    if n == 0:
                        cur_ = TT2[par[0]]
                        k.op("act", ACT(Sctx[:, 0:4, :], cur_[:, 0:4, :], AF.Copy), reads=bTT[par[0]][0:4], writes=[bSctx])
                        k.op("dve", MS(cur_[:, 0:4, :], 0.0), reads=[bSctx], writes=bTT[par[0]][0:4])
                    ps1, bps1 = psf.next(); ps2, bps2 = psf.next()
                    for h in range(4):
                        k.op("pe", MM(ps1[:96, h * 96:(h + 1) * 96], kf[:, h, :], vc[:, h * 96:(h + 1) * 96], True, True),
                             reads=[bkf, bvc], writes=[bps1], inc=(h == 3), pe_accum=True)
                    for h in range(4):
                        k.op("pe", MM(ps2[:96, h * 96:(h + 1) * 96], kb[:, h, :], vc[:, h * 96:(h + 1) * 96], True, True),
                             reads=[bkb, bvc], writes=[bps2], inc=(h == 3), pe_accum=True)
                    k.op("act", ACT(kvb[:, n, :, :], ps2[:96, 0:384].rearrange("p (h e) -> p h e", h=4), AF.Copy), reads=[bps2], writes=[bkvb])
                    p_ = par[0]
                    cur_, nxt_ = TT2[p_], TT2[1 - p_]
                    k.op("act", ACT(tent[:, n, 0:4, :], cur_[:, 0:4, :], AF.Copy), reads=bTT[p_][0:4], writes=[btent])
                    for h in range(4):
                        k.op("dve", STT(nxt_[:, h, :], cur_[:, h, :], g128[:, h:h + 1], ps1[:96, h * 96:(h + 1) * 96], ALU.mult, ALU.add),
                             reads=[bTT[p_][h], b_g, bps1], writes=[bTT[1 - p_][h]])
                    par[0] = 1 - p_

                order = [16, 17] + list(range(16))
                pend = preA(order[0])
                for oi, n in enumerate(order):
                    nxt = preA(order[oi + 1]) if oi + 1 < len(order) else None
                    preB(n, pend)
                    pend = nxt
                for n in [17, 16] + list(range(15, -1, -1)):
                    p_ = par[1]
                    cur_, nxt_ = TT2[p_], TT2[1 - p_]
                    if n == 15:
                        k.op("act", ACT(Sctx[:, 4:8, :], cur_[:, 4:8, :], AF.Copy), reads=bTT[p_][4:8], writes=[bSctx])
                        k.op("dve", MS(cur_[:, 4:8, :], 0.0), reads=[bSctx], writes=bTT[p_][4:8])
                    k.op("act", ACT(tent[:, n, 4:8, :], cur_[:, 4:8, :], AF.Copy), reads=bTT[p_][4:8], writes=[btent])
                    for h in range(4):
                        k.op("dve", STT(nxt_[:, 4 + h, :], cur_[:, 4 + h, :], g128[:, 4 + h:5 + h], kvb[:, n, h, :], ALU.mult, ALU.add),
                             reads=[bTT[p_][4 + h], b_g, bkvb], writes=[bTT[1 - p_][4 + h]])
                    par[1] = 1 - p_
                if stop == "ret_pre":
                    k.flush()
                    s3.close()
                    return
                k.dma("sp", st_c[:, 0:384], TT2[par[0]][:, 0:4, :].rearrange("p a e -> p (a e)"), reads=bTT[par[0]][0:4], writes=[B["st_c"]])
                k.dma("sp", st_c[:, 384:768], TT2[par[1]][:, 4:8, :].rearrange("p a e -> p (a e)"), reads=bTT[par[1]][4:8], writes=[B["st_c"]])
                AG(st_c[:, :], st_g[:, :], B["st_c"], B["st_g"])
                k.dma("sp", Sg[:], st_g.rearrange("(j p) c -> p j c", p=96), reads=[B["st_g"]], writes=[bSg])
                bSinL = [Buf(f"Sin{dh}") for dh in range(8)]
                for dh in range(8):
                    k.op("dve", TS(Sin[:, dh, :], Sctx[:, dh, :], cx[:, 4, dh:dh + 1], ALU.mult), reads=[bSctx, *RT], writes=[bSinL[dh]])
                for j in range(4):
                    for dh in range(8):
                        k.op("dve", STT(Sin[:, dh, :], Sg[:, j, dh * 96:(dh + 1) * 96], cx[:, j, dh:dh + 1], Sin[:, dh, :], ALU.mult, ALU.add),
                             reads=[bSg, *RT, bSinL[dh]], writes=[bSinL[dh]])
                k.op("dve", MS(epsc[:], EPS), reads=bSinL, writes=[bSin, bc])
                k.flush()
                s3.close()
                if stop == "ret_ag":
                    return
                Sfr = Rot(nc, s2, "Sf", [96, 8, 96], BF16, 3)
                qsr = Rot(nc, s2, "qs", [96, 8, 128], BF16, 3)
                Pr = Rot(nc, s2, "Pr", [128, 4, 128], BF16, 3)
                sqr2 = Rot(nc, s2, "sq2", [96, 512], BF16, 2)
                rs2 = Rot(nc, s2, "rs2", [96, 512], F32, 2)
                y1r = Rot(nc, s2, "y1r", [96, 512], F32, 2)
                y2r = Rot(nc, s2, "y2r", [96, 512], F32, 2)
                chunks = list(range(16)) + ([] if last else [16, 17])

                def phaseA(n):
                    kc, bkc = load_k(n)
                    vc, bvc = load_v(n)
                    qc, bqc = qchr.next()
                    k.dma("sp", qc[:], s_q[:, :, tok0(n):tok0(n) + 128].rearrange("h d t -> d h t"), reads=[B["s_q"]], writes=[bqc])
                    gc, bgc = gchr.next()
                    k.dma("sp", gc[:], s_g[:, :, tok0(n):tok0(n) + 128].rearrange("h d t -> d h t"), reads=[B["s_g"]], writes=[bgc])
                    S, bS = Sfr.next()
                    if n < 16:
                        for dh in range(8):
                            k.op("dve", STT(S[:, dh, :], Sin[:, dh, :], cn[:, n, dh:dh + 1], tent[:, n, dh, :], ALU.mult, ALU.add),
                                 reads=[bSin, *RT, btent], writes=[bS])
                    else:
                        k.op("act", ACT(S[:], tent[:, n, :, :], AF.Copy), reads=[btent], writes=[bS])
                    qs, bqs = qsr.next()
                    k.op("dve", TTo(qs[:, 0:4, :], qc[:], qd[:, 0:4, :], ALU.mult), reads=[bqc, *RT], writes=[bqs])
                    k.op("pool", TTo(qs[:, 4:8, :], qc[:], qd[:, 4:8, :], ALU.mult), reads=[bqc, *RT], writes=[bqs])
                    ps, bps = psf.next()
                    for h in range(4):
                        k.op("pe", MM(ps[:, h * 128:(h + 1) * 128], kc[:, h, :], qc[:, h, :], True, True), reads=[bkc, bqc], writes=[bps],
                             inc=(h == 3), pe_accum=True)
                    P, bP = Pr.next()
                    k.op("dve", TTo(P[:], ps[:].rearrange("p (h i) -> p h i", h=4), mt[:], ALU.mult), reads=[bps, *RT], writes=[bP])
                    return (vc, bvc, gc, bgc, S, bS, qs, bqs, P, bP)

                def phaseB(n, ctx_):
                    vc, bvc, gc, bgc, S, bS, qs, bqs, P, bP = ctx_
                    bi = blk_of(tok0(n))
                    po, bpo = psl.next()
                    for h in range(4):
                        o_ = po[:96, h * 128:(h + 1) * 128]
                        k.op("pe", MM(o_, vc[:, h * 96:(h + 1) * 96], P[:, h, :], True, False), reads=[bvc, bP], writes=[bpo], inc=False, pe_accum=True)
                        k.op("pe", MM(o_, S[:, h, :], qs[:, h, :], False, False), reads=[bS, bqs], writes=[bpo], inc=False, pe_accum=True)
                        k.op("pe", MM(o_, S[:, 4 + h, :], qs[:, 4 + h, :], False, True), reads=[bS, bqs], writes=[bpo], inc=(h == 3), pe_accum=True)
                    sq, bsq = sqr2.next()
                    k.op("act", ACT(sq[:], po[:96, :], AF.Square), reads=[bpo], writes=[bsq])
                    pss, bpss = psf.next()
                    k.op("pe", MM(pss[:96, :], onesb[:96, :96], sq[:], True, True), reads=[bsq, bc], writes=[bpss])
                    rs, brs = rs2.next()
                    k.op("act", ACT(rs[:], pss[:96, :], AF.Ln, scale=1.0 / 96, bias=epsc[:96, :]), reads=[bpss, bc], writes=[brs])
                    k.op("act", ACT(rs[:], rs[:], AF.Exp, scale=-0.5), reads=[brs], writes=[brs])
                    y1, by1 = y1r.next()
                    k.op("dve", TTo(y1[:], po[:96, :], rs[:], ALU.mult), reads=[bpo, brs], writes=[by1])
                    y2, by2 = y2r.next()
                    k.op("pool", TTo(y2[:], y1[:], gain[:].rearrange("p h i -> p (h i)"), ALU.mult), reads=[by1, *RT], writes=[by2])
                    k.op("pool", TTo(hy[:96, 2:6, tok0(n):tok0(n) + 128], y2[:].rearrange("p (h i) -> p h i", h=4), gc[:], ALU.mult),
                         reads=[by2, bgc], writes=[bhy[bi]])

                pq = [phaseA(chunks[0]), phaseA(chunks[1])]
                for ci, n in enumerate(chunks):
                    if ci + 2 < len(chunks):
                        pq.append(phaseA(chunks[ci + 2]))
                    phaseB(n, pq.pop(0))
                k.flush()

        def natten(l, last):
            k.stage = "natten"
            with ExitStack() as s2:
                sb = lambda n, shp, dt: s2.enter_context(nc.sbuf_tensor(U("sb_" + n), shp, dt))
                nkf = sb("nkf", [128, 3, 2560], BF16); nvf = sb("nvf", [128, 20, 390], BF16)
                nkc = sb("nkc", [128, 3, 256], BF16); nvc = sb("nvc", [128, 2, 390], BF16)
                bint = sb("bint", [128, 6, 5, 128], BF16); mh = sb("mh", [128, 8], F32)
                bnk = Buf("nkf"); bnv = Buf("nvf"); bctx = Buf("nctx"); bbi = Buf("bint"); bmh = Buf("mh")
                k.dma("sp", mh[:], I["mh"][:, :], writes=[bmh])
                k.dma("pool", bint[:], I["bias_int"][l], writes=[bbi], max_dma_last_dim=2048)
                k.dma("sp", nkf[:, :, 256:2304], s_nk[:, :, 0:TOK].rearrange("h p t -> p h t"), reads=[B["s_nk"]], writes=[bnk])
                k.dma("sp", nvf[:, 2:18, :], s_nv[0:TOK, :].rearrange("(t p) c -> p t c", p=128), reads=[B["s_nv"]], writes=[bnv])
                k.dma("sp", nkc[:], s_nk[:, :, TOK:TT].rearrange("h p t -> p h t"), reads=[B["s_nk"]], writes=[bctx])
                k.dma("sp", nvc[:], s_nv[TOK:TT, :].rearrange("(t p) c -> p t c", p=128), reads=[B["s_nv"]], writes=[bctx])
                with ExitStack() as s3:
                    keg = s3.enter_context(nc.sbuf_tensor(U("sb_keg"), [128, 4, 3, 448], BF16))
                    veg = s3.enter_context(nc.sbuf_tensor(U("sb_veg"), [128, 4, 4, 390], BF16))
                    bkeg = Buf("keg"); bveg = Buf("veg")
                    k.op("pool", MS(veg[:], 0.0), writes=[bveg])
                    k.op("pool", MS(nkf[:, :, 2496:2560], 0.0), writes=[bnk])
                    k.dma("sp", keg[:], ke_g.rearrange("(j h p) t -> p j h t", j=4, h=3), reads=[B["ke_g"]], writes=[bkeg])
                    vg4 = ve_g.rearrange("(j r) c -> j r c", j=4)
                    for j in range(4):
                        k.dma("sp", veg[:, j, 0, :], vg4[j, 0:128, :], reads=[B["ve_g"]], writes=[bveg])
                        k.dma("sp", veg[0:64, j, 1, :], vg4[j, 128:192, :], reads=[B["ve_g"]], writes=[bveg])
                        k.dma("sp", veg[:, j, 2:4, :], vg4[j, 192:448, :].rearrange("(t p) c -> p t c", p=128), reads=[B["ve_g"]], writes=[bveg])
                    for (dstk, srck, dstv, srcv, m0) in (
                            (nkf[:, :, 0:256], lambda j: keg[:, j, :, 192:448], nvf[:, 0:2, :], lambda j: veg[:, j, 2:4, :], 0),
                            (nkf[:, :, 2304:2496], lambda j: keg[:, j, :, 0:192], nvf[:, 18:20, :], lambda j: veg[:, j, 0:2, :], 4)):
                        k.op("dve", TS(dstk, srck(0), mh[:, m0:m0 + 1], ALU.mult), reads=[bkeg, bmh], writes=[bnk])
                        k.op("dve", TS(dstv, srcv(0), mh[:, m0:m0 + 1], ALU.mult), reads=[bveg, bmh], writes=[bnv])
                        for j in range(1, 4):
                            k.op("dve", STT(dstk, srck(j), mh[:, m0 + j:m0 + j + 1], dstk, ALU.mult, ALU.add), reads=[bkeg, bmh, bnk], writes=[bnk])
                            k.op("dve", STT(dstv, srcv(j), mh[:, m0 + j:m0 + j + 1], dstv, ALU.mult, ALU.add), reads=[bveg, bmh, bnv], writes=[bnv])
                    k.flush()
                wo = s2.enter_context(nc.sbuf_tensor(U("sb_wo"), [128, 9, D], BF16))
                bwo = Buf("wo")
                wsrc = I["w_out"][l]
                k.dma("pool", wo[:, 0:2, :], wsrc[0:256, :].rearrange("(s p) c -> p s c", p=128), writes=[bwo])
                k.dma("pool", wo[:96, 2:6, :], wsrc[256:640, :].rearrange("(s p) c -> p s c", p=96), writes=[bwo])
                k.dma("pool", wo[:, 6:9, :], wsrc[640:1024, :].rearrange("(s p) c -> p s c", p=128), writes=[bwo])
                nqr = Rot(nc, s2, "nqt", [128, 3, 128], BF16, 4)
                ber = Rot(nc, s2, "bedge", [128, 2, 6, 128], BF16, 4)
                ssr = Rot(nc, s2, "ssb", [128, 6, 128], F32, 3)
                Pr = Rot(nc, s2, "Pn", [128, 8, 128], BF16, 3)
                otr = Rot(nc, s2, "otok", [128, 384], BF16, 2)
                rdr = Rot(nc, s2, "rden", [128, 6], F32, 2)
                tiles = list(range(16)) + ([] if last else [16, 17])
                units = [(t, hp, hh) for t in tiles for hp in range(3) for hh in range(2)]
                tstate = {}
                sbanks = list(psf.items) + list(psl.items)
                sb_i = [0]
                pso_fix = (psb.items[0][0][:, :].bitcast(F32), psb.items[0][1])
                ptr_fix = psb.items[1]

                def next_bank():
                    it = sbanks[sb_i[0] % 6]
                    sb_i[0] += 1
                    return it

                def tile_cfg(t):
                    if t >= 16:
                        return 0, 0, None
                    if t == 0:
                        return 6, 0, 0
                    if t == 15:
                        return 6, 1792, 3
                    return 5, 128 * t, {1: 1, 14: 2}.get(t)

                def scoresU(u):
                    t, hp, hh = u
                    nkt, kb0, edge = tile_cfg(t)
                    t0_ = t * 128
                    if hp == 0 and hh == 0:
                        nq, bnq = nqr.next()
                        k.dma("sp", nq[:], s_nq[:, :, t0_:t0_ + 128].rearrange("h p t -> p h t"), reads=[B["s_nq"]], writes=[bnq])
                        tstate[t] = {"nq": (nq, bnq)}
                    ts_ = tstate[t]
                    nq, bnq = ts_["nq"]
                    if edge is not None and hh == 0:
                        be, bbe = ber.next()
                        k.dma("pool", be[:], I["bias_edge"][l][edge][:, 2 * hp:2 * hp + 2, :, :], writes=[bbe], max_dma_last_dim=2048)
                        ts_["be"] = (be, bbe)
                    r0 = 64 * hh
                    pA, bpA = next_bank(); pB, bpB = next_bank()
                    for kt in range(nkt):
                        bank, bbank = (pA, bpA) if kt < 4 else (pB, bpB)
                        sl = kt % 4
                        k.op("pe", MM(bank[:, sl * 128:(sl + 1) * 128], nkf[r0:r0 + 64, hp, kb0 + kt * 128:kb0 + (kt + 1) * 128],
                                      nq[r0:r0 + 64, hp, :], True, True), reads=[bnk, bnq], writes=[bbank], pe_accum=True)
                    for c in range(2):
                        k.op("pe", MM(pB[:, (2 + c) * 128:(3 + c) * 128], nkc[r0:r0 + 64, hp, c * 128:(c + 1) * 128], nq[r0:r0 + 64, hp, :], True, True),
                             reads=[bctx, bnq], writes=[bpB], pe_accum=True)
                    return (pA, bpA, pB, bpB, ts_.get("be"))

                def restU(u, sc_):
                    t, hp, hh = u
                    pA, bpA, pB, bpB, be_ = sc_
                    nkt, kb0, edge = tile_cfg(t)
                    t0_ = t * 128
                    bi = blk_of(t0_)
                    h = 2 * hp + hh
                    ts_ = tstate[t]
                    pso, bpso = pso_fix
                    P, bP = Pr.next()
                    if nkt > 0:
                        ss, bss = ssr.next()
                        if edge is not None:
                            bsrc, bbuf = be_[0][:, hh, :, :], be_[1]
                        else:
                            bsrc, bbuf = bint[:, h, :, :], bbi
                        k.op("dve", TTo(ss[:, 0:4, :], pA[:].rearrange("p (s q) -> p s q", s=4), bsrc[:, 0:4, :], ALU.add),
                             reads=[bpA, bbuf], writes=[bss])
                        k.op("dve", TTo(ss[:, 4:nkt, :], pB[:, 0:(nkt - 4) * 128].rearrange("p (s q) -> p s q", q=128), bsrc[:, 4:nkt, :], ALU.add),
                             reads=[bpB, bbuf], writes=[bss])
                        k.op("act", ACT(P[:, 0:nkt, :], ss[:, 0:nkt, :], AF.Exp), reads=[bss], writes=[bP])
                    k.op("act", ACT(P[:, 6:8, :], pB[:, 256:512].rearrange("p (s q) -> p s q", s=2), AF.Exp), reads=[bpB], writes=[bP])
                    o_ = pso[:, h * 65:(h + 1) * 65]
                    for kt in range(nkt):
                        k.op("pe", MM(o_, P[:, kt, :], nvf[:, kb0 // 128 + kt, h * 65:(h + 1) * 65], kt == 0, False),
                             reads=[bP, bnv], writes=[bpso], inc=False, pe_accum=True)
                    for c in range(2):
                        k.op("pe", MM(o_, P[:, 6 + c, :], nvc[:, c, h * 65:(h + 1) * 65], (nkt == 0 and c == 0), c == 1),
                             reads=[bP, bctx], writes=[bpso], inc=(c == 1), pe_accum=True)
                    if hp == 2 and hh == 1:
                        rd, brd = rdr.next()
                        k.op("dve", RCP(rd[:], pso[:, 0:390].rearrange("p (h c) -> p h c", c=65)[:, :, 64]), reads=[bpso], writes=[brd])
                        ot, bot = otr.next()
                        for h2 in range(6):
                            k.op("dve", TS(ot[:, h2 * 64:(h2 + 1) * 64], pso[:, h2 * 65:h2 * 65 + 64], rd[:, h2:h2 + 1], ALU.mult), reads=[bpso, brd], writes=[bot])
                        pb, bpb = ptr_fix
                        for hp2 in range(3):
                            k.op("pe", TR(pb[:, hp2 * 128:(hp2 + 1) * 128], ot[:, hp2 * 128:(hp2 + 1) * 128], identb[:]), reads=[bot, bc], writes=[bpb],
                                 inc=(hp2 == 2), pe_accum=True)
                        k.op("act", ACT(hy[:, 6:9, t0_:t0_ + 128], pb[:, 0:384].rearrange("p (s q) -> p s q", s=3), AF.Copy), reads=[bpb], writes=[bhy[bi]])
                        del tstate[t]

                pendq = [scoresU(units[0])]
                if len(units) > 1:
                    pendq.append(scoresU(units[1]))
                for ui, u in enumerate(units):
                    if ui + 2 < len(units):
                        pendq.append(scoresU(units[ui + 2]))
                    restU(u, pendq.pop(0))
                if stop != "na":
                    k.stage = "wout"
                    wout_body(l, last, wo, bwo)
                k.flush()

        def wout_body(l, last, wo, bwo):
            KS = [128, 128, 96, 96, 96, 96, 128, 128, 128]
            for bi, (c0, wd) in enumerate(BLOCKS):
                if last and bi == 4:
                    continue
                j = 0 if bi < 4 else 1
                for ct in range(8):
                    ps, bps = psf.next()
                    for s_ in range(9):
                        K_ = KS[s_]
                        k.op("pe", MM(ps[:, :wd], wo[:K_, s_, ct * 128:(ct + 1) * 128], hy[:K_, s_, c0:c0 + wd], s_ == 0, s_ == 8),
                             reads=[bwo, bhy[bi]], writes=[bps], inc=(s_ == 8), pe_accum=True)
                    k.op("dve", STT(xT[:, ct, c0:c0 + wd], ps[:, :wd], mod[:, 16 + ct, j:j + 1], xT[:, ct, c0:c0 + wd], ALU.mult, ALU.add),
                         reads=[bps, B["mod"], bx[bi]], writes=[bx[bi]])

        def ffn(l, last):
            k.stage = "ffn"
            blocks = [bi for bi in range(5) if not (last and bi == 4)]
            with ExitStack() as s2:
                w2r = s2.enter_context(nc.sbuf_tensor(U("sb_w2r"), [128, 22, D], BF16))
                bw2 = Buf("w2r")
                w13 = Rot(nc, s2, "w13", [128, 8, 256], BF16, 2)
                slr = Rot(nc, s2, "sil", [128, 512], F32, 2)
                ust = Rot(nc, s2, "ust", [128, 512], BF16, 3)
                ubr = Rot(nc, s2, "ub", [128, 22, 512], BF16, 1)
                nxt_ada = None
                if l + 1 < nlayers:
                    wrot = Rot(nc, s2, "wada", [128, 8, 256], BF16, 2)
                    nxt_ada = ada_steps(l + 1, wrot)
                ai = 0
                for ft in range(22):
                    w, bw = w13.next()
                    k.dma("pool", w[:, :, 0:128], I["w1"][l][:, ft * 128:(ft + 1) * 128].rearrange("(kt p) c -> p kt c", p=128), writes=[bw])
                    k.dma("pool", w[:, :, 128:256], I["w3"][l][:, ft * 128:(ft + 1) * 128].rearrange("(kt p) c -> p kt c", p=128), writes=[bw])
                    if ft in (2, 6, 10, 14):
                        cq = (ft - 2) // 4
                        k.dma("pool", w2r[:, :, cq * 256:(cq + 1) * 256],
                              I["w2"][l][:, cq * 256:(cq + 1) * 256].rearrange("(kt p) c -> p kt c", p=128), writes=[bw2])
                    for bi in blocks:
                        c0, wd = BLOCKS[bi]
                        pa, bpa = psf.next(); pb_, bpb_ = psf.next()
                        for kt in range(8):
                            k.op("pe", MM(pa[:, :wd], w[:, kt, 0:128], hy[:, kt, c0:c0 + wd], kt == 0, kt == 7), reads=[bw, bhy[bi]], writes=[bpa],
                                 inc=(kt == 7), pe_accum=True)
                        for kt in range(8):
                            k.op("pe", MM(pb_[:, :wd], w[:, kt, 128:256], hy[:, kt, c0:c0 + wd], kt == 0, kt == 7), reads=[bw, bhy[bi]], writes=[bpb_],
                                 inc=(kt == 7), pe_accum=True)
                        sl, bsl = slr.next()
                        k.op("act", ACT(sl[:, :wd], pa[:, :wd], AF.Silu), reads=[bpa], writes=[bsl])
                        u, bu = ust.next()
                        k.op("dve", TTo(u[:, :wd], sl[:, :wd], pb_[:, :wd], ALU.mult), reads=[bsl, bpb_], writes=[bu])
                        k.dma("sp", s_u[ft][:, c0:c0 + wd], u[:, :wd], reads=[bu], writes=[B["s_u"]])
                    if nxt_ada is not None and ai < 24:
                        nxt_ada[0][ai](); ai += 1
                ub_a, bub_a = ubr.next()
                ub_h = hy[:].rearrange("p s t -> p (s t)")[:, 0:22 * 512].rearrange("p (f t) -> p f t", f=22)
                for ii_, bi in enumerate(blocks):
                    c0, wd = BLOCKS[bi]
                    j = 0 if bi < 4 else 1
                    if ii_ % 2 == 0:
                        ub, rd_b, wr_b = ub_a, [bub_a], [bub_a]
                    else:
                        ub, rd_b, wr_b = ub_h, list(bhy), list(bhy)
                    k.dma("sp", ub[:, :, :wd], s_u[:, :, c0:c0 + wd].rearrange("f p t -> p f t"), reads=[B["s_u"]], writes=wr_b)
                    for ct in range(8):
                        ps, bps = psf.next()
                        for ft in range(22):
                            k.op("pe", MM(ps[:, :wd], w2r[:, ft, ct * 128:(ct + 1) * 128], ub[:, ft, :wd], ft == 0, ft == 21),
                                 reads=[bw2] + rd_b, writes=[bps], inc=(ft == 21), pe_accum=True)
                        k.op("dve", STT(xT[:, ct, c0:c0 + wd], ps[:, :wd], mod[:, 40 + ct, j:j + 1], xT[:, ct, c0:c0 + wd], ALU.mult, ALU.add),
                             reads=[bps, B["mod"], bx[bi]], writes=[bx[bi]])
                    if nxt_ada is not None and ai < 24:
                        nxt_ada[0][ai](); ai += 1
                if nxt_ada is not None:
                    while ai < 24:
                        nxt_ada[0][ai](); ai += 1
                    nxt_ada[1]()
                k.flush()

        def final_out():
            k.stage = "final_out"
            with ExitStack() as s2:
                norm(0, range(4), final=True, scope=s2)
                otl = Rot(nc, s2, "otile", [128, D], F32, 2)
                for t in range(16):
                    bi = blk_of(t * 128)
                    o, bo = otl.next()
                    for half in range(2):
                        ps, bps = psf.next()
                        for j in range(4):
                            kt = half * 4 + j
                            k.op("pe", TR(ps[:, j * 128:(j + 1) * 128], xT[:, kt, t * 128:(t + 1) * 128], identf[:]), reads=[bx[bi], bc], writes=[bps],
                                 inc=(j == 3), pe_accum=True)
                        if half == 0:
                            k.op("act", ACT(o[:, 0:512], ps[:], AF.Copy), reads=[bps], writes=[bo])
                        else:
                            k.op("dve", CP(o[:, 512:1024], ps[:]), reads=[bps], writes=[bo])
                    k.dma("sp", out_d[t * 128:(t + 1) * 128, :], o[:], reads=[bo], writes=[B["out"]])
                k.flush()

        done = False
        for l in range(nlayers):
            last = (l == DEPTH - 1)
            deferred_ada = None
            if only is None and l == 0:
                deferred_ada = ada(l)
            if debug and l == 0 and only is None:
                k.dma("sp", d_mod[:, :, :], mod[:], reads=[B["mod"]])
            if stop == "norm":
                norm(0, range(5))
                break
            if only is None:
                project(l, last, deferred=deferred_ada)
            if stop == "proj":
                break
            if only is None:
                fourier(l, last)
            if stop == "four":
                break
            if only in (None, "ret"):
                retention(l, last)
            if stop in ("ret", "ret_tab", "ret_pre", "ret_ag"):
                break
            natten(l, last)
            if stop == "na":
                break
            if stop == "wout":
                break
            norm(1, range(4) if last else range(5))
            ffn(l, last)
            if stop == "ffn":
                break
        else:
            if nlayers == DEPTH:
                final_out()
        if debug and only is None:
            k.dma("sp", d_xT[:, :, :], xT[:], reads=bx)
            k.dma("sp", d_hy[:, :, :], hy[:], reads=bhy)
        k.flush(final=True)
    return nc


_NC_CACHE = {}


def kernel(**inputs):
    inp = {k_: np.asarray(v) for k_, v in inputs.items()}
    maps = make_in_maps(inp)
    if "nc" not in _NC_CACHE:
        _NC_CACHE["nc"] = build()
    res = run_bass_kernel_spmd(_NC_CACHE["nc"], maps, core_ids=list(range(8)))
    out = np.empty((2, L, D), np.float32)
    for core in range(8):
        b, q = core // 4, core % 4
        out[b, TOK * q:TOK * (q + 1)] = np.asarray(res.results[core]["out"], np.float32)
    return out
```

```python
import math
from contextlib import ExitStack
import numpy as np
import ml_dtypes
import concourse.bass as bass
import concourse.mybir as mybir
from concourse.bass_utils import run_bass_kernel_spmd

F32 = mybir.dt.float32
BF16 = mybir.dt.bfloat16
AF = mybir.ActivationFunctionType
ALU = mybir.AluOpType
NPBF = ml_dtypes.bfloat16

D = 1024; L = 8192; LC = 256; DEPTH = 2; PW = 2944; DFF = 2816
TOK = 2048; TT = 2304; NCH = 18
BLOCKS = [(0, 512), (512, 512), (1024, 512), (1536, 512), (2048, 256)]
C_F, C_RQ, C_RK, C_RV, C_RG, C_NQ, C_NK, C_NV = 0, 256, 640, 1024, 1408, 1792, 2176, 2560
EPS = 1e-6
GROUPS = [[0, 1, 2, 3], [4, 5, 6, 7]]


class Buf:
    __slots__ = ("name", "w", "r")

    def __init__(self, name=""):
        self.name = name
        self.w = None
        self.r = {}


class K:
    ENGS = ("pe", "dve", "act", "pool", "sp")

    def __init__(self, nc, st):
        self.nc = nc
        self.prog = {e: [] for e in self.ENGS}
        self.cnt = {}
        self.seen = {e: {} for e in self.ENGS}
        self.sems = {}
        names = ["c_pe", "c_dve", "c_act", "c_pool", "cc"]
        self.ndma = 8
        self.dma_rr = {}
        for e in ("sp", "act", "pool"):
            self.dma_rr[e] = 0
            names += [f"d_{e}{i}" for i in range(self.ndma)]
        for sn in names:
            self.cnt[sn] = 0
            self.sems[sn] = st.enter_context(nc.semaphore(sn))
        self.nblk = 0
        self.stage = "init"

    def _need(self, eng, ev, waits):
        if ev is None:
            return
        sn, val = ev
        if self.seen[eng].get(sn, 0) >= val:
            return
        waits[sn] = max(waits.get(sn, 0), val)

    def _deps(self, eng, reads, writes, pe_accum=False):
        waits = {}
        for b in reads:
            self._need(eng, b.w, waits)
        own = "c_" + eng
        for b in writes:
            if not (b.w is not None and b.w[0] == own and (pe_accum or eng != "pe")):
                self._need(eng, b.w, waits)
            for sn, val in b.r.items():
                self._need(eng, (sn, val), waits)
        for sn, val in waits.items():
            self.seen[eng][sn] = val
            self.prog[eng].append(("wait", sn, val))

    def _mark(self, ev, reads, writes):
        sn, val = ev
        for b in reads:
            b.r[sn] = max(b.r.get(sn, 0), val)
        for b in writes:
            b.w = ev
            b.r = {}

    def op(self, eng, fn, reads=(), writes=(), inc=True, pe_accum=False):
        self._deps(eng, reads, writes, pe_accum)
        sn = "c_" + eng
        val = self.cnt[sn] + 1
        self._mark((sn, val), reads, writes)
        if inc:
            self.cnt[sn] = val
            self.prog[eng].append(("op", fn, sn, 1))
        else:
            self.prog[eng].append(("op", fn, None, 0))

    def dma(self, eng, out, in_, reads=(), writes=(), **kw):
        self._deps(eng, reads, writes)
        i = self.dma_rr[eng]
        self.dma_rr[eng] = (i + 1) % self.ndma
        sn = f"d_{eng}{i}"
        prev = self.cnt[sn]
        if prev > 0 and self.seen[eng].get(sn, 0) < prev:
            self.seen[eng][sn] = prev
            self.prog[eng].append(("wait", sn, prev))
        val = prev + 16
        self.cnt[sn] = val
        self._mark((sn, val), reads, writes)
        self.prog[eng].append(("dma", out, in_, sn, kw))

    def cc(self, fn, reads=(), writes=()):
        eng = "pool"
        self._deps(eng, reads, writes)
        sn = "cc"
        val = self.cnt[sn] + 1
        self.cnt[sn] = val
        self._mark((sn, val), reads, writes)
        self.prog[eng].append(("op", fn, sn, None))

    def barrier(self, final=False):
        snap = dict(self.cnt)
        for e in self.ENGS:
            for sn, val in snap.items():
                if sn == "cc" and not final:
                    continue
                if val > 0 and self.seen[e].get(sn, 0) < val:
                    self.seen[e][sn] = val
                    self.prog[e].append(("wait", sn, val))

    def flush(self, final=False):
        self.barrier(final)
        nc = self.nc
        sems = self.sems
        prog = self.prog
        self.prog = {e: [] for e in self.ENGS}
        self.nblk += 1

        def run(e, items):
            for it in items:
                if it[0] == "wait":
                    e.wait_ge(sems[it[1]], it[2])
                elif it[0] == "op":
                    ins = it[1](e)
                    if it[2] is not None:
                        if it[3] is None:
                            ins.then_inc(sems[it[2]])
                        else:
                            ins.then_inc(sems[it[2]], it[3])
                else:
                    e.dma_start(out=it[1], in_=it[2], **it[4]).then_inc(sems[it[3]], 16)

        with nc.named_scope(f"{self.stage}_{self.nblk}"), nc.Block() as block:
            @block.tensor
            def _(e):
                run(e, prog["pe"])

            @block.vector
            def _(e):
                run(e, prog["dve"])

            @block.scalar
            def _(e):
                run(e, prog["act"])

            @block.gpsimd
            def _(e):
                run(e, prog["pool"])

            @block.sync
            def _(e):
                run(e, prog["sp"])


_UC = [0]


def U(name):
    _UC[0] += 1
    return f"{name}_{_UC[0]}"


class Rot:
    def __init__(self, nc, st, name, shape, dtype, n, psum=False):
        self.items = []
        for i in range(n):
            if psum:
                t = st.enter_context(nc.psum_tensor(U(f"ps_{name}{i}"), shape, dtype))
            else:
                t = st.enter_context(nc.sbuf_tensor(U(f"sb_{name}{i}"), shape, dtype))
            self.items.append((t, Buf(f"{name}{i}")))
        self.i = 0

    def next(self):
        it = self.items[self.i]
        self.i = (self.i + 1) % len(self.items)
        return it


def _bf(a):
    return np.ascontiguousarray(np.asarray(a, np.float32).astype(NPBF))


def _consts_common():
    c = {}
    c["ident_f"] = np.eye(128, dtype=np.float32)
    c["ident_b"] = _bf(np.eye(128))
    c["ones_b"] = _bf(np.ones((128, 128)))
    j = np.arange(128)[:, None]; i = np.arange(128)[None, :]
    dm = np.zeros((128, 4, 128), np.float32)
    dm[:, 0] = np.maximum(i - j, 0); dm[:, 1] = (i >= j)
    dm[:, 2] = np.maximum(j - i, 0); dm[:, 3] = (j >= i)
    c["dmask"] = dm
    eq = np.zeros((128, 2, 128), np.float32)
    eq[:, 0, :] = np.arange(128)[None, :] + 1.0
    eq[:, 1, :] = 128.0 - np.arange(128)[None, :]
    c["eq"] = eq
    ek = np.zeros((128, 2), np.float32)
    ek[:, 0] = 127.0 - np.arange(128); ek[:, 1] = np.arange(128)
    c["ek"] = ek
    ec = np.zeros((128, 2, 16), np.float32)
    ec[:, 0, :] = 128.0 * np.arange(16)[None, :]
    ec[:, 1, :] = 128.0 * (15 - np.arange(16))[None, :]
    c["ec"] = ec
    r = np.arange(128)[:, None].astype(np.float64); l1 = np.arange(128)[None, :].astype(np.float64)
    nrm = 1.0 / math.sqrt(L * 64.0)
    ang = 2 * np.pi * r * l1 / 128.0
    c["d128"] = _bf(np.concatenate([np.cos(ang) * nrm, -np.sin(ang) * nrm], 1))
    cc = np.arange(64)[:, None].astype(np.float64); cp = np.arange(64)[None, :].astype(np.float64)
    a64 = 2 * np.pi * cc * cp / 64.0
    c64 = np.zeros((128, 128)); s64 = np.zeros((128, 128))
    for g in range(2):
        c64[64 * g:64 * g + 64, 64 * g:64 * g + 64] = np.cos(a64)
        s64[64 * g:64 * g + 64, 64 * g:64 * g + 64] = -np.sin(a64)
    c["c64"] = _bf(c64); c["s64n"] = _bf(s64)
    pos = np.arange(256).astype(np.float64)[:, None]; lp = np.arange(256).astype(np.float64)[None, :]
    a256 = 2 * np.pi * pos * lp / 256.0
    nc_ = 1.0 / math.sqrt(256 * 64.0)
    dc = np.concatenate([np.cos(a256) * nc_, np.sin(a256) * nc_], 1)
    c["dc256"] = _bf(dc.reshape(2, 128, 512).transpose(1, 0, 2))
    return c


def _consts_core(q):
    c = {}
    t = np.arange(TOK) + TOK * q
    prow = (t // 64).astype(np.float32); pcol = (t % 64).astype(np.float32)
    inv = (10000.0 ** (-np.arange(24, dtype=np.float32) / 24)).astype(np.float32)
    cos = np.zeros((96, TOK), np.float32); sin = np.zeros((96, TOK), np.float32)
    for i in range(96):
        part, rr = i // 48, i % 48
        m = rr % 24
        ang = ((prow if part == 0 else pcol) * inv[m]).astype(np.float32)
        cos[i] = np.cos(ang)
        sin[i] = -np.sin(ang) if rr < 24 else np.sin(ang)
    ks = np.float32(96 ** -0.5)
    c["rope"] = np.stack([cos, sin, cos * ks, sin * ks], 1)
    ex = np.zeros((128, 5, 2), np.float32); mx = np.zeros((128, 5, 2), np.float32)
    for j in range(4):
        if j < q:
            ex[:, j, 0] = 2048.0 * (q - 1 - j); mx[:, j, 0] = 1
        if j > q:
            ex[:, j, 1] = 2048.0 * (j - q - 1); mx[:, j, 1] = 1
    ex[:, 4, 0] = 2048.0 * q; mx[:, 4, 0] = 1
    ex[:, 4, 1] = 2048.0 * (3 - q); mx[:, 4, 1] = 1
    c["exmx"] = np.stack([ex, mx], 1)
    mh = np.zeros((128, 8), np.float32)
    if q > 0:
        mh[:, q - 1] = 1
    if q < 3:
        mh[:, 4 + q + 1] = 1
    c["mh"] = mh
    w = np.arange(64, dtype=np.float64)[:, None, None]
    l1 = np.arange(128, dtype=np.float64)[None, :, None]
    l2 = (16 * q + np.arange(16, dtype=np.float64))[None, None, :]
    th = 2 * np.pi * w * (l1 + 128 * l2) / 8192.0
    k2 = np.zeros((64, 128, 2, 32))
    k2[:, :, 0, 0:16] = np.cos(th); k2[:, :, 0, 16:32] = np.sin(th)
    k2[:, :, 1, 0:16] = np.sin(th); k2[:, :, 1, 16:32] = -np.cos(th)
    c["k2s"] = _bf(k2.transpose(2, 0, 1, 3).reshape(128, 128, 32))
    return c


def _na_bias(rpb, q):
    NEG = np.float32(-1e30)

    def table(t, kb0, nkt):
        qi = np.arange(128)
        lr = 2 * t + qi // 64
        r = 32 * q + lr
        col = qi % 64
        kb = kb0 + np.arange(nkt * 128)
        br = kb // 64; kc = kb % 64
        kr = 32 * q + (br - 4)
        r0 = np.clip(r - 4, 0, 120)
        rok = (kr[:, None] >= r0[None, :]) & (kr[:, None] < r0[None, :] + 8) & (kr[:, None] >= 0) & (kr[:, None] < 128)
        ws = np.clip(col - 8, 0, 48)
        cok = (kc[:, None] >= ws[None, :]) & (kc[:, None] < ws[None, :] + 16)
        ok = rok & cok
        dr = np.clip(kr[:, None] - r[None, :] + 7, 0, 14)
        dcc = np.clip(kc[:, None] - col[None, :] + 15, 0, 30)
        out = np.empty((6, nkt * 128, 128), np.float32)
        for h in range(6):
            out[h] = np.where(ok, rpb[h][dr, dcc], NEG)
        return out.reshape(6, nkt, 128, 128).transpose(2, 0, 1, 3)

    inter = table(5, 128 * 5, 5)
    edge = np.full((4, 128, 6, 6, 128), NEG, np.float32)
    edge[0] = table(0, 0, 6)
    edge[1][:, :, :5] = table(1, 128, 5)
    edge[2][:, :, :5] = table(14, 128 * 14, 5)
    edge[3] = table(15, 1792, 6)
    return np.ascontiguousarray(inter), np.ascontiguousarray(edge)


def _col8(v):
    return np.ascontiguousarray(np.asarray(v, np.float32).reshape(8, 128).T)


def make_in_maps(inp):
    cm = _consts_common()
    maps = []
    for core in range(8):
        b, q = core // 4, core % 4
        m = dict(cm)
        m.update(_consts_core(q))
        m["x_in"] = np.ascontiguousarray(inp["x"][b, TOK * q:TOK * (q + 1)])
        m["ctx_in"] = np.ascontiguousarray(inp["ctx"][b])
        m["cvec"] = np.ascontiguousarray(np.stack([_col8(inp["c"][b]), _col8(inp["c_ctx"])], 2))
        m["b_ada"] = np.ascontiguousarray(np.stack(
            [np.asarray(inp["b_ada"][l], np.float32).reshape(48, 128).T for l in range(DEPTH)], 0))
        m["g_mix"] = np.stack([_col8(inp["g_mix"][l]) for l in range(DEPTH)], 0)
        m["g_ffn"] = np.stack([_col8(inp["g_ffn"][l]) for l in range(DEPTH)], 0)
        m["g_final"] = _col8(inp["g_final"])
        rg = np.asarray(inp["ret_norm_g"], np.float32).reshape(DEPTH, 4, 96).transpose(0, 2, 1)
        m["ret_gain"] = np.ascontiguousarray(np.broadcast_to(rg[:, :, :, None], (DEPTH, 96, 4, 128)))
        lg = np.asarray(inp["ret_decay_logit"], np.float32).reshape(DEPTH, 1, 8)
        m["logit"] = np.ascontiguousarray(np.broadcast_to(lg, (DEPTH, 128, 8)))
        bi, be = [], []
        for l in range(DEPTH):
            a, e = _na_bias(np.asarray(inp["na_rpb"][l], np.float32), q)
            bi.append(a); be.append(e)
        m["bias_int"] = np.stack(bi, 0)
        m["bias_edge"] = np.stack(be, 0)
        for nme in ("w_ada", "w_in", "w_four", "w_out", "w1", "w3", "w2"):
            m[nme] = np.ascontiguousarray(np.asarray(inp[nme], np.float32))
        maps.append(m)
    return maps


IN_SPECS = {
    "x_in": ([TOK, D], F32), "ctx_in": ([LC, D], F32), "cvec": ([128, 8, 2], F32),
    "b_ada": ([DEPTH, 128, 48], F32), "g_mix": ([DEPTH, 128, 8], F32), "g_ffn": ([DEPTH, 128, 8], F32),
    "g_final": ([128, 8], F32), "ret_gain": ([DEPTH, 96, 4, 128], F32), "logit": ([DEPTH, 128, 8], F32),
    "bias_int": ([DEPTH, 128, 6, 5, 128], F32), "bias_edge": ([DEPTH, 4, 128, 6, 6, 128], F32),
    "w_ada": ([DEPTH, D, 6 * D], F32), "w_in": ([DEPTH, D, PW], F32), "w_four": ([DEPTH, 256, 256], F32),
    "w_out": ([DEPTH, D, D], F32), "w1": ([DEPTH, D, DFF], F32), "w3": ([DEPTH, D, DFF], F32),
    "w2": ([DEPTH, DFF, D], F32),
    "ident_f": ([128, 128], F32), "ident_b": ([128, 128], BF16), "ones_b": ([128, 128], BF16),
    "dmask": ([128, 4, 128], F32), "eq": ([128, 2, 128], F32), "ek": ([128, 2], F32), "ec": ([128, 2, 16], F32),
    "d128": ([128, 256], BF16), "c64": ([128, 128], BF16), "s64n": ([128, 128], BF16), "dc256": ([128, 2, 512], BF16),
    "rope": ([96, 4, TOK], F32), "exmx": ([128, 2, 5, 2], F32), "mh": ([128, 8], F32), "k2s": ([128, 128, 32], BF16),
}


def MM(out, lhsT, rhs, start, stop):
    return lambda e: e.matmul(out, lhsT, rhs, start=start, stop=stop)


def TR(out, in_, ident):
    return lambda e: e.transpose(out=out, in_=in_, identity=ident)


def ACT(out, in_, func, **kw):
    return lambda e: e.activation(out=out, in_=in_, func=func, **kw)


def TTo(out, in0, in1, op):
    return lambda e: e.tensor_tensor(out=out, in0=in0, in1=in1, op=op)


def TS(out, in0, s1, op0, s2=None, op1=None):
    if op1 is None:
        return lambda e: e.tensor_scalar(out=out, in0=in0, scalar1=s1, scalar2=None, op0=op0)
    return lambda e: e.tensor_scalar(out=out, in0=in0, scalar1=s1, scalar2=s2, op0=op0, op1=op1)


def STT(out, in0, scalar, in1, op0, op1):
    return lambda e: e.scalar_tensor_tensor(out=out, in0=in0, scalar=scalar, in1=in1, op0=op0, op1=op1)


def CP(out, in_):
    return lambda e: e.tensor_copy(out=out, in_=in_)


def MS(ap, v):
    return lambda e: e.memset(ap, v)


def RCP(out, in_):
    return lambda e: e.reciprocal(out=out, in_=in_)


def build(debug=False, nlayers=DEPTH, stop=None, only=None):
    nc = bass.Bass("TRN2", target_bir_lowering=False)
    I = {n: nc.dram_tensor(n, shp, dt, kind="ExternalInput").ap() for n, (shp, dt) in IN_SPECS.items()}
    out_d = nc.dram_tensor("out", [TOK, D], F32, kind="ExternalOutput").ap()
    dbgk = "ExternalOutput" if debug else None

    def scr(name, shape, dt=BF16, dbg=True):
        if debug and dbg:
            return nc.dram_tensor(name, shape, dt, kind="ExternalOutput").ap()
        return nc.dram_tensor(name, shape, dt).ap()

    s_q = scr("s_q", [4, 96, TT]); s_k = scr("s_k", [4, 96, TT]); s_g = scr("s_g", [4, 96, TT])
    s_v = scr("s_v", [TT, 384]); s_nq = scr("s_nq", [3, 128, TT]); s_nk = scr("s_nk", [3, 128, TT])
    s_nv = scr("s_nv", [TT, 390]); s_u = scr("s_u", [22, 128, TT], dbg=False)
    f_c = scr("f_c", [TOK, 256], dbg=False); f_g = scr("f_g", [4 * TOK, 256], dbg=False)
    st_c = scr("st_c", [96, 768], F32, dbg=False); st_g = scr("st_g", [4 * 96, 768], F32, dbg=False)
    ke_c = scr("ke_c", [384, 448], dbg=False); ke_g = scr("ke_g", [4 * 384, 448], dbg=False)
    ve_c = scr("ve_c", [448, 390], dbg=False); ve_g = scr("ve_g", [4 * 448, 390], dbg=False)
    if debug:
        d_xT = nc.dram_tensor("d_xT", [128, 8, TT], F32, kind="ExternalOutput").ap()
        d_hy = nc.dram_tensor("d_hy", [128, 9, TT], BF16, kind="ExternalOutput").ap()
        d_mod = nc.dram_tensor("d_mod", [128, 48, 2], F32, kind="ExternalOutput").ap()
    B = {n: Buf(n) for n in ("s_q s_k s_g s_v s_nq s_nk s_nv s_u f_c f_g st_c st_g ke_c ke_g ve_c ve_g out "
                             "const mod gs fctx").split()}

    with ExitStack() as st:
        k = K(nc, st)
        xT = st.enter_context(nc.sbuf_tensor(U("sb_xT"), [128, 8, TT], F32))
        hy = st.enter_context(nc.sbuf_tensor(U("sb_hy"), [128, 9, TT], BF16))
        bx = [Buf(f"x{i}") for i in range(5)]
        bhy = [Buf(f"hy{i}") for i in range(5)]
        identf = st.enter_context(nc.sbuf_tensor(U("sb_identf"), [128, 128], F32))
        identb = st.enter_context(nc.sbuf_tensor(U("sb_identb"), [128, 128], BF16))
        onesb = st.enter_context(nc.sbuf_tensor(U("sb_onesb"), [128, 128], BF16))
        epsc = st.enter_context(nc.sbuf_tensor(U("sb_epsc"), [128, 1], F32))
        cvec = st.enter_context(nc.sbuf_tensor(U("sb_cvec"), [128, 8, 2], F32))
        scv = st.enter_context(nc.sbuf_tensor(U("sb_scv"), [128, 8, 2], F32))
        scvb = st.enter_context(nc.sbuf_tensor(U("sb_scvb"), [128, 8, 2], BF16))
        mod = st.enter_context(nc.sbuf_tensor(U("sb_mod"), [128, 48, 2], F32))
        gs = st.enter_context(nc.sbuf_tensor(U("sb_gs"), [128, 2, 8, 2], F32))
        bad = st.enter_context(nc.sbuf_tensor(U("sb_bad"), [128, 48], F32))
        gvec = st.enter_context(nc.sbuf_tensor(U("sb_gvec"), [128, 3, 8], F32))
        fctx = st.enter_context(nc.sbuf_tensor(U("sb_fctx"), [128, 2, 256], BF16))
        psf = Rot(nc, st, "psf", [128, 512], F32, 4, psum=True)
        psl = Rot(nc, st, "psl", [128, 512], F32, 2, psum=True)
        psb = Rot(nc, st, "psb", [128, 1024], BF16, 2, psum=True)
        bc = B["const"]
        k.dma("sp", identf[:], I["ident_f"][:, :], writes=[bc])
        k.dma("sp", identb[:], I["ident_b"][:, :], writes=[bc])
        k.dma("sp", onesb[:], I["ones_b"][:, :], writes=[bc])
        k.dma("sp", cvec[:], I["cvec"][:, :, :], writes=[bc])
        k.op("dve", MS(epsc[:], EPS), writes=[bc])
        k.op("act", ACT(scv[:], cvec[:], AF.Silu), reads=[bc], writes=[bc])
        k.op("act", ACT(scvb[:], cvec[:], AF.Silu), reads=[bc], writes=[bc])
        k.dma("sp", gvec[:, 2, :], I["g_final"][:, :], writes=[bc])

        def blk_of(c0):
            return min(c0 // 512, 4)

        def load_x(s2):
            xtok = Rot(nc, s2, "xtok", [128, D], F32, 2)
            for t in range(18):
                src = I["x_in"][t * 128:(t + 1) * 128, :] if t < 16 else I["ctx_in"][(t - 16) * 128:(t - 15) * 128, :]
                xt, bxt = xtok.next()
                k.dma("sp", xt[:], src, writes=[bxt])
                bi = blk_of(t * 128)
                for half in range(2):
                    ps, bps = psf.next()
                    for j in range(4):
                        kt = half * 4 + j
                        k.op("pe", TR(ps[:, j * 128:(j + 1) * 128], xt[:, kt * 128:(kt + 1) * 128], identf[:]),
                             reads=[bxt, bc], writes=[bps], inc=(j == 3), pe_accum=True)
                    if half == 0:
                        k.op("act", ACT(xT[:, half * 4:half * 4 + 4, t * 128:(t + 1) * 128], ps[:].rearrange("p (j c) -> p j c", j=4), AF.Copy),
                             reads=[bps], writes=[bx[bi]])
                    else:
                        k.op("dve", CP(xT[:, half * 4:half * 4 + 4, t * 128:(t + 1) * 128], ps[:].rearrange("p (j c) -> p j c", j=4)),
                             reads=[bps], writes=[bx[bi]])

        def ada_steps(l, wrot):
            psA, bpsA = psl.next()
            rotbox = [wrot]
            tiles = {}
            state = {"ld": 0, "mm": 0, "cap": 24}

            def load(cb):
                w, bw = rotbox[0].next()
                k.dma("pool", w[:], I["w_ada"][l][:, cb * 256:(cb + 1) * 256].rearrange("(kt p) c -> p kt c", p=128), writes=[bw])
                tiles[cb] = (w, bw)

            def mm(cb):
                w, bw = tiles.pop(cb)
                for ct in range(2):
                    col = cb * 2 + ct
                    for kt in range(8):
                        k.op("pe", MM(psA[:, col * 2:col * 2 + 2], w[:, kt, ct * 128:(ct + 1) * 128], scvb[:, kt, :], kt == 0, kt == 7),
                             reads=[bw, bc], writes=[bpsA], inc=(kt == 7 and ct == 1), pe_accum=True)

            def step():
                ahead = len(rotbox[0].items) - 1
                while state["ld"] < min(state["cap"], state["mm"] + 1 + ahead):
                    load(state["ld"]); state["ld"] += 1
                mm(state["mm"]); state["mm"] += 1

            steps = [step] * 24

            def fin(part=None):
                pv = psA[:, 0:96].rearrange("p (t j) -> p t j", j=2)
                if part in (None, "a"):
                    k.dma("sp", bad[:], I["b_ada"][l], writes=[B["mod"]])
                    k.dma("sp", gvec[:, 0, :], I["g_mix"][l], writes=[B["mod"]])
                    k.dma("sp", gvec[:, 1, :], I["g_ffn"][l], writes=[B["mod"]])
                lo, hi = {None: (0, 48), "a": (0, 16), "b": (16, 48)}[part]
                for j in range(2):
                    k.op("dve", TTo(mod[:, lo:hi, j], pv[:, lo:hi, j], bad[:, lo:hi], ALU.add), reads=[bpsA, B["mod"], B["gs"]], writes=[B["mod"]])
                for j in range(2):
                    if part in (None, "a"):
                        k.op("dve", STT(gs[:, 0, :, j], mod[:, 8:16, j], 1.0, gvec[:, 0, :], ALU.add, ALU.mult),
                             reads=[B["mod"]], writes=[B["gs"]])
                    if part in (None, "b"):
                        k.op("dve", STT(gs[:, 1, :, j], mod[:, 32:40, j], 1.0, gvec[:, 1, :], ALU.add, ALU.mult),
                             reads=[B["mod"]], writes=[B["gs"]])
            fin.rotbox = rotbox
            fin.state = state
            return steps, fin

        def ada(l):
            k.stage = "ada"
            with ExitStack() as s2:
                wrot = Rot(nc, s2, "wada", [128, 8, 256], BF16, 3)
                steps, fin = ada_steps(l, wrot)
                fin.state["cap"] = 8
                for st_ in steps[:4]:
                    st_()
                if l == 0:
                    load_x(s2)
                for st_ in steps[4:8]:
                    st_()
                fin("a")
                k.flush()
            return steps[8:], fin

        def norm(a, blocks, final=False, scope=None):
            if scope is None:
                k.stage = "norm"
            blocks = list(blocks)
            with ExitStack() as s2_own:
                s2 = s2_own if scope is None else scope
                sqra = Rot(nc, s2, "sqra", [128, 4, 512], BF16, 2)
                sqrb = Rot(nc, s2, "sqrb", [128, 4, 512], BF16, 2)
                rsr = Rot(nc, s2, "rsr", [128, 512], F32, 2)
                tmr = Rot(nc, s2, "tmr", [128, 512], F32, 3)

                def ph1(bi):
                    c0, wd = BLOCKS[bi]
                    sqa, bsqa = sqra.next()
                    sqb, bsqb = sqrb.next()
                    k.op("pool", TTo(sqa[:, :, :wd], xT[:, 0:4, c0:c0 + wd], xT[:, 0:4, c0:c0 + wd], ALU.mult), reads=[bx[bi]], writes=[bsqa])
                    k.op("act", ACT(sqb[:, :, :wd], xT[:, 4:8, c0:c0 + wd], AF.Square), reads=[bx[bi]], writes=[bsqb])
                    ps, bps = psf.next()
                    for kt in range(8):
                        sq_, bsq_ = (sqa, bsqa) if kt < 4 else (sqb, bsqb)
                        k.op("pe", MM(ps[:, :wd], onesb[:], sq_[:, kt % 4, :wd], kt == 0, kt == 7), reads=[bsq_, bc],
                             writes=[bps], inc=(kt == 7), pe_accum=True)
                    rs, brs = rsr.next()
                    k.op("act", ACT(rs[:, :wd], ps[:, :wd], AF.Ln, scale=1.0 / D, bias=epsc[:]), reads=[bps, bc], writes=[brs])
                    k.op("act", ACT(rs[:, :wd], rs[:, :wd], AF.Exp, scale=-0.5), reads=[brs], writes=[brs])
                    return rs, brs

                def ph2(bi, rs, brs):
                    c0, wd = BLOCKS[bi]
                    j = 0 if bi < 4 else 1
                    for kt in range(8):
                        tm, btm = tmr.next()
                        k.op("dve", TTo(tm[:, :wd], xT[:, kt, c0:c0 + wd], rs[:, :wd], ALU.mult), reads=[bx[bi], brs],
                             writes=[btm])
                        if final:
                            k.op("dve", TS(xT[:, kt, c0:c0 + wd], tm[:, :wd], gvec[:, 2, kt:kt + 1], ALU.mult),
                                 reads=[btm, bc], writes=[bx[bi]])
                        else:
                            k.op("act", ACT(hy[:, kt, c0:c0 + wd], tm[:, :wd], AF.Identity, scale=gs[:, a, kt, j:j + 1],
                                            bias=mod[:, 24 * a + kt, j:j + 1]),
                                 reads=[btm, B["gs"], B["mod"]], writes=[bhy[bi]])

                pend = ph1(blocks[0])
                for i_, bi in enumerate(blocks):
                    nxt = ph1(blocks[i_ + 1]) if i_ + 1 < len(blocks) else None
                    ph2(bi, *pend)
                    pend = nxt
                if scope is None:
                    k.flush()

        def AG(src, dst, bsrc, bdst):
            k.cc(lambda e: e.collective_compute("AllGather", ALU.bypass, replica_groups=GROUPS,
                                                ins=[src], outs=[dst]), reads=[bsrc], writes=[bdst])

        def na_halo_exchange():
            k.dma("sp", ke_c.rearrange("(h p) t -> h p t", p=128)[:, :, 0:192], s_nk[:, :, 0:192], reads=[B["s_nk"]], writes=[B["ke_c"]])
            k.dma("sp", ke_c.rearrange("(h p) t -> h p t", p=128)[:, :, 192:448], s_nk[:, :, 1792:2048], reads=[B["s_nk"]], writes=[B["ke_c"]])
            k.dma("sp", ve_c[0:192, :], s_nv[0:192, :], reads=[B["s_nv"]], writes=[B["ve_c"]])
            k.dma("sp", ve_c[192:448, :], s_nv[1792:2048, :], reads=[B["s_nv"]], writes=[B["ve_c"]])
            AG(ke_c[:, :], ke_g[:, :], B["ke_c"], B["ke_g"])
            AG(ve_c[:, :], ve_g[:, :], B["ve_c"], B["ve_g"])

        def load_w(rot, src2d, ncols, krows=8):
            w, bw = rot.next()
            k.dma("pool", w[:, :krows, :ncols], src2d.rearrange("(kt p) c -> p kt c", p=128), writes=[bw])
            return w, bw

        def project(l, last, deferred=None):
            k.stage = "project"
            wl = I["w_in"][l]
            with ExitStack() as s2:
                dq = []
                if deferred is not None:
                    dsteps, dfin = deferred
                    dfin.rotbox[0] = Rot(nc, s2, "wada2", [128, 8, 256], BF16, 3)
                    dfin.state["cap"] = 24
                    dq = list(dsteps)

                def ada_tick():
                    if dq:
                        dq.pop(0)()

                wbr = Rot(nc, s2, "wbr", [128, 8, 512], BF16, 2)
                wpr = Rot(nc, s2, "wpr", [128, 8, 384], BF16, 1)
                stg = Rot(nc, s2, "stg", [128, 512], BF16, 6)
                ropr = Rot(nc, s2, "ropr", [96, 4, 512], F32, 2)
                tA = Rot(nc, s2, "tA", [96, 512], F32, 2)
                tB = Rot(nc, s2, "tB", [96, 512], F32, 2)
                nvs = Rot(nc, s2, "nvs", [128, 6, 65], BF16, 2)
                for tle, bb in nvs.items:
                    k.op("pool", MS(tle[:], 1.0), writes=[bb])
                norm(0, range(5), scope=s2)

                def fm_group(w, bw, m0, M, evac):
                    for bi, (c0, wd) in enumerate(BLOCKS):
                        ps, bps = psf.next()
                        for kt in range(8):
                            k.op("pe", MM(ps[:M, :wd], w[:, kt, m0:m0 + M], hy[:, kt, c0:c0 + wd], kt == 0, kt == 7),
                                 reads=[bw, bhy[bi]], writes=[bps], inc=(kt == 7), pe_accum=True)
                        evac(ps, bps, bi, c0, wd)

                def tm_group(w, bw, ncols, evac, tiles=range(18)):
                    for t in tiles:
                        bi = blk_of(t * 128)
                        ps, bps = psf.next()
                        for kt in range(8):
                            k.op("pe", MM(ps[:, :ncols], hy[:, kt, t * 128:(t + 1) * 128], w[:, kt, :ncols], kt == 0, kt == 7),
                                 reads=[bw, bhy[bi]], writes=[bps], inc=(kt == 7), pe_accum=True)
                        evac(ps, bps, t)

                w, bw = load_w(wbr, wl[:, C_F:C_F + 256], 256)

                def ev_f(ps, bps, t):
                    if t < 16:
                        s, bs = stg.next()
                        k.op("act", ACT(s[:, :256], ps[:, :256], AF.Copy), reads=[bps], writes=[bs])
                        k.dma("sp", f_c[t * 128:(t + 1) * 128, :], s[:, :256], reads=[bs], writes=[B["f_c"]])
                    else:
                        k.op("act", ACT(fctx[:, t - 16, :], ps[:, :256], AF.Copy), reads=[bps], writes=[B["fctx"]])
                tm_group(w, bw, 256, ev_f)
                ada_tick()

                for which, c_off, dst, bdst in ((0, C_RQ, s_q, B["s_q"]), (1, C_RK, s_k, B["s_k"])):
                    w, bw = load_w(wbr, wl[:, c_off:c_off + 384], 384)
                    wp, bwp = wpr.next()
                    w5 = w[:, :, 0:384].rearrange("p k (a b c) -> p k a b c", a=8, b=2, c=24)
                    p5 = wp[:, :, 0:384].rearrange("p k (a b c) -> p k a b c", a=8, b=2, c=24)
                    for b_ in range(2):
                        k.op("pool", CP(p5[:, :, :, b_, :], w5[:, :, :, 1 - b_, :]), reads=[bw], writes=[bwp])
                    for bi, (c0, wd) in enumerate(BLOCKS):
                        ada_tick()
                        if bi < 4:
                            rp, brp = ropr.next()
                            k.dma("sp", rp[:], I["rope"][:, :, c0:c0 + wd], writes=[brp])
                        for h in range(4):
                            ps, bps = psf.next()
                            for kt in range(8):
                                k.op("pe", MM(ps[:96, :wd], w[:, kt, h * 96:(h + 1) * 96], hy[:, kt, c0:c0 + wd], kt == 0, kt == 7),
                                     reads=[bw, bhy[bi]], writes=[bps], inc=(kt == 7), pe_accum=True)
                            s, bs = stg.next()
                            if bi < 4:
                                ps2, bps2 = psf.next()
                                for kt in range(8):
                                    k.op("pe", MM(ps2[:96, :wd], wp[:, kt, h * 96:(h + 1) * 96], hy[:, kt, c0:c0 + wd], kt == 0, kt == 7),
                                         reads=[bwp, bhy[bi]], writes=[bps2], inc=(kt == 7), pe_accum=True)
                                a_, ba_ = tA.next()
                                b2, bb2 = tB.next()
                                k.op("dve", TTo(a_[:, :wd], ps[:96, :wd], rp[:, 2 * which, :wd], ALU.mult), reads=[bps, brp], writes=[ba_])
                                k.op("dve", TTo(b2[:, :wd], ps2[:96, :wd], rp[:, 2 * which + 1, :wd], ALU.mult), reads=[bps2, brp], writes=[bb2])
                                k.op("pool", TTo(s[:96, :wd], a_[:, :wd], b2[:, :wd], ALU.add), reads=[ba_, bb2], writes=[bs])
                            else:
                                k.op("act", ACT(s[:96, :wd], ps[:96, :wd], AF.Copy, scale=(1.0 if which == 0 else 96 ** -0.5)),
                                     reads=[bps], writes=[bs])
                            k.dma("sp", dst[h][:, c0:c0 + wd], s[:96, :wd], reads=[bs], writes=[bdst])

                w, bw = load_w(wbr, wl[:, C_RV:C_RV + 384], 384)

                def ev_v(ps, bps, t):
                    s, bs = stg.next()
                    k.op("act", ACT(s[:, :384], ps[:, :384], AF.Copy), reads=[bps], writes=[bs])
                    k.dma("sp", s_v[t * 128:(t + 1) * 128, :], s[:, :384], reads=[bs], writes=[B["s_v"]])
                tm_group(w, bw, 384, ev_v)
                ada_tick()

                w, bw = load_w(wbr, wl[:, C_RG:C_RG + 384], 384)
                for h in range(4):
                    def ev_g(ps, bps, bi, c0, wd, h=h):
                        s, bs = stg.next()
                        k.op("act", ACT(s[:96, :wd], ps[:96, :wd], AF.Silu), reads=[bps], writes=[bs])
                        k.dma("sp", s_g[h][:, c0:c0 + wd], s[:96, :wd], reads=[bs], writes=[B["s_g"]])
                    fm_group(w, bw, h * 96, 96, ev_g)
                    ada_tick()

                for c_off, dst, bdst, scl in ((C_NQ, s_nq, B["s_nq"], 0.125), (C_NK, s_nk, B["s_nk"], 1.0)):
                    w, bw = load_w(wbr, wl[:, c_off:c_off + 384], 384)
                    for hp in range(3):
                        def ev_n(ps, bps, bi, c0, wd, hp=hp, dst=dst, bdst=bdst, scl=scl):
                            s, bs = stg.next()
                            k.op("act", ACT(s[:, :wd], ps[:, :wd], AF.Copy, scale=scl), reads=[bps], writes=[bs])
                            k.dma("sp", dst[hp][:, c0:c0 + wd], s[:, :wd], reads=[bs], writes=[bdst])
                        fm_group(w, bw, hp * 128, 128, ev_n)

                w, bw = load_w(wbr, wl[:, C_NV:C_NV + 384], 384)

                def ev_nv(ps, bps, t):
                    s, bs = nvs.next()
                    k.op("act", ACT(s[:, :, 0:64], ps[:, :384].rearrange("p (h c) -> p h c", h=6), AF.Copy), reads=[bps], writes=[bs])
                    k.dma("sp", s_nv[t * 128:(t + 1) * 128, :], s[:].rearrange("p h c -> p (h c)"), reads=[bs], writes=[B["s_nv"]])
                tm_group(w, bw, 384, ev_nv)
                while dq:
                    ada_tick()
                if deferred is not None:
                    dfin("b")
                AG(f_c[:, :], f_g[:, :], B["f_c"], B["f_g"])
                na_halo_exchange()
                k.flush()

        def fourier(l, last):
            k.stage = "fourier"
            with ExitStack() as s2:
                sb = lambda n, shp, dt: s2.enter_context(nc.sbuf_tensor(U("sb_" + n), shp, dt))
                d128 = sb("d128", [128, 2, 128], BF16); c64 = sb("c64", [128, 128], BF16); s64n = sb("s64n", [128, 128], BF16)
                dc256 = sb("dc256", [128, 2, 512], BF16)
                wfb = sb("wfb", [128, 2, 256], BF16)
                wcs = sb("wcs", [128, 2, 2, 256], BF16)
                AB = sb("AB", [128, 2, 2, TT], BF16)
                bAB = Buf("AB"); bfc = Buf("fconst"); bwcs = Buf("wcs")
                Xr = Rot(nc, s2, "Xr", [128, 64, 64], BF16, 2)
                Yr = Rot(nc, s2, "Yr", [128, 64, 64], BF16, 2)
                k2r = Rot(nc, s2, "k2r", [128, 64, 32], BF16, 2)
                k.dma("sp", d128[:], I["d128"].rearrange("p (a c) -> p a c", a=2), writes=[bfc])
                k.dma("sp", c64[:], I["c64"][:, :], writes=[bfc])
                k.dma("sp", s64n[:], I["s64n"][:, :], writes=[bfc])
                k.dma("sp", dc256[:], I["dc256"][:, :, :], writes=[bfc])
                wff = sb("wff", [128, 2, 256], F32)
                bwff = Buf("wff")
                k.dma("sp", wff[:], I["w_four"][l].rearrange("(g p) c -> p g c", p=128), writes=[bwff])
                k.op("act", ACT(wfb[:], wff[:], AF.Copy), reads=[bwff], writes=[bfc])
                for gp in range(2):
                    for ab, cm in ((0, c64), (1, s64n)):
                        ps, bps = psf.next()
                        k.op("pe", MM(ps[:, :256], cm[:], wfb[:, gp, :], True, True), reads=[bfc], writes=[bps])
                        k.op("act", ACT(wcs[:, gp, ab, :], ps[:, :256], AF.Copy), reads=[bps], writes=[bwcs])
                if not last:
                    for g in range(4):
                        gp, r0 = g // 2, (g % 2) * 64
                        ps, bps = psf.next()
                        for pt in range(2):
                            k.op("pe", MM(ps[r0:r0 + 64, :], fctx[:, pt, g * 64:(g + 1) * 64], dc256[:, pt, :], pt == 0, pt == 1),
                                 reads=[B["fctx"], bfc], writes=[bps], inc=(pt == 1), pe_accum=True)
                        k.op("act", ACT(AB[r0:r0 + 64, gp, :, TOK:TT], ps[r0:r0 + 64, :].rearrange("p (a t) -> p a t", a=2), AF.Copy),
                             reads=[bps], writes=[bAB])
                ei = 0
                for g in range(4):
                    gp, r0 = g // 2, (g % 2) * 64
                    X, bX = Xr.next()
                    k.dma("sp", X[:], f_g.rearrange("(r w) c -> r w c", w=64)[:, :, g * 64:(g + 1) * 64], reads=[B["f_g"]], writes=[bX])
                    for lq in range(2):
                        Y, bY = Yr.next()
                        for c8 in range(8):
                            ps, bps = psf.next()
                            for ci in range(8):
                                c = c8 * 8 + ci
                                k.op("pe", MM(ps[0:64, ci * 64:(ci + 1) * 64], X[:, :, c], d128[:, 0, lq * 64:(lq + 1) * 64], True, True),
                                     reads=[bX, bfc], writes=[bps], inc=False, pe_accum=True)
                                k.op("pe", MM(ps[64:128, ci * 64:(ci + 1) * 64], X[:, :, c], d128[:, 1, lq * 64:(lq + 1) * 64], True, True),
                                     reads=[bX, bfc], writes=[bps], inc=(ci == 7), pe_accum=True)
                            src = ps[:, :].rearrange("p (c l) -> p c l", c=8)
                            dstY = Y[:, c8 * 8:(c8 + 1) * 8, :]
                            ei += 1
                            if ei % 2:
                                k.op("act", ACT(dstY, src, AF.Copy), reads=[bps], writes=[bY])
                            else:
                                k.op("dve", CP(dstY, src), reads=[bps], writes=[bY])
                        k2, bk2 = k2r.next()
                        k.dma("sp", k2[:], I["k2s"][:, lq * 64:(lq + 1) * 64, :], writes=[bk2])
                        for hb in range(4):
                            ps, bps = psf.next()
                            for li in range(16):
                                l1i = hb * 16 + li
                                k.op("pe", MM(ps[r0:r0 + 64, li * 32:(li + 1) * 32], Y[:, :, l1i], k2[:, l1i, :], True, True),
                                     reads=[bY, bk2], writes=[bps], inc=(li == 15), pe_accum=True)
                            l1s = lq * 64 + hb * 16
                            src = ps[r0:r0 + 64, :].rearrange("p (l a m) -> p a m l", l=16, a=2)
                            dstA = AB[r0:r0 + 64, gp, :, 0:TOK].rearrange("p a (m l) -> p a m l", l=128)[:, :, :, l1s:l1s + 16]
                            k.op("dve", CP(dstA, src), reads=[bps], writes=[bAB])
                for bi, (c0, wd) in enumerate(BLOCKS):
                    if last and bi == 4:
                        continue
                    for mt_ in range(2):
                        ps, bps = psf.next()
                        n_ = 0
                        for gp in range(2):
                            for ab in range(2):
                                k.op("pe", MM(ps[:, :wd], wcs[:, gp, ab, mt_ * 128:(mt_ + 1) * 128], AB[:, gp, ab, c0:c0 + wd], n_ == 0, n_ == 3),
                                     reads=[bwcs, bAB], writes=[bps], inc=(n_ == 3), pe_accum=True)
                                n_ += 1
                        k.op("act", ACT(hy[:, mt_, c0:c0 + wd], ps[:, :wd], AF.Copy), reads=[bps], writes=[bhy[bi]])
                k.flush()

        def retention(l, last):
            k.stage = "retention"
            with ExitStack() as s2:
                sb = lambda n, shp, dt: s2.enter_context(nc.sbuf_tensor(U("sb_" + n), shp, dt))
                lg = sb("lg", [128, 8], F32); kd = sb("kd", [128, 8], F32); g128 = sb("g128", [96, 8], F32)
                cn = sb("cn", [96, 16, 8], F32); cx = sb("cx", [96, 5, 8], F32); qd = sb("qd", [96, 8, 128], F32)
                mt = sb("mt", [128, 4, 128], BF16); gain = sb("gain", [96, 4, 128], F32)
                dmask = sb("dmask", [128, 4, 128], F32); eq = sb("eq", [128, 2, 128], F32); ek = sb("ek", [128, 2], F32)
                ec = sb("ec", [128, 2, 16], F32); exmx = sb("exmx", [128, 2, 5, 2], F32)
                t1 = sb("t1", [128, 128], F32); t2 = sb("t2", [128, 128], F32)
                kdt = sb("kdt", [128, 8, 96], F32)
                tent = sb("tent", [96, 18, 8, 96], BF16)
                s3 = ExitStack()
                Tst = sb("Tst", [96, 8, 96], F32); Sin = sb("Sin", [96, 8, 96], F32); Sctx = sb("Sctx", [96, 8, 96], F32)
                Tst2 = sb("Tst2", [96, 8, 96], F32)
                TT2 = [Tst, Tst2]
                bTT = [[Buf(f"T{i_}_{d_}") for d_ in range(8)] for i_ in range(2)]
                par = [0, 0]
                kchr = Rot(nc, s2, "kch", [96, 4, 128], BF16, 2); qchr = Rot(nc, s2, "qch", [96, 4, 128], BF16, 2)
                gchr = Rot(nc, s2, "gch", [96, 4, 128], BF16, 3); vchr = Rot(nc, s2, "vch", [128, 384], BF16, 3)
                kvb = s3.enter_context(nc.sbuf_tensor(U("sb_kvb"), [96, 18, 4, 96], BF16))
                Sg = s3.enter_context(nc.sbuf_tensor(U("sb_Sg"), [96, 4, 768], F32))
                kfr = Rot(nc, s3, "kf", [128, 4, 96], BF16, 2); kbr = Rot(nc, s3, "kb", [128, 4, 96], BF16, 2)
                bT = Buf("Tst"); btent = Buf("tent"); bkvb = Buf("kvb"); bSin = Buf("Sin"); bSctx = Buf("Sctx"); bSg = Buf("Sg")
                b_cst = Buf("cst"); b_lg = Buf("lg"); b_kd = Buf("kd"); b_g = Buf("g128"); b_cn = Buf("cn"); b_cx = Buf("cx")
                b_qd = Buf("qd"); b_mt = Buf("mt"); b_t1 = Buf("t1"); b_t2 = Buf("t2"); b_kdt = Buf("kdt"); b_gt = Buf("gt"); b_gain = Buf("gain")
                RT = [b_lg, b_kd, b_g, b_cn, b_cx, b_qd, b_mt, b_kdt, b_gain]
                for dst_, nme in ((dmask, "dmask"), (eq, "eq"), (ek, "ek"), (ec, "ec"), (exmx, "exmx")):
                    k.dma("sp", dst_[:], I[nme], writes=[b_cst])
                k.dma("sp", lg[:], I["logit"][l], writes=[b_lg])
                k.dma("sp", gain[:], I["ret_gain"][l], writes=[b_gain])
                k.op("act", ACT(lg[:], lg[:], AF.Exp, scale=-1.0), reads=[b_lg], writes=[b_lg])
                k.op("dve", TS(lg[:], lg[:], 1.0, ALU.add), reads=[b_lg], writes=[b_lg])
                k.op("act", ACT(lg[:], lg[:], AF.Ln), reads=[b_lg], writes=[b_lg])
                k.op("dve", TS(lg[:], lg[:], -1.0, ALU.mult), reads=[b_lg], writes=[b_lg])
                k.op("act", ACT(g128[:], lg[:96, :], AF.Exp, scale=128.0), reads=[b_lg], writes=[b_g])
                k.op("dve", MS(kdt[:], 1.0), writes=[b_kdt])
                for dh in range(8):
                    d_ = dh // 4
                    sc_ = lg[:, dh:dh + 1]
                    k.op("act", ACT(kd[:, dh:dh + 1], ek[:, d_:d_ + 1], AF.Exp, scale=sc_), reads=[b_lg, b_cst], writes=[b_kd])
                    k.op("act", ACT(cn[:, :, dh], ec[:96, d_, :], AF.Exp, scale=lg[:96, dh:dh + 1]), reads=[b_lg, b_cst], writes=[b_cn])
                    k.op("act", ACT(cx[:, :, dh], exmx[:96, 0, :, d_], AF.Exp, scale=lg[:96, dh:dh + 1]), reads=[b_lg, b_cst], writes=[b_cx])
                    k.op("act", ACT(qd[:, dh, :], eq[:96, d_, :], AF.Exp, scale=lg[:96, dh:dh + 1]), reads=[b_lg, b_cst], writes=[b_qd])
                for dh in range(8):
                    d_ = dh // 4
                    k.op("dve", TTo(cx[:, :, dh], cx[:, :, dh], exmx[:96, 1, :, d_], ALU.mult), reads=[b_cx, b_cst], writes=[b_cx])
                    k.op("dve", TS(kdt[:, dh, :], kdt[:, dh, :], kd[:, dh:dh + 1], ALU.mult), reads=[b_kd, b_kdt], writes=[b_kdt])
                for h in range(4):
                    k.op("act", ACT(t1[:], dmask[:, 0, :], AF.Exp, scale=lg[:, h:h + 1]), reads=[b_lg, b_cst], writes=[b_t1])
                    k.op("act", ACT(t2[:], dmask[:, 2, :], AF.Exp, scale=lg[:, 4 + h:5 + h]), reads=[b_lg, b_cst], writes=[b_t2])
                    k.op("dve", TTo(t1[:], t1[:], dmask[:, 1, :], ALU.mult), reads=[b_t1, b_cst], writes=[b_t1])
                    k.op("dve", TTo(t2[:], t2[:], dmask[:, 3, :], ALU.mult), reads=[b_t2, b_cst], writes=[b_t2])
                    k.op("dve", TTo(mt[:, h, :], t1[:], t2[:], ALU.add), reads=[b_t1, b_t2], writes=[b_mt])
                k.op("dve", MS(Tst[:], 0.0), writes=[bT] + bTT[0])
                k.op("dve", MS(Tst2[:], 0.0), writes=bTT[1])
                if stop == "ret_tab":
                    s3.close()
                    k.flush()
                    return

                def tok0(n):
                    return n * 128

                def load_k(n):
                    kc, bkc = kchr.next()
                    k.dma("sp", kc[:], s_k[:, :, tok0(n):tok0(n) + 128].rearrange("h d t -> d h t"), reads=[B["s_k"]], writes=[bkc])
                    return kc, bkc

                def load_v(n):
                    vc, bvc = vchr.next()
                    k.dma("sp", vc[:], s_v[tok0(n):tok0(n) + 128, :], reads=[B["s_v"]], writes=[bvc])
                    return vc, bvc

                def preA(n):
                    kc, bkc = load_k(n)
                    vc, bvc = load_v(n)
                    pb, bpb = psb.next()
                    for h in range(4):
                        k.op("pe", TR(pb[:, h * 96:(h + 1) * 96], kc[:, h, :], identb[:96, :96]), reads=[bkc, bc], writes=[bpb],
                             inc=(h == 3), pe_accum=True)
                    kf, bkf = kfr.next(); kb, bkb = kbr.next()
                    pv_ = pb[:, 0:384].rearrange("p (h d) -> p h d", h=4)
                    k.op("dve", TTo(kf[:], pv_, kdt[:, 0:4, :], ALU.mult), reads=[bpb, *RT], writes=[bkf])
                    k.op("dve", TTo(kb[:], pv_, kdt[:, 4:8, :], ALU.mult), reads=[bpb, *RT], writes=[bkb])
                    return (vc, bvc, kf, bkf, kb, bkb)

                def preB(n, c_):
                    vc, bvc, kf, bkf, kb, bkb = c_
                    if n == 0:
                        cur_ = TT2[par[0]]
                        k.op("act", ACT(Sctx[:, 0:4, :], cur_[:, 0:4, :], AF.Copy), reads=bTT[par[0]][0:4], writes=[bSctx])
                        k.op("dve", MS(cur_[:, 0:4, :], 0.0), reads=[bSctx], writes=bTT[par[0]][0:4])
                    ps1, bps1 = psf.next(); ps2, bps2 = psf.next()
                    for h in range(4):
                        k.op("pe", MM(ps1[:96, h * 96:(h + 1) * 96], kf[:, h, :], vc[:, h * 96:(h + 1) * 96], True, True),
                             reads=[bkf, bvc], writes=[bps1], inc=(h == 3), pe_accum=True)
                    for h in range(4):
                        k.op("pe", MM(ps2[:96, h * 96:(h + 1) * 96], kb[:, h, :], vc[:, h * 96:(h + 1) * 96], True, True),
                             reads=[bkb, bvc], writes=[bps2], inc=(h == 3), pe_accum=True)
                    k.op("act", ACT(kvb[:, n, :, :], ps2[:96, 0:384].rearrange("p (h e) -> p h e", h=4), AF.Copy), reads=[bps2], writes=[bkvb])
                    p_ = par[0]
                    cur_, nxt_ = TT2[p_], TT2[1 - p_]
                    k.op("act", ACT(tent[:, n, 0:4, :], cur_[:, 0:4, :], AF.Copy), reads=bTT[p_][0:4], writes=[btent])
                    for h in range(4):
                        k.op("dve", STT(nxt_[:, h, :], cur_[:, h, :], g128[:, h:h + 1], ps1[:96, h * 96:(h + 1) * 96], ALU.mult, ALU.add),
                             reads=[bTT[p_][h], b_g, bps1], writes=[bTT[1 - p_][h]])
                    par[0] = 1 - p_

                order = [16, 17] + list(range(16))
                pend = preA(order[0])
                for oi, n in enumerate(order):
                    nxt = preA(order[oi + 1]) if oi + 1 < len(order) else None
                    preB(n, pend)
                    pend = nxt
                for n in [17, 16] + list(range(15, -1, -1)):
                    p_ = par[1]
                    cur_, nxt_ = TT2[p_], TT2[1 - p_]
                    if n == 15:
                        k.op("act", ACT(Sctx[:, 4:8, :], cur_[:, 4:8, :], AF.Copy), reads=bTT[p_][4:8], writes=[bSctx])
                        k.op("dve", MS(cur_[:, 4:8, :], 0.0), reads=[bSctx], writes=bTT[p_][4:8])
                    k.op("act", ACT(tent[:, n, 4:8, :], cur_[:, 4:8, :], AF.Copy), reads=bTT[p_][4:8], writes=[btent])
                    for h in range(4):
                        k.op("dve", STT(nxt_[:, 4 + h, :], cur_[:, 4 + h, :], g128[:, 4 + h:5 + h], kvb[:, n, h, :], ALU.mult, ALU.add),
                             reads=[bTT[p_][4 + h], b_g, bkvb], writes=[bTT[1 - p_][4 + h]])
                    par[1] = 1 - p_
                if stop == "ret_pre":
                    k.flush()
                    s3.close()
                    return
                k.dma("sp", st_c[:, 0:384], TT2[par[0]][:, 0:4, :].rearrange("p a e -> p (a e)"), reads=bTT[par[0]][0:4], writes=[B["st_c"]])
                k.dma("sp", st_c[:, 384:768], TT2[par[1]][:, 4:8, :].rearrange("p a e -> p (a e)"), reads=bTT[par[1]][4:8], writes=[B["st_c"]])
                AG(st_c[:, :], st_g[:, :], B["st_c"], B["st_g"])
                k.dma("sp", Sg[:], st_g.rearrange("(j p) c -> p j c", p=96), reads=[B["st_g"]], writes=[bSg])
                bSinL = [Buf(f"Sin{dh}") for dh in range(8)]
                for dh in range(8):
                    k.op("dve", TS(Sin[:, dh, :], Sctx[:, dh, :], cx[:, 4, dh:dh + 1], ALU.mult), reads=[bSctx, *RT], writes=[bSinL[dh]])
                for j in range(4):
                    for dh in range(8):
                        k.op("dve", STT(Sin[:, dh, :], Sg[:, j, dh * 96:(dh + 1) * 96], cx[:, j, dh:dh + 1], Sin[:, dh, :], ALU.mult, ALU.add),
                             reads=[bSg, *RT, bSinL[dh]], writes=[bSinL[dh]])
                k.op("dve", MS(epsc[:], EPS), reads=bSinL, writes=[bSin, bc])
                k.flush()
                s3.close()
                if stop == "ret_ag":
                    return
                Sfr = Rot(nc, s2, "Sf", [96, 8, 96], BF16, 3)
                qsr = Rot(nc, s2, "qs", [96, 8, 128], BF16, 3)
                Pr = Rot(nc, s2, "Pr", [128, 4, 128], BF16, 3)
                sqr2 = Rot(nc, s2, "sq2", [96, 512], BF16, 2)
                rs2 = Rot(nc, s2, "rs2", [96, 512], F32, 2)
                y1r = Rot(nc, s2, "y1r", [96, 512], F32, 2)
                y2r = Rot(nc, s2, "y2r", [96, 512], F32, 2)
                chunks = list(range(16)) + ([] if last else [16, 17])

                def phaseA(n):
                    kc, bkc = load_k(n)
                    vc, bvc = load_v(n)
                    qc, bqc = qchr.next()
                    k.dma("sp", qc[:], s_q[:, :, tok0(n):tok0(n) + 128].rearrange("h d t -> d h t"), reads=[B["s_q"]], writes=[bqc])
                    gc, bgc = gchr.next()
                    k.dma("sp", gc[:], s_g[:, :, tok0(n):tok0(n) + 128].rearrange("h d t -> d h t"), reads=[B["s_g"]], writes=[bgc])
                    S, bS = Sfr.next()
                    if n < 16:
                        for dh in range(8):
                            k.op("dve", STT(S[:, dh, :], Sin[:, dh, :], cn[:, n, dh:dh + 1], tent[:, n, dh, :], ALU.mult, ALU.add),
                                 reads=[bSin, *RT, btent], writes=[bS])
                    else:
                        k.op("act", ACT(S[:], tent[:, n, :, :], AF.Copy), reads=[btent], writes=[bS])
                    qs, bqs = qsr.next()
                    k.op("dve", TTo(qs[:, 0:4, :], qc[:], qd[:, 0:4, :], ALU.mult), reads=[bqc, *RT], writes=[bqs])
                    k.op("pool", TTo(qs[:, 4:8, :], qc[:], qd[:, 4:8, :], ALU.mult), reads=[bqc, *RT], writes=[bqs])
                    ps, bps = psf.next()
                    for h in range(4):
                        k.op("pe", MM(ps[:, h * 128:(h + 1) * 128], kc[:, h, :], qc[:, h, :], True, True), reads=[bkc, bqc], writes=[bps],
                             inc=(h == 3), pe_accum=True)
                    P, bP = Pr.next()
                    k.op("dve", TTo(P[:], ps[:].rearrange("p (h i) -> p h i", h=4), mt[:], ALU.mult), reads=[bps, *RT], writes=[bP])
                    return (vc, bvc, gc, bgc, S, bS, qs, bqs, P, bP)

                def phaseB(n, ctx_):
                    vc, bvc, gc, bgc, S, bS, qs, bqs, P, bP = ctx_
                    bi = blk_of(tok0(n))
                    po, bpo = psl.next()
                    for h in range(4):
                        o_ = po[:96, h * 128:(h + 1) * 128]
                        k.op("pe", MM(o_, vc[:, h * 96:(h + 1) * 96], P[:, h, :], True, False), reads=[bvc, bP], writes=[bpo], inc=False, pe_accum=True)
                        k.op("pe", MM(o_, S[:, h, :], qs[:, h, :], False, False), reads=[bS, bqs], writes=[bpo], inc=False, pe_accum=True)
                        k.op("pe", MM(o_, S[:, 4 + h, :], qs[:, 4 + h, :], False, True), reads=[bS, bqs], writes=[bpo], inc=(h == 3), pe_accum=True)
                    sq, bsq = sqr2.next()
                    k.op("act", ACT(sq[:], po[:96, :], AF.Square), reads=[bpo], writes=[bsq])
                    pss, bpss = psf.next()
                    k.op("pe", MM(pss[:96, :], onesb[:96, :96], sq[:], True, True), reads=[bsq, bc], writes=[bpss])
                    rs, brs = rs2.next()
                    k.op("act", ACT(rs[:], pss[:96, :], AF.Ln, scale=1.0 / 96, bias=epsc[:96, :]), reads=[bpss, bc], writes=[brs])
                    k.op("act", ACT(rs[:], rs[:], AF.Exp, scale=-0.5), reads=[brs], writes=[brs])
                    y1, by1 = y1r.next()
                    k.op("dve", TTo(y1[:], po[:96, :], rs[:], ALU.mult), reads=[bpo, brs], writes=[by1])
                    y2, by2 = y2r.next()
                    k.op("pool", TTo(y2[:], y1[:], gain[:].rearrange("p h i -> p (h i)"), ALU.mult), reads=[by1, *RT], writes=[by2])
                    k.op("pool", TTo(hy[:96, 2:6, tok0(n):tok0(n) + 128], y2[:].rearrange("p (h i) -> p h i", h=4), gc[:], ALU.mult),
                         reads=[by2, bgc], writes=[bhy[bi]])

                pq = [phaseA(chunks[0]), phaseA(chunks[1])]
                for ci, n in enumerate(chunks):
                    if ci + 2 < len(chunks):
                        pq.append(phaseA(chunks[ci + 2]))
                    phaseB(n, pq.pop(0))
                k.flush()

        def natten(l, last):
            k.stage = "natten"
            with ExitStack() as s2:
                sb = lambda n, shp, dt: s2.enter_context(nc.sbuf_tensor(U("sb_" + n), shp, dt))
                nkf = sb("nkf", [128, 3, 2560], BF16); nvf = sb("nvf", [128, 20, 390], BF16)
                nkc = sb("nkc", [128, 3, 256], BF16); nvc = sb("nvc", [128, 2, 390], BF16)
                bint = sb("bint", [128, 6, 5, 128], BF16); mh = sb("mh", [128, 8], F32)
                bnk = Buf("nkf"); bnv = Buf("nvf"); bctx = Buf("nctx"); bbi = Buf("bint"); bmh = Buf("mh")
                k.dma("sp", mh[:], I["mh"][:, :], writes=[bmh])
                k.dma("pool", bint[:], I["bias_int"][l], writes=[bbi], max_dma_last_dim=2048)
                k.dma("sp", nkf[:, :, 256:2304], s_nk[:, :, 0:TOK].rearrange("h p t -> p h t"), reads=[B["s_nk"]], writes=[bnk])
                k.dma("sp", nvf[:, 2:18, :], s_nv[0:TOK, :].rearrange("(t p) c -> p t c", p=128), reads=[B["s_nv"]], writes=[bnv])
                k.dma("sp", nkc[:], s_nk[:, :, TOK:TT].rearrange("h p t -> p h t"), reads=[B["s_nk"]], writes=[bctx])
                k.dma("sp", nvc[:], s_nv[TOK:TT, :].rearrange("(t p) c -> p t c", p=128), reads=[B["s_nv"]], writes=[bctx])
                with ExitStack() as s3:
                    keg = s3.enter_context(nc.sbuf_tensor(U("sb_keg"), [128, 4, 3, 448], BF16))
                    veg = s3.enter_context(nc.sbuf_tensor(U("sb_veg"), [128, 4, 4, 390], BF16))
                    bkeg = Buf("keg"); bveg = Buf("veg")
                    k.op("pool", MS(veg[:], 0.0), writes=[bveg])
                    k.op("pool", MS(nkf[:, :, 2496:2560], 0.0), writes=[bnk])
                    k.dma("sp", keg[:], ke_g.rearrange("(j h p) t -> p j h t", j=4, h=3), reads=[B["ke_g"]], writes=[bkeg])
                    vg4 = ve_g.rearrange("(j r) c -> j r c", j=4)
                    for j in range(4):
                        k.dma("sp", veg[:, j, 0, :], vg4[j, 0:128, :], reads=[B["ve_g"]], writes=[bveg])
                        k.dma("sp", veg[0:64, j, 1, :], vg4[j, 128:192, :], reads=[B["ve_g"]], writes=[bveg])
                        k.dma("sp", veg[:, j, 2:4, :], vg4[j, 192:448, :].rearrange("(t p) c -> p t c", p=128), reads=[B["ve_g"]], writes=[bveg])
                    for (dstk, srck, dstv, srcv, m0) in (
                            (nkf[:, :, 0:256], lambda j: keg[:, j, :, 192:448], nvf[:, 0:2, :], lambda j: veg[:, j, 2:4, :], 0),
                            (nkf[:, :, 2304:2496], lambda j: keg[:, j, :, 0:192], nvf[:, 18:20, :], lambda j: veg[:, j, 0:2, :], 4)):
                        k.op("dve", TS(dstk, srck(0), mh[:, m0:m0 + 1], ALU.mult), reads=[bkeg, bmh], writes=[bnk])
                        k.op("dve", TS(dstv, srcv(0), mh[:, m0:m0 + 1], ALU.mult), reads=[bveg, bmh], writes=[bnv])
                        for j in range(1, 4):
                            k.op("dve", STT(dstk, srck(j), mh[:, m0 + j:m0 + j + 1], dstk, ALU.mult, ALU.add), reads=[bkeg, bmh, bnk], writes=[bnk])
                            k.op("dve", STT(dstv, srcv(j), mh[:, m0 + j:m0 + j + 1], dstv, ALU.mult, ALU.add), reads=[bveg, bmh, bnv], writes=[bnv])
                    k.flush()
                wo = s2.enter_context(nc.sbuf_tensor(U("sb_wo"), [128, 9, D], BF16))
                bwo = Buf("wo")
                wsrc = I["w_out"][l]
                k.dma("pool", wo[:, 0:2, :], wsrc[0:256, :].rearrange("(s p) c -> p s c", p=128), writes=[bwo])
                k.dma("pool", wo[:96, 2:6, :], wsrc[256:640, :].rearrange("(s p) c -> p s c", p=96), writes=[bwo])
                k.dma("pool", wo[:, 6:9, :], wsrc[640:1024, :].rearrange("(s p) c -> p s c", p=128), writes=[bwo])
                nqr = Rot(nc, s2, "nqt", [128, 3, 128], BF16, 4)
                ber = Rot(nc, s2, "bedge", [128, 2, 6, 128], BF16, 4)
                ssr = Rot(nc, s2, "ssb", [128, 6, 128], F32, 3)
                Pr = Rot(nc, s2, "Pn", [128, 8, 128], BF16, 3)
                otr = Rot(nc, s2, "otok", [128, 384], BF16, 2)
                rdr = Rot(nc, s2, "rden", [128, 6], F32, 2)
                tiles = list(range(16)) + ([] if last else [16, 17])
                units = [(t, hp, hh) for t in tiles for hp in range(3) for hh in range(2)]
                tstate = {}
                sbanks = list(psf.items) + list(psl.items)
                sb_i = [0]
                pso_fix = (psb.items[0][0][:, :].bitcast(F32), psb.items[0][1])
                ptr_fix = psb.items[1]

                def next_bank():
                    it = sbanks[sb_i[0] % 6]
                    sb_i[0] += 1
                    return it

                def tile_cfg(t):
                    if t >= 16:
                        return 0, 0, None
                    if t == 0:
                        return 6, 0, 0
                    if t == 15:
                        return 6, 1792, 3
                    return 5, 128 * t, {1: 1, 14: 2}.get(t)

                def scoresU(u):
                    t, hp, hh = u
                    nkt, kb0, edge = tile_cfg(t)
                    t0_ = t * 128
                    if hp == 0 and hh == 0:
                        nq, bnq = nqr.next()
                        k.dma("sp", nq[:], s_nq[:, :, t0_:t0_ + 128].rearrange("h p t -> p h t"), reads=[B["s_nq"]], writes=[bnq])
                        tstate[t] = {"nq": (nq, bnq)}
                    ts_ = tstate[t]
                    nq, bnq = ts_["nq"]
                    if edge is not None and hh == 0:
                        be, bbe = ber.next()
                        k.dma("pool", be[:], I["bias_edge"][l][edge][:, 2 * hp:2 * hp + 2, :, :], writes=[bbe], max_dma_last_dim=2048)
                        ts_["be"] = (be, bbe)
                    r0 = 64 * hh
                    pA, bpA = next_bank(); pB, bpB = next_bank()
                    for kt in range(nkt):
                        bank, bbank = (pA, bpA) if kt < 4 else (pB, bpB)
                        sl = kt % 4
                        k.op("pe", MM(bank[:, sl * 128:(sl + 1) * 128], nkf[r0:r0 + 64, hp, kb0 + kt * 128:kb0 + (kt + 1) * 128],
                                      nq[r0:r0 + 64, hp, :], True, True), reads=[bnk, bnq], writes=[bbank], pe_accum=True)
                    for c in range(2):
                        k.op("pe", MM(pB[:, (2 + c) * 128:(3 + c) * 128], nkc[r0:r0 + 64, hp, c * 128:(c + 1) * 128], nq[r0:r0 + 64, hp, :], True, True),
                             reads=[bctx, bnq], writes=[bpB], pe_accum=True)
                    return (pA, bpA, pB, bpB, ts_.get("be"))

                def restU(u, sc_):
                    t, hp, hh = u
                    pA, bpA, pB, bpB, be_ = sc_
                    nkt, kb0, edge = tile_cfg(t)
                    t0_ = t * 128
                    bi = blk_of(t0_)
                    h = 2 * hp + hh
                    ts_ = tstate[t]
                    pso, bpso = pso_fix
                    P, bP = Pr.next()
                    if nkt > 0:
                        ss, bss = ssr.next()
                        if edge is not None:
                            bsrc, bbuf = be_[0][:, hh, :, :], be_[1]
                        else:
                            bsrc, bbuf = bint[:, h, :, :], bbi
                        k.op("dve", TTo(ss[:, 0:4, :], pA[:].rearrange("p (s q) -> p s q", s=4), bsrc[:, 0:4, :], ALU.add),
                             reads=[bpA, bbuf], writes=[bss])
                        k.op("dve", TTo(ss[:, 4:nkt, :], pB[:, 0:(nkt - 4) * 128].rearrange("p (s q) -> p s q", q=128), bsrc[:, 4:nkt, :], ALU.add),
                             reads=[bpB, bbuf], writes=[bss])
                        k.op("act", ACT(P[:, 0:nkt, :], ss[:, 0:nkt, :], AF.Exp), reads=[bss], writes=[bP])
                    k.op("act", ACT(P[:, 6:8, :], pB[:, 256:512].rearrange("p (s q) -> p s q", s=2), AF.Exp), reads=[bpB], writes=[bP])
                    o_ = pso[:, h * 65:(h + 1) * 65]
                    for kt in range(nkt):
                        k.op("pe", MM(o_, P[:, kt, :], nvf[:, kb0 // 128 + kt, h * 65:(h + 1) * 65], kt == 0, False),
                             reads=[bP, bnv], writes=[bpso], inc=False, pe_accum=True)
                    for c in range(2):
                        k.op("pe", MM(o_, P[:, 6 + c, :], nvc[:, c, h * 65:(h + 1) * 65], (nkt == 0 and c == 0), c == 1),
                             reads=[bP, bctx], writes=[bpso], inc=(c == 1), pe_accum=True)
                    if hp == 2 and hh == 1:
                        rd, brd = rdr.next()
                        k.op("dve", RCP(rd[:], pso[:, 0:390].rearrange("p (h c) -> p h c", c=65)[:, :, 64]), reads=[bpso], writes=[brd])
                        ot, bot = otr.next()
                        for h2 in range(6):
                            k.op("dve", TS(ot[:, h2 * 64:(h2 + 1) * 64], pso[:, h2 * 65:h2 * 65 + 64], rd[:, h2:h2 + 1], ALU.mult), reads=[bpso, brd], writes=[bot])
                        pb, bpb = ptr_fix
                        for hp2 in range(3):
                            k.op("pe", TR(pb[:, hp2 * 128:(hp2 + 1) * 128], ot[:, hp2 * 128:(hp2 + 1) * 128], identb[:]), reads=[bot, bc], writes=[bpb],
                                 inc=(hp2 == 2), pe_accum=True)
                        k.op("act", ACT(hy[:, 6:9, t0_:t0_ + 128], pb[:, 0:384].rearrange("p (s q) -> p s q", s=3), AF.Copy), reads=[bpb], writes=[bhy[bi]])
                        del tstate[t]

                pendq = [scoresU(units[0])]
                if len(units) > 1:
                    pendq.append(scoresU(units[1]))
                for ui, u in enumerate(units):
                    if ui + 2 < len(units):
                        pendq.append(scoresU(units[ui + 2]))
                    restU(u, pendq.pop(0))
                if stop != "na":
                    k.stage = "wout"
                    wout_body(l, last, wo, bwo)
                k.flush()

        def wout_body(l, last, wo, bwo):
            KS = [128, 128, 96, 96, 96, 96, 128, 128, 128]
            for bi, (c0, wd) in enumerate(BLOCKS):
                if last and bi == 4:
                    continue
                j = 0 if bi < 4 else 1
                for ct in range(8):
                    ps, bps = psf.next()
                    for s_ in range(9):
                        K_ = KS[s_]
                        k.op("pe", MM(ps[:, :wd], wo[:K_, s_, ct * 128:(ct + 1) * 128], hy[:K_, s_, c0:c0 + wd], s_ == 0, s_ == 8),
                             reads=[bwo, bhy[bi]], writes=[bps], inc=(s_ == 8), pe_accum=True)
                    k.op("dve", STT(xT[:, ct, c0:c0 + wd], ps[:, :wd], mod[:, 16 + ct, j:j + 1], xT[:, ct, c0:c0 + wd], ALU.mult, ALU.add),
                         reads=[bps, B["mod"], bx[bi]], writes=[bx[bi]])

        def ffn(l, last):
            k.stage = "ffn"
            blocks = [bi for bi in range(5) if not (last and bi == 4)]
            with ExitStack() as s2:
                w2r = s2.enter_context(nc.sbuf_tensor(U("sb_w2r"), [128, 22, D], BF16))
                bw2 = Buf("w2r")
                w13 = Rot(nc, s2, "w13", [128, 8, 256], BF16, 2)
                slr = Rot(nc, s2, "sil", [128, 512], F32, 2)
                ust = Rot(nc, s2, "ust", [128, 512], BF16, 3)
                ubr = Rot(nc, s2, "ub", [128, 22, 512], BF16, 1)
                nxt_ada = None
                if l + 1 < nlayers:
                    wrot = Rot(nc, s2, "wada", [128, 8, 256], BF16, 2)
                    nxt_ada = ada_steps(l + 1, wrot)
                ai = 0
                for ft in range(22):
                    w, bw = w13.next()
                    k.dma("pool", w[:, :, 0:128], I["w1"][l][:, ft * 128:(ft + 1) * 128].rearrange("(kt p) c -> p kt c", p=128), writes=[bw])
                    k.dma("pool", w[:, :, 128:256], I["w3"][l][:, ft * 128:(ft + 1) * 128].rearrange("(kt p) c -> p kt c", p=128), writes=[bw])
                    if ft in (2, 6, 10, 14):
                        cq = (ft - 2) // 4
                        k.dma("pool", w2r[:, :, cq * 256:(cq + 1) * 256],
                              I["w2"][l][:, cq * 256:(cq + 1) * 256].rearrange("(kt p) c -> p kt c", p=128), writes=[bw2])
                    for bi in blocks:
                        c0, wd = BLOCKS[bi]
                        pa, bpa = psf.next(); pb_, bpb_ = psf.next()
                        for kt in range(8):
                            k.op("pe", MM(pa[:, :wd], w[:, kt, 0:128], hy[:, kt, c0:c0 + wd], kt == 0, kt == 7), reads=[bw, bhy[bi]], writes=[bpa],
                                 inc=(kt == 7), pe_accum=True)
                        for kt in range(8):
                            k.op("pe", MM(pb_[:, :wd], w[:, kt, 128:256], hy[:, kt, c0:c0 + wd], kt == 0, kt == 7), reads=[bw, bhy[bi]], writes=[bpb_],
                                 inc=(kt == 7), pe_accum=True)
                        sl, bsl = slr.next()
                        k.op("act", ACT(sl[:, :wd], pa[:, :wd], AF.Silu), reads=[bpa], writes=[bsl])
                        u, bu = ust.next()
                        k.op("dve", TTo(u[:, :wd], sl[:, :wd], pb_[:, :wd], ALU.mult), reads=[bsl, bpb_], writes=[bu])
                        k.dma("sp", s_u[ft][:, c0:c0 + wd], u[:, :wd], reads=[bu], writes=[B["s_u"]])
                    if nxt_ada is not None and ai < 24:
                        nxt_ada[0][ai](); ai += 1
                ub_a, bub_a = ubr.next()
                ub_h = hy[:].rearrange("p s t -> p (s t)")[:, 0:22 * 512].rearrange("p (f t) -> p f t", f=22)
                for ii_, bi in enumerate(blocks):
                    c0, wd = BLOCKS[bi]
                    j = 0 if bi < 4 else 1
                    if ii_ % 2 == 0:
                        ub, rd_b, wr_b = ub_a, [bub_a], [bub_a]
                    else:
                        ub, rd_b, wr_b = ub_h, list(bhy), list(bhy)
                    k.dma("sp", ub[:, :, :wd], s_u[:, :, c0:c0 + wd].rearrange("f p t -> p f t"), reads=[B["s_u"]], writes=wr_b)
                    for ct in range(8):
                        ps, bps = psf.next()
                        for ft in range(22):
                            k.op("pe", MM(ps[:, :wd], w2r[:, ft, ct * 128:(ct + 1) * 128], ub[:, ft, :wd], ft == 0, ft == 21),
                                 reads=[bw2] + rd_b, writes=[bps], inc=(ft == 21), pe_accum=True)
                        k.op("dve", STT(xT[:, ct, c0:c0 + wd], ps[:, :wd], mod[:, 40 + ct, j:j + 1], xT[:, ct, c0:c0 + wd], ALU.mult, ALU.add),
                             reads=[bps, B["mod"], bx[bi]], writes=[bx[bi]])
                    if nxt_ada is not None and ai < 24:
                        nxt_ada[0][ai](); ai += 1
                if nxt_ada is not None:
                    while ai < 24:
                        nxt_ada[0][ai](); ai += 1
                    nxt_ada[1]()
                k.flush()

        def final_out():
            k.stage = "final_out"
            with ExitStack() as s2:
                norm(0, range(4), final=True, scope=s2)
                otl = Rot(nc, s2, "otile", [128, D], F32, 2)
                for t in range(16):
                    bi = blk_of(t * 128)
                    o, bo = otl.next()
                    for half in range(2):
                        ps, bps = psf.next()
                        for j in range(4):
                            kt = half * 4 + j
                            k.op("pe", TR(ps[:, j * 128:(j + 1) * 128], xT[:, kt, t * 128:(t + 1) * 128], identf[:]), reads=[bx[bi], bc], writes=[bps],
                                 inc=(j == 3), pe_accum=True)
                        if half == 0:
                            k.op("act", ACT(o[:, 0:512], ps[:], AF.Copy), reads=[bps], writes=[bo])
                        else:
                            k.op("dve", CP(o[:, 512:1024], ps[:]), reads=[bps], writes=[bo])
                    k.dma("sp", out_d[t * 128:(t + 1) * 128, :], o[:], reads=[bo], writes=[B["out"]])
                k.flush()

        done = False
        for l in range(nlayers):
            last = (l == DEPTH - 1)
            deferred_ada = None
            if only is None and l == 0:
                deferred_ada = ada(l)
            if debug and l == 0 and only is None:
                k.dma("sp", d_mod[:, :, :], mod[:], reads=[B["mod"]])
            if stop == "norm":
                norm(0, range(5))
                break
            if only is None:
                project(l, last, deferred=deferred_ada)
            if stop == "proj":
                break
            if only is None:
                fourier(l, last)
            if stop == "four":
                break
            if only in (None, "ret"):
                retention(l, last)
            if stop in ("ret", "ret_tab", "ret_pre", "ret_ag"):
                break
            natten(l, last)
            if stop == "na":
                break
            if stop == "wout":
                break
            norm(1, range(4) if last else range(5))
            ffn(l, last)
            if stop == "ffn":
                break
        else:
            if nlayers == DEPTH:
                final_out()
        if debug and only is None:
            k.dma("sp", d_xT[:, :, :], xT[:], reads=bx)
            k.dma("sp", d_hy[:, :, :], hy[:], reads=bhy)
        k.flush(final=True)
    return nc


_NC_CACHE = {}


def kernel(**inputs):
    inp = {k_: np.asarray(v) for k_, v in inputs.items()}
    maps = make_in_maps(inp)
    if "nc" not in _NC_CACHE:
        _NC_CACHE["nc"] = build()
    res = run_bass_kernel_spmd(_NC_CACHE["nc"], maps, core_ids=list(range(8)))
    out = np.empty((2, L, D), np.float32)
    for core in range(8):
        b, q = core // 4, core % 4
        out[b, TOK * q:TOK * (q + 1)] = np.asarray(res.results[core]["out"], np.float32)
    return out
```

```python
import math
from contextlib import ExitStack
import numpy as np
import ml_dtypes
import concourse.bass as bass
import concourse.mybir as mybir
from concourse.bass_utils import run_bass_kernel_spmd

F32 = mybir.dt.float32
BF16 = mybir.dt.bfloat16
AF = mybir.ActivationFunctionType
ALU = mybir.AluOpType
NPBF = ml_dtypes.bfloat16

D = 1024; L = 8192; LC = 256; DEPTH = 2; PW = 2944; DFF = 2816
TOK = 2048; TT = 2304; NCH = 18
BLOCKS = [(0, 512), (512, 512), (1024, 512), (1536, 512), (2048, 256)]
C_F, C_RQ, C_RK, C_RV, C_RG, C_NQ, C_NK, C_NV = 0, 256, 640, 1024, 1408, 1792, 2176, 2560
EPS = 1e-6
GROUPS = [[0, 1, 2, 3], [4, 5, 6, 7]]


class Buf:
    __slots__ = ("name", "w", "r")

    def __init__(self, name=""):
        self.name = name
        self.w = None
        self.r = {}


class K:
    ENGS = ("pe", "dve", "act", "pool", "sp")

    def __init__(self, nc, st):
        self.nc = nc
        self.prog = {e: [] for e in self.ENGS}
        self.cnt = {}
        self.seen = {e: {} for e in self.ENGS}
        self.sems = {}
        names = ["c_pe", "c_dve", "c_act", "c_pool", "cc"]
        self.ndma = 8
        self.dma_rr = {}
        for e in ("sp", "act", "pool"):
            self.dma_rr[e] = 0
            names += [f"d_{e}{i}" for i in range(self.ndma)]
        for sn in names:
            self.cnt[sn] = 0
            self.sems[sn] = st.enter_context(nc.semaphore(sn))
        self.nblk = 0
        self.stage = "init"

    def _need(self, eng, ev, waits):
        if ev is None:
            return
        sn, val = ev
        if self.seen[eng].get(sn, 0) >= val:
            return
        waits[sn] = max(waits.get(sn, 0), val)

    def _deps(self, eng, reads, writes, pe_accum=False):
        waits = {}
        for b in reads:
            self._need(eng, b.w, waits)
        own = "c_" + eng
        for b in writes:
            if not (b.w is not None and b.w[0] == own and (pe_accum or eng != "pe")):
                self._need(eng, b.w, waits)
            for sn, val in b.r.items():
                self._need(eng, (sn, val), waits)
        for sn, val in waits.items():
            self.seen[eng][sn] = val
            self.prog[eng].append(("wait", sn, val))

    def _mark(self, ev, reads, writes):
        sn, val = ev
        for b in reads:
            b.r[sn] = max(b.r.get(sn, 0), val)
        for b in writes:
            b.w = ev
            b.r = {}

    def op(self, eng, fn, reads=(), writes=(), inc=True, pe_accum=False):
        self._deps(eng, reads, writes, pe_accum)
        sn = "c_" + eng
        val = self.cnt[sn] + 1
        self._mark((sn, val), reads, writes)
        if inc:
            self.cnt[sn] = val
            self.prog[eng].append(("op", fn, sn, 1))
        else:
            self.prog[eng].append(("op", fn, None, 0))

    def dma(self, eng, out, in_, reads=(), writes=(), **kw):
        self._deps(eng, reads, writes)
        i = self.dma_rr[eng]
        self.dma_rr[eng] = (i + 1) % self.ndma
        sn = f"d_{eng}{i}"
        prev = self.cnt[sn]
        if prev > 0 and self.seen[eng].get(sn, 0) < prev:
            self.seen[eng][sn] = prev
            self.prog[eng].append(("wait", sn, prev))
        val = prev + 16
        self.cnt[sn] = val
        self._mark((sn, val), reads, writes)
        self.prog[eng].append(("dma", out, in_, sn, kw))

    def cc(self, fn, reads=(), writes=()):
        eng = "pool"
        self._deps(eng, reads, writes)
        sn = "cc"
        val = self.cnt[sn] + 1
        self.cnt[sn] = val
        self._mark((sn, val), reads, writes)
        self.prog[eng].append(("op", fn, sn, None))

    def barrier(self, final=False):
        snap = dict(self.cnt)
        for e in self.ENGS:
            for sn, val in snap.items():
                if sn == "cc" and not final:
                    continue
                if val > 0 and self.seen[e].get(sn, 0) < val:
                    self.seen[e][sn] = val
                    self.prog[e].append(("wait", sn, val))

    def flush(self, final=False):
        self.barrier(final)
        nc = self.nc
        sems = self.sems
        prog = self.prog
        self.prog = {e: [] for e in self.ENGS}
        self.nblk += 1

        def run(e, items):
            for it in items:
                if it[0] == "wait":
                    e.wait_ge(sems[it[1]], it[2])
                elif it[0] == "op":
                    ins = it[1](e)
                    if it[2] is not None:
                        if it[3] is None:
                            ins.then_inc(sems[it[2]])
                        else:
                            ins.then_inc(sems[it[2]], it[3])
                else:
                    e.dma_start(out=it[1], in_=it[2], **it[4]).then_inc(sems[it[3]], 16)

        with nc.named_scope(f"{self.stage}_{self.nblk}"), nc.Block() as block:
            @block.tensor
            def _(e):
                run(e, prog["pe"])

            @block.vector
            def _(e):
                run(e, prog["dve"])

            @block.scalar
            def _(e):
                run(e, prog["act"])

            @block.gpsimd
            def _(e):
                run(e, prog["pool"])

            @block.sync
            def _(e):
                run(e, prog["sp"])


_UC = [0]


def U(name):
    _UC[0] += 1
    return f"{name}_{_UC[0]}"


class Rot:
    def __init__(self, nc, st, name, shape, dtype, n, psum=False):
        self.items = []
        for i in range(n):
            if psum:
                t = st.enter_context(nc.psum_tensor(U(f"ps_{name}{i}"), shape, dtype))
            else:
                t = st.enter_context(nc.sbuf_tensor(U(f"sb_{name}{i}"), shape, dtype))
            self.items.append((t, Buf(f"{name}{i}")))
        self.i = 0

    def next(self):
        it = self.items[self.i]
        self.i = (self.i + 1) % len(self.items)
        return it


def _bf(a):
    return np.ascontiguousarray(np.asarray(a, np.float32).astype(NPBF))


def _consts_common():
    c = {}
    c["ident_f"] = np.eye(128, dtype=np.float32)
    c["ident_b"] = _bf(np.eye(128))
    c["ones_b"] = _bf(np.ones((128, 128)))
    j = np.arange(128)[:, None]; i = np.arange(128)[None, :]
    dm = np.zeros((128, 4, 128), np.float32)
    dm[:, 0] = np.maximum(i - j, 0); dm[:, 1] = (i >= j)
    dm[:, 2] = np.maximum(j - i, 0); dm[:, 3] = (j >= i)
    c["dmask"] = dm
    eq = np.zeros((128, 2, 128), np.float32)
    eq[:, 0, :] = np.arange(128)[None, :] + 1.0
    eq[:, 1, :] = 128.0 - np.arange(128)[None, :]
    c["eq"] = eq
    ek = np.zeros((128, 2), np.float32)
    ek[:, 0] = 127.0 - np.arange(128); ek[:, 1] = np.arange(128)
    c["ek"] = ek
    ec = np.zeros((128, 2, 16), np.float32)
    ec[:, 0, :] = 128.0 * np.arange(16)[None, :]
    ec[:, 1, :] = 128.0 * (15 - np.arange(16))[None, :]
    c["ec"] = ec
    r = np.arange(128)[:, None].astype(np.float64); l1 = np.arange(128)[None, :].astype(np.float64)
    nrm = 1.0 / math.sqrt(L * 64.0)
    ang = 2 * np.pi * r * l1 / 128.0
    c["d128"] = _bf(np.concatenate([np.cos(ang) * nrm, -np.sin(ang) * nrm], 1))
    cc = np.arange(64)[:, None].astype(np.float64); cp = np.arange(64)[None, :].astype(np.float64)
    a64 = 2 * np.pi * cc * cp / 64.0
    c64 = np.zeros((128, 128)); s64 = np.zeros((128, 128))
    for g in range(2):
        c64[64 * g:64 * g + 64, 64 * g:64 * g + 64] = np.cos(a64)
        s64[64 * g:64 * g + 64, 64 * g:64 * g + 64] = -np.sin(a64)
    c["c64"] = _bf(c64); c["s64n"] = _bf(s64)
    pos = np.arange(256).astype(np.float64)[:, None]; lp = np.arange(256).astype(np.float64)[None, :]
    a256 = 2 * np.pi * pos * lp / 256.0
    nc_ = 1.0 / math.sqrt(256 * 64.0)
    dc = np.concatenate([np.cos(a256) * nc_, np.sin(a256) * nc_], 1)
    c["dc256"] = _bf(dc.reshape(2, 128, 512).transpose(1, 0, 2))
    return c


def _consts_core(q):
    c = {}
    t = np.arange(TOK) + TOK * q
    prow = (t // 64).astype(np.float32); pcol = (t % 64).astype(np.float32)
    inv = (10000.0 ** (-np.arange(24, dtype=np.float32) / 24)).astype(np.float32)
    cos = np.zeros((96, TOK), np.float32); sin = np.zeros((96, TOK), np.float32)
    for i in range(96):
        part, rr = i // 48, i % 48
        m = rr % 24
        ang = ((prow if part == 0 else pcol) * inv[m]).astype(np.float32)
        cos[i] = np.cos(ang)
        sin[i] = -np.sin(ang) if rr < 24 else np.sin(ang)
    ks = np.float32(96 ** -0.5)
    c["rope"] = np.stack([cos, sin, cos * ks, sin * ks], 1)
    ex = np.zeros((128, 5, 2), np.float32); mx = np.zeros((128, 5, 2), np.float32)
    for j in range(4):
        if j < q:
            ex[:, j, 0] = 2048.0 * (q - 1 - j); mx[:, j, 0] = 1
        if j > q:
            ex[:, j, 1] = 2048.0 * (j - q - 1); mx[:, j, 1] = 1
    ex[:, 4, 0] = 2048.0 * q; mx[:, 4, 0] = 1
    ex[:, 4, 1] = 2048.0 * (3 - q); mx[:, 4, 1] = 1
    c["exmx"] = np.stack([ex, mx], 1)
    mh = np.zeros((128, 8), np.float32)
    if q > 0:
        mh[:, q - 1] = 1
    if q < 3:
        mh[:, 4 + q + 1] = 1
    c["mh"] = mh
    w = np.arange(64, dtype=np.float64)[:, None, None]
    l1 = np.arange(128, dtype=np.float64)[None, :, None]
    l2 = (16 * q + np.arange(16, dtype=np.float64))[None, None, :]
    th = 2 * np.pi * w * (l1 + 128 * l2) / 8192.0
    k2 = np.zeros((64, 128, 2, 32))
    k2[:, :, 0, 0:16] = np.cos(th); k2[:, :, 0, 16:32] = np.sin(th)
    k2[:, :, 1, 0:16] = np.sin(th); k2[:, :, 1, 16:32] = -np.cos(th)
    c["k2s"] = _bf(k2.transpose(2, 0, 1, 3).reshape(128, 128, 32))
    return c


def _na_bias(rpb, q):
    NEG = np.float32(-1e30)

    def table(t, kb0, nkt):
        qi = np.arange(128)
        lr = 2 * t + qi // 64
        r = 32 * q + lr
        col = qi % 64
        kb = kb0 + np.arange(nkt * 128)
        br = kb // 64; kc = kb % 64
        kr = 32 * q + (br - 4)
        r0 = np.clip(r - 4, 0, 120)
        rok = (kr[:, None] >= r0[None, :]) & (kr[:, None] < r0[None, :] + 8) & (kr[:, None] >= 0) & (kr[:, None] < 128)
        ws = np.clip(col - 8, 0, 48)
        cok = (kc[:, None] >= ws[None, :]) & (kc[:, None] < ws[None, :] + 16)
        ok = rok & cok
        dr = np.clip(kr[:, None] - r[None, :] + 7, 0, 14)
        dcc = np.clip(kc[:, None] - col[None, :] + 15, 0, 30)
        out = np.empty((6, nkt * 128, 128), np.float32)
        for h in range(6):
            out[h] = np.where(ok, rpb[h][dr, dcc], NEG)
        return out.reshape(6, nkt, 128, 128).transpose(2, 0, 1, 3)

    inter = table(5, 128 * 5, 5)
    edge = np.full((4, 128, 6, 6, 128), NEG, np.float32)
    edge[0] = table(0, 0, 6)
    edge[1][:, :, :5] = table(1, 128, 5)
    edge[2][:, :, :5] = table(14, 128 * 14, 5)
    edge[3] = table(15, 1792, 6)
    return np.ascontiguousarray(inter), np.ascontiguousarray(edge)


def _col8(v):
    return np.ascontiguousarray(np.asarray(v, np.float32).reshape(8, 128).T)


def make_in_maps(inp):
    cm = _consts_common()
    maps = []
    for core in range(8):
        b, q = core // 4, core % 4
        m = dict(cm)
        m.update(_consts_core(q))
        m["x_in"] = np.ascontiguousarray(inp["x"][b, TOK * q:TOK * (q + 1)])
        m["ctx_in"] = np.ascontiguousarray(inp["ctx"][b])
        m["cvec"] = np.ascontiguousarray(np.stack([_col8(inp["c"][b]), _col8(inp["c_ctx"])], 2))
        m["b_ada"] = np.ascontiguousarray(np.stack(
            [np.asarray(inp["b_ada"][l], np.float32).reshape(48, 128).T for l in range(DEPTH)], 0))
        m["g_mix"] = np.stack([_col8(inp["g_mix"][l]) for l in range(DEPTH)], 0)
        m["g_ffn"] = np.stack([_col8(inp["g_ffn"][l]) for l in range(DEPTH)], 0)
        m["g_final"] = _col8(inp["g_final"])
        rg = np.asarray(inp["ret_norm_g"], np.float32).reshape(DEPTH, 4, 96).transpose(0, 2, 1)
        m["ret_gain"] = np.ascontiguousarray(np.broadcast_to(rg[:, :, :, None], (DEPTH, 96, 4, 128)))
        lg = np.asarray(inp["ret_decay_logit"], np.float32).reshape(DEPTH, 1, 8)
        m["logit"] = np.ascontiguousarray(np.broadcast_to(lg, (DEPTH, 128, 8)))
        bi, be = [], []
        for l in range(DEPTH):
            a, e = _na_bias(np.asarray(inp["na_rpb"][l], np.float32), q)
            bi.append(a); be.append(e)
        m["bias_int"] = np.stack(bi, 0)
        m["bias_edge"] = np.stack(be, 0)
        for nme in ("w_ada", "w_in", "w_four", "w_out", "w1", "w3", "w2"):
            m[nme] = np.ascontiguousarray(np.asarray(inp[nme], np.float32))
        maps.append(m)
    return maps


IN_SPECS = {
    "x_in": ([TOK, D], F32), "ctx_in": ([LC, D], F32), "cvec": ([128, 8, 2], F32),
    "b_ada": ([DEPTH, 128, 48], F32), "g_mix": ([DEPTH, 128, 8], F32), "g_ffn": ([DEPTH, 128, 8], F32),
    "g_final": ([128, 8], F32), "ret_gain": ([DEPTH, 96, 4, 128], F32), "logit": ([DEPTH, 128, 8], F32),
    "bias_int": ([DEPTH, 128, 6, 5, 128], F32), "bias_edge": ([DEPTH, 4, 128, 6, 6, 128], F32),
    "w_ada": ([DEPTH, D, 6 * D], F32), "w_in": ([DEPTH, D, PW], F32), "w_four": ([DEPTH, 256, 256], F32),
    "w_out": ([DEPTH, D, D], F32), "w1": ([DEPTH, D, DFF], F32), "w3": ([DEPTH, D, DFF], F32),
    "w2": ([DEPTH, DFF, D], F32),
    "ident_f": ([128, 128], F32), "ident_b": ([128, 128], BF16), "ones_b": ([128, 128], BF16),
    "dmask": ([128, 4, 128], F32), "eq": ([128, 2, 128], F32), "ek": ([128, 2], F32), "ec": ([128, 2, 16], F32),
    "d128": ([128, 256], BF16), "c64": ([128, 128], BF16), "s64n": ([128, 128], BF16), "dc256": ([128, 2, 512], BF16),
    "rope": ([96, 4, TOK], F32), "exmx": ([128, 2, 5, 2], F32), "mh": ([128, 8], F32), "k2s": ([128, 128, 32], BF16),
}


def MM(out, lhsT, rhs, start, stop):
    return lambda e: e.matmul(out, lhsT, rhs, start=start, stop=stop)


def TR(out, in_, ident):
    return lambda e: e.transpose(out=out, in_=in_, identity=ident)


def ACT(out, in_, func, **kw):
    return lambda e: e.activation(out=out, in_=in_, func=func, **kw)


def TTo(out, in0, in1, op):
    return lambda e: e.tensor_tensor(out=out, in0=in0, in1=in1, op=op)


def TS(out, in0, s1, op0, s2=None, op1=None):
    if op1 is None:
        return lambda e: e.tensor_scalar(out=out, in0=in0, scalar1=s1, scalar2=None, op0=op0)
    return lambda e: e.tensor_scalar(out=out, in0=in0, scalar1=s1, scalar2=s2, op0=op0, op1=op1)


def STT(out, in0, scalar, in1, op0, op1):
    return lambda e: e.scalar_tensor_tensor(out=out, in0=in0, scalar=scalar, in1=in1, op0=op0, op1=op1)


def CP(out, in_):
    return lambda e: e.tensor_copy(out=out, in_=in_)


def MS(ap, v):
    return lambda e: e.memset(ap, v)


def RCP(out, in_):
    return lambda e: e.reciprocal(out=out, in_=in_)


def build(debug=False, nlayers=DEPTH, stop=None, only=None):
    nc = bass.Bass("TRN2", target_bir_lowering=False)
    I = {n: nc.dram_tensor(n, shp, dt, kind="ExternalInput").ap() for n, (shp, dt) in IN_SPECS.items()}
    out_d = nc.dram_tensor("out", [TOK, D], F32, kind="ExternalOutput").ap()
    dbgk = "ExternalOutput" if debug else None

    def scr(name, shape, dt=BF16, dbg=True):
        if debug and dbg:
            return nc.dram_tensor(name, shape, dt, kind="ExternalOutput").ap()
        return nc.dram_tensor(name, shape, dt).ap()

    s_q = scr("s_q", [4, 96, TT]); s_k = scr("s_k", [4, 96, TT]); s_g = scr("s_g", [4, 96, TT])
    s_v = scr("s_v", [TT, 384]); s_nq = scr("s_nq", [3, 128, TT]); s_nk = scr("s_nk", [3, 128, TT])
    s_nv = scr("s_nv", [TT, 390]); s_u = scr("s_u", [22, 128, TT], dbg=False)
    f_c = scr("f_c", [TOK, 256], dbg=False); f_g = scr("f_g", [4 * TOK, 256], dbg=False)
    st_c = scr("st_c", [96, 768], F32, dbg=False); st_g = scr("st_g", [4 * 96, 768], F32, dbg=False)
    ke_c = scr("ke_c", [384, 448], dbg=False); ke_g = scr("ke_g", [4 * 384, 448], dbg=False)
    ve_c = scr("ve_c", [448, 390], dbg=False); ve_g = scr("ve_g", [4 * 448, 390], dbg=False)
    if debug:
        d_xT = nc.dram_tensor("d_xT", [128, 8, TT], F32, kind="ExternalOutput").ap()
        d_hy = nc.dram_tensor("d_hy", [128, 9, TT], BF16, kind="ExternalOutput").ap()
        d_mod = nc.dram_tensor("d_mod", [128, 48, 2], F32, kind="ExternalOutput").ap()
    B = {n: Buf(n) for n in ("s_q s_k s_g s_v s_nq s_nk s_nv s_u f_c f_g st_c st_g ke_c ke_g ve_c ve_g out "
                             "const mod gs fctx").split()}

    with ExitStack() as st:
        k = K(nc, st)
        xT = st.enter_context(nc.sbuf_tensor(U("sb_xT"), [128, 8, TT], F32))
        hy = st.enter_context(nc.sbuf_tensor(U("sb_hy"), [128, 9, TT], BF16))
        bx = [Buf(f"x{i}") for i in range(5)]
        bhy = [Buf(f"hy{i}") for i in range(5)]
        identf = st.enter_context(nc.sbuf_tensor(U("sb_identf"), [128, 128], F32))
        identb = st.enter_context(nc.sbuf_tensor(U("sb_identb"), [128, 128], BF16))
        onesb = st.enter_context(nc.sbuf_tensor(U("sb_onesb"), [128, 128], BF16))
        epsc = st.enter_context(nc.sbuf_tensor(U("sb_epsc"), [128, 1], F32))
        cvec = st.enter_context(nc.sbuf_tensor(U("sb_cvec"), [128, 8, 2], F32))
        scv = st.enter_context(nc.sbuf_tensor(U("sb_scv"), [128, 8, 2], F32))
        scvb = st.enter_context(nc.sbuf_tensor(U("sb_scvb"), [128, 8, 2], BF16))
        mod = st.enter_context(nc.sbuf_tensor(U("sb_mod"), [128, 48, 2], F32))
        gs = st.enter_context(nc.sbuf_tensor(U("sb_gs"), [128, 2, 8, 2], F32))
        bad = st.enter_context(nc.sbuf_tensor(U("sb_bad"), [128, 48], F32))
        gvec = st.enter_context(nc.sbuf_tensor(U("sb_gvec"), [128, 3, 8], F32))
        fctx = st.enter_context(nc.sbuf_tensor(U("sb_fctx"), [128, 2, 256], BF16))
        psf = Rot(nc, st, "psf", [128, 512], F32, 4, psum=True)
        psl = Rot(nc, st, "psl", [128, 512], F32, 2, psum=True)
        psb = Rot(nc, st, "psb", [128, 1024], BF16, 2, psum=True)
        bc = B["const"]
        k.dma("sp", identf[:], I["ident_f"][:, :], writes=[bc])
        k.dma("sp", identb[:], I["ident_b"][:, :], writes=[bc])
        k.dma("sp", onesb[:], I["ones_b"][:, :], writes=[bc])
        k.dma("sp", cvec[:], I["cvec"][:, :, :], writes=[bc])
        k.op("dve", MS(epsc[:], EPS), writes=[bc])
        k.op("act", ACT(scv[:], cvec[:], AF.Silu), reads=[bc], writes=[bc])
        k.op("act", ACT(scvb[:], cvec[:], AF.Silu), reads=[bc], writes=[bc])
        k.dma("sp", gvec[:, 2, :], I["g_final"][:, :], writes=[bc])

        def blk_of(c0):
            return min(c0 // 512, 4)

        def load_x(s2):
            xtok = Rot(nc, s2, "xtok", [128, D], F32, 2)
            for t in range(18):
                src = I["x_in"][t * 128:(t + 1) * 128, :] if t < 16 else I["ctx_in"][(t - 16) * 128:(t - 15) * 128, :]
                xt, bxt = xtok.next()
                k.dma("sp", xt[:], src, writes=[bxt])
                bi = blk_of(t * 128)
                for half in range(2):
                    ps, bps = psf.next()
                    for j in range(4):
                        kt = half * 4 + j
                        k.op("pe", TR(ps[:, j * 128:(j + 1) * 128], xt[:, kt * 128:(kt + 1) * 128], identf[:]),
                             reads=[bxt, bc], writes=[bps], inc=(j == 3), pe_accum=True)
                    if half == 0:
                        k.op("act", ACT(xT[:, half * 4:half * 4 + 4, t * 128:(t + 1) * 128], ps[:].rearrange("p (j c) -> p j c", j=4), AF.Copy),
                             reads=[bps], writes=[bx[bi]])
                    else:
                        k.op("dve", CP(xT[:, half * 4:half * 4 + 4, t * 128:(t + 1) * 128], ps[:].rearrange("p (j c) -> p j c", j=4)),
                             reads=[bps], writes=[bx[bi]])

        def ada_steps(l, wrot):
            psA, bpsA = psl.next()
            rotbox = [wrot]
            tiles = {}
            state = {"ld": 0, "mm": 0, "cap": 24}

            def load(cb):
                w, bw = rotbox[0].next()
                k.dma("pool", w[:], I["w_ada"][l][:, cb * 256:(cb + 1) * 256].rearrange("(kt p) c -> p kt c", p=128), writes=[bw])
                tiles[cb] = (w, bw)

            def mm(cb):
                w, bw = tiles.pop(cb)
                for ct in range(2):
                    col = cb * 2 + ct
                    for kt in range(8):
                        k.op("pe", MM(psA[:, col * 2:col * 2 + 2], w[:, kt, ct * 128:(ct + 1) * 128], scvb[:, kt, :], kt == 0, kt == 7),
                             reads=[bw, bc], writes=[bpsA], inc=(kt == 7 and ct == 1), pe_accum=True)

            def step():
                ahead = len(rotbox[0].items) - 1
                while state["ld"] < min(state["cap"], state["mm"] + 1 + ahead):
                    load(state["ld"]); state["ld"] += 1
                mm(state["mm"]); state["mm"] += 1

            steps = [step] * 24

            def fin(part=None):
                pv = psA[:, 0:96].rearrange("p (t j) -> p t j", j=2)
                if part in (None, "a"):
                    k.dma("sp", bad[:], I["b_ada"][l], writes=[B["mod"]])
                    k.dma("sp", gvec[:, 0, :], I["g_mix"][l], writes=[B["mod"]])
                    k.dma("sp", gvec[:, 1, :], I["g_ffn"][l], writes=[B["mod"]])
                lo, hi = {None: (0, 48), "a": (0, 16), "b": (16, 48)}[part]
                for j in range(2):
                    k.op("dve", TTo(mod[:, lo:hi, j], pv[:, lo:hi, j], bad[:, lo:hi], ALU.add), reads=[bpsA, B["mod"], B["gs"]], writes=[B["mod"]])
                for j in range(2):
                    if part in (None, "a"):
                        k.op("dve", STT(gs[:, 0, :, j], mod[:, 8:16, j], 1.0, gvec[:, 0, :], ALU.add, ALU.mult),
                             reads=[B["mod"]], writes=[B["gs"]])
                    if part in (None, "b"):
                        k.op("dve", STT(gs[:, 1, :, j], mod[:, 32:40, j], 1.0, gvec[:, 1, :], ALU.add, ALU.mult),
                             reads=[B["mod"]], writes=[B["gs"]])
            fin.rotbox = rotbox
            fin.state = state
            return steps, fin

        def ada(l):
            k.stage = "ada"
            with ExitStack() as s2:
                wrot = Rot(nc, s2, "wada", [128, 8, 256], BF16, 3)
                steps, fin = ada_steps(l, wrot)
                fin.state["cap"] = 8
                for st_ in steps[:4]:
                    st_()
                if l == 0:
                    load_x(s2)
                for st_ in steps[4:8]:
                    st_()
                fin("a")
                k.flush()
            return steps[8:], fin

        def norm(a, blocks, final=False, scope=None):
            if scope is None:
                k.stage = "norm"
            blocks = list(blocks)
            with ExitStack() as s2_own:
                s2 = s2_own if scope is None else scope
                sqra = Rot(nc, s2, "sqra", [128, 4, 512], BF16, 2)
                sqrb = Rot(nc, s2, "sqrb", [128, 4, 512], BF16, 2)
                rsr = Rot(nc, s2, "rsr", [128, 512], F32, 2)
                tmr = Rot(nc, s2, "tmr", [128, 512], F32, 3)

                def ph1(bi):
                    c0, wd = BLOCKS[bi]
                    sqa, bsqa = sqra.next()
                    sqb, bsqb = sqrb.next()
                    k.op("pool", TTo(sqa[:, :, :wd], xT[:, 0:4, c0:c0 + wd], xT[:, 0:4, c0:c0 + wd], ALU.mult), reads=[bx[bi]], writes=[bsqa])
                    k.op("act", ACT(sqb[:, :, :wd], xT[:, 4:8, c0:c0 + wd], AF.Square), reads=[bx[bi]], writes=[bsqb])
                    ps, bps = psf.next()
                    for kt in range(8):
                        sq_, bsq_ = (sqa, bsqa) if kt < 4 else (sqb, bsqb)
                        k.op("pe", MM(ps[:, :wd], onesb[:], sq_[:, kt % 4, :wd], kt == 0, kt == 7), reads=[bsq_, bc],
                             writes=[bps], inc=(kt == 7), pe_accum=True)
                    rs, brs = rsr.next()
                    k.op("act", ACT(rs[:, :wd], ps[:, :wd], AF.Ln, scale=1.0 / D, bias=epsc[:]), reads=[bps, bc], writes=[brs])
                    k.op("act", ACT(rs[:, :wd], rs[:, :wd], AF.Exp, scale=-0.5), reads=[brs], writes=[brs])
                    return rs, brs

                def ph2(bi, rs, brs):
                    c0, wd = BLOCKS[bi]
                    j = 0 if bi < 4 else 1
                    for kt in range(8):
                        tm, btm = tmr.next()
                        k.op("dve", TTo(tm[:, :wd], xT[:, kt, c0:c0 + wd], rs[:, :wd], ALU.mult), reads=[bx[bi], brs],
                             writes=[btm])
                        if final:
                            k.op("dve", TS(xT[:, kt, c0:c0 + wd], tm[:, :wd], gvec[:, 2, kt:kt + 1], ALU.mult),
                                 reads=[btm, bc], writes=[bx[bi]])
                        else:
                            k.op("act", ACT(hy[:, kt, c0:c0 + wd], tm[:, :wd], AF.Identity, scale=gs[:, a, kt, j:j + 1],
                                            bias=mod[:, 24 * a + kt, j:j + 1]),
                                 reads=[btm, B["gs"], B["mod"]], writes=[bhy[bi]])

                pend = ph1(blocks[0])
                for i_, bi in enumerate(blocks):
                    nxt = ph1(blocks[i_ + 1]) if i_ + 1 < len(blocks) else None
                    ph2(bi, *pend)
                    pend = nxt
                if scope is None:
                    k.flush()

        def AG(src, dst, bsrc, bdst):
            k.cc(lambda e: e.collective_compute("AllGather", ALU.bypass, replica_groups=GROUPS,
                                                ins=[src], outs=[dst]), reads=[bsrc], writes=[bdst])

        def na_halo_exchange():
            k.dma("sp", ke_c.rearrange("(h p) t -> h p t", p=128)[:, :, 0:192], s_nk[:, :, 0:192], reads=[B["s_nk"]], writes=[B["ke_c"]])
            k.dma("sp", ke_c.rearrange("(h p) t -> h p t", p=128)[:, :, 192:448], s_nk[:, :, 1792:2048], reads=[B["s_nk"]], writes=[B["ke_c"]])
            k.dma("sp", ve_c[0:192, :], s_nv[0:192, :], reads=[B["s_nv"]], writes=[B["ve_c"]])
            k.dma("sp", ve_c[192:448, :], s_nv[1792:2048, :], reads=[B["s_nv"]], writes=[B["ve_c"]])
            AG(ke_c[:, :], ke_g[:, :], B["ke_c"], B["ke_g"])
            AG(ve_c[:, :], ve_g[:, :], B["ve_c"], B["ve_g"])

        def load_w(rot, src2d, ncols, krows=8):
            w, bw = rot.next()
            k.dma("pool", w[:, :krows, :ncols], src2d.rearrange("(kt p) c -> p kt c", p=128), writes=[bw])
            return w, bw

        def project(l, last, deferred=None):
            k.stage = "project"
            wl = I["w_in"][l]
            with ExitStack() as s2:
                dq = []
                if deferred is not None:
                    dsteps, dfin = deferred
                    dfin.rotbox[0] = Rot(nc, s2, "wada2", [128, 8, 256], BF16, 3)
                    dfin.state["cap"] = 24
                    dq = list(dsteps)

                def ada_tick():
                    if dq:
                        dq.pop(0)()

                wbr = Rot(nc, s2, "wbr", [128, 8, 512], BF16, 2)
                wpr = Rot(nc, s2, "wpr", [128, 8, 384], BF16, 1)
                stg = Rot(nc, s2, "stg", [128, 512], BF16, 6)
                ropr = Rot(nc, s2, "ropr", [96, 4, 512], F32, 2)
                tA = Rot(nc, s2, "tA", [96, 512], F32, 2)
                tB = Rot(nc, s2, "tB", [96, 512], F32, 2)
                nvs = Rot(nc, s2, "nvs", [128, 6, 65], BF16, 2)
                for tle, bb in nvs.items:
                    k.op("pool", MS(tle[:], 1.0), writes=[bb])
                norm(0, range(5), scope=s2)

                def fm_group(w, bw, m0, M, evac, skip_ctx=False):
                    for bi, (c0, wd) in enumerate(BLOCKS):
                        if skip_ctx and bi == 4:
                            continue
                        ps, bps = psf.next()
                        for kt in range(8):
                            k.op("pe", MM(ps[:M, :wd], w[:, kt, m0:m0 + M], hy[:, kt, c0:c0 + wd], kt == 0, kt == 7),
                                 reads=[bw, bhy[bi]], writes=[bps], inc=(kt == 7), pe_accum=True)
                        evac(ps, bps, bi, c0, wd)

                def tm_group(w, bw, ncols, evac, tiles=range(18)):
                    for t in tiles:
                        bi = blk_of(t * 128)
                        ps, bps = psf.next()
                        for kt in range(8):
                            k.op("pe", MM(ps[:, :ncols], hy[:, kt, t * 128:(t + 1) * 128], w[:, kt, :ncols], kt == 0, kt == 7),
                                 reads=[bw, bhy[bi]], writes=[bps], inc=(kt == 7), pe_accum=True)
                        evac(ps, bps, t)

                w, bw = load_w(wbr, wl[:, C_F:C_F + 256], 256)

                def ev_f(ps, bps, t):
                    if t < 16:
                        s, bs = stg.next()
                        k.op("act", ACT(s[:, :256], ps[:, :256], AF.Copy), reads=[bps], writes=[bs])
                        k.dma("sp", f_c[t * 128:(t + 1) * 128, :], s[:, :256], reads=[bs], writes=[B["f_c"]])
                    else:
                        k.op("act", ACT(fctx[:, t - 16, :], ps[:, :256], AF.Copy), reads=[bps], writes=[B["fctx"]])
                tm_group(w, bw, 256, ev_f, tiles=(range(16) if last else range(18)))
                ada_tick()

                for which, c_off, dst, bdst in ((0, C_RQ, s_q, B["s_q"]), (1, C_RK, s_k, B["s_k"])):
                    w, bw = load_w(wbr, wl[:, c_off:c_off + 384], 384)
                    wp, bwp = wpr.next()
                    w5 = w[:, :, 0:384].rearrange("p k (a b c) -> p k a b c", a=8, b=2, c=24)
                    p5 = wp[:, :, 0:384].rearrange("p k (a b c) -> p k a b c", a=8, b=2, c=24)
                    for b_ in range(2):
                        k.op("pool", CP(p5[:, :, :, b_, :], w5[:, :, :, 1 - b_, :]), reads=[bw], writes=[bwp])
                    for bi, (c0, wd) in enumerate(BLOCKS):
                        ada_tick()
                        if last and which == 0 and bi == 4:
                            continue
                        if bi < 4:
                            rp, brp = ropr.next()
                            k.dma("sp", rp[:], I["rope"][:, :, c0:c0 + wd], writes=[brp])
                        for h in range(4):
                            ps, bps = psf.next()
                            for kt in range(8):
                                k.op("pe", MM(ps[:96, :wd], w[:, kt, h * 96:(h + 1) * 96], hy[:, kt, c0:c0 + wd], kt == 0, kt == 7),
                                     reads=[bw, bhy[bi]], writes=[bps], inc=(kt == 7), pe_accum=True)
                            s, bs = stg.next()
                            if bi < 4:
                                ps2, bps2 = psf.next()
                                for kt in range(8):
                                    k.op("pe", MM(ps2[:96, :wd], wp[:, kt, h * 96:(h + 1) * 96], hy[:, kt, c0:c0 + wd], kt == 0, kt == 7),
                                         reads=[bwp, bhy[bi]], writes=[bps2], inc=(kt == 7), pe_accum=True)
                                a_, ba_ = tA.next()
                                b2, bb2 = tB.next()
                                k.op("dve", TTo(a_[:, :wd], ps[:96, :wd], rp[:, 2 * which, :wd], ALU.mult), reads=[bps, brp], writes=[ba_])
                                k.op("dve", TTo(b2[:, :wd], ps2[:96, :wd], rp[:, 2 * which + 1, :wd], ALU.mult), reads=[bps2, brp], writes=[bb2])
                                k.op("pool", TTo(s[:96, :wd], a_[:, :wd], b2[:, :wd], ALU.add), reads=[ba_, bb2], writes=[bs])
                            else:
                                k.op("act", ACT(s[:96, :wd], ps[:96, :wd], AF.Copy, scale=(1.0 if which == 0 else 96 ** -0.5)),
                                     reads=[bps], writes=[bs])
                            k.dma("sp", dst[h][:, c0:c0 + wd], s[:96, :wd], reads=[bs], writes=[bdst])

                w, bw = load_w(wbr, wl[:, C_RV:C_RV + 384], 384)

                def ev_v(ps, bps, t):
                    s, bs = stg.next()
                    k.op("act", ACT(s[:, :384], ps[:, :384], AF.Copy), reads=[bps], writes=[bs])
                    k.dma("sp", s_v[t * 128:(t + 1) * 128, :], s[:, :384], reads=[bs], writes=[B["s_v"]])
                tm_group(w, bw, 384, ev_v)
                ada_tick()

                w, bw = load_w(wbr, wl[:, C_RG:C_RG + 384], 384)
                for h in range(4):
                    def ev_g(ps, bps, bi, c0, wd, h=h):
                        s, bs = stg.next()
                        k.op("act", ACT(s[:96, :wd], ps[:96, :wd], AF.Silu), reads=[bps], writes=[bs])
                        k.dma("sp", s_g[h][:, c0:c0 + wd], s[:96, :wd], reads=[bs], writes=[B["s_g"]])
                    fm_group(w, bw, h * 96, 96, ev_g, skip_ctx=last)
                    ada_tick()

                for c_off, dst, bdst, scl in ((C_NQ, s_nq, B["s_nq"], 0.125), (C_NK, s_nk, B["s_nk"], 1.0)):
                    w, bw = load_w(wbr, wl[:, c_off:c_off + 384], 384)
                    for hp in range(3):
                        def ev_n(ps, bps, bi, c0, wd, hp=hp, dst=dst, bdst=bdst, scl=scl):
                            s, bs = stg.next()
                            k.op("act", ACT(s[:, :wd], ps[:, :wd], AF.Copy, scale=scl), reads=[bps], writes=[bs])
                            k.dma("sp", dst[hp][:, c0:c0 + wd], s[:, :wd], reads=[bs], writes=[bdst])
                        fm_group(w, bw, hp * 128, 128, ev_n, skip_ctx=(last and scl != 1.0))

                w, bw = load_w(wbr, wl[:, C_NV:C_NV + 384], 384)

                def ev_nv(ps, bps, t):
                    s, bs = nvs.next()
                    k.op("act", ACT(s[:, :, 0:64], ps[:, :384].rearrange("p (h c) -> p h c", h=6), AF.Copy), reads=[bps], writes=[bs])
                    k.dma("sp", s_nv[t * 128:(t + 1) * 128, :], s[:].rearrange("p h c -> p (h c)"), reads=[bs], writes=[B["s_nv"]])
                tm_group(w, bw, 384, ev_nv)
                while dq:
                    ada_tick()
                if deferred is not None:
                    dfin("b")
                AG(f_c[:, :], f_g[:, :], B["f_c"], B["f_g"])
                na_halo_exchange()
                k.flush()

        def fourier(l, last):
            k.stage = "fourier"
            with ExitStack() as s2:
                sb = lambda n, shp, dt: s2.enter_context(nc.sbuf_tensor(U("sb_" + n), shp, dt))
                d128 = sb("d128", [128, 2, 128], BF16); c64 = sb("c64", [128, 128], BF16); s64n = sb("s64n", [128, 128], BF16)
                dc256 = sb("dc256", [128, 2, 512], BF16)
                wfb = sb("wfb", [128, 2, 256], BF16)
                wcs = sb("wcs", [128, 2, 2, 256], BF16)
                AB = sb("AB", [128, 2, 2, TT], BF16)
                bAB = Buf("AB"); bfc = Buf("fconst"); bwcs = Buf("wcs")
                Xr = Rot(nc, s2, "Xr", [128, 64, 64], BF16, 2)
                Yr = Rot(nc, s2, "Yr", [128, 64, 64], BF16, 2)
                k2r = Rot(nc, s2, "k2r", [128, 64, 32], BF16, 2)
                k.dma("sp", d128[:], I["d128"].rearrange("p (a c) -> p a c", a=2), writes=[bfc])
                k.dma("sp", c64[:], I["c64"][:, :], writes=[bfc])
                k.dma("sp", s64n[:], I["s64n"][:, :], writes=[bfc])
                k.dma("sp", dc256[:], I["dc256"][:, :, :], writes=[bfc])
                wff = sb("wff", [128, 2, 256], F32)
                bwff = Buf("wff")
                k.dma("sp", wff[:], I["w_four"][l].rearrange("(g p) c -> p g c", p=128), writes=[bwff])
                k.op("act", ACT(wfb[:], wff[:], AF.Copy), reads=[bwff], writes=[bfc])
                for gp in range(2):
                    for ab, cm in ((0, c64), (1, s64n)):
                        ps, bps = psf.next()
                        k.op("pe", MM(ps[:, :256], cm[:], wfb[:, gp, :], True, True), reads=[bfc], writes=[bps])
                        k.op("act", ACT(wcs[:, gp, ab, :], ps[:, :256], AF.Copy), reads=[bps], writes=[bwcs])
                if not last:
                    for g in range(4):
                        gp, r0 = g // 2, (g % 2) * 64
                        ps, bps = psf.next()
                        for pt in range(2):
                            k.op("pe", MM(ps[r0:r0 + 64, :], fctx[:, pt, g * 64:(g + 1) * 64], dc256[:, pt, :], pt == 0, pt == 1),
                                 reads=[B["fctx"], bfc], writes=[bps], inc=(pt == 1), pe_accum=True)
                        k.op("act", ACT(AB[r0:r0 + 64, gp, :, TOK:TT], ps[r0:r0 + 64, :].rearrange("p (a t) -> p a t", a=2), AF.Copy),
                             reads=[bps], writes=[bAB])
                ei = 0
                for g in range(4):
                    gp, r0 = g // 2, (g % 2) * 64
                    X, bX = Xr.next()
                    k.dma("sp", X[:], f_g.rearrange("(r w) c -> r w c", w=64)[:, :, g * 64:(g + 1) * 64], reads=[B["f_g"]], writes=[bX])
                    for lq in range(2):
                        Y, bY = Yr.next()
                        for c8 in range(8):
                            ps, bps = psf.next()
                            for ci in range(8):
                                c = c8 * 8 + ci
                                k.op("pe", MM(ps[0:64, ci * 64:(ci + 1) * 64], X[:, :, c], d128[:, 0, lq * 64:(lq + 1) * 64], True, True),
                                     reads=[bX, bfc], writes=[bps], inc=False, pe_accum=True)
                                k.op("pe", MM(ps[64:128, ci * 64:(ci + 1) * 64], X[:, :, c], d128[:, 1, lq * 64:(lq + 1) * 64], True, True),
                                     reads=[bX, bfc], writes=[bps], inc=(ci == 7), pe_accum=True)
                            src = ps[:, :].rearrange("p (c l) -> p c l", c=8)
                            dstY = Y[:, c8 * 8:(c8 + 1) * 8, :]
                            ei += 1
                            if ei % 2:
                                k.op("act", ACT(dstY, src, AF.Copy), reads=[bps], writes=[bY])
                            else:
                                k.op("dve", CP(dstY, src), reads=[bps], writes=[bY])
                        k2, bk2 = k2r.next()
                        k.dma("sp", k2[:], I["k2s"][:, lq * 64:(lq + 1) * 64, :], writes=[bk2])
                        for hb in range(4):
                            ps, bps = psf.next()
                            for li in range(16):
                                l1i = hb * 16 + li
                                k.op("pe", MM(ps[r0:r0 + 64, li * 32:(li + 1) * 32], Y[:, :, l1i], k2[:, l1i, :], True, True),
                                     reads=[bY, bk2], writes=[bps], inc=(li == 15), pe_accum=True)
                            l1s = lq * 64 + hb * 16
                            src = ps[r0:r0 + 64, :].rearrange("p (l a m) -> p a m l", l=16, a=2)
                            dstA = AB[r0:r0 + 64, gp, :, 0:TOK].rearrange("p a (m l) -> p a m l", l=128)[:, :, :, l1s:l1s + 16]
                            k.op("dve", CP(dstA, src), reads=[bps], writes=[bAB])
                for bi, (c0, wd) in enumerate(BLOCKS):
                    if last and bi == 4:
                        continue
                    for mt_ in range(2):
                        ps, bps = psf.next()
                        n_ = 0
                        for gp in range(2):
                            for ab in range(2):
                                k.op("pe", MM(ps[:, :wd], wcs[:, gp, ab, mt_ * 128:(mt_ + 1) * 128], AB[:, gp, ab, c0:c0 + wd], n_ == 0, n_ == 3),
                                     reads=[bwcs, bAB], writes=[bps], inc=(n_ == 3), pe_accum=True)
                                n_ += 1
                        k.op("act", ACT(hy[:, mt_, c0:c0 + wd], ps[:, :wd], AF.Copy), reads=[bps], writes=[bhy[bi]])
                k.flush()

        def retention(l, last):
            k.stage = "retention"
            with ExitStack() as s2:
                sb = lambda n, shp, dt: s2.enter_context(nc.sbuf_tensor(U("sb_" + n), shp, dt))
                lg = sb("lg", [128, 8], F32); kd = sb("kd", [128, 8], F32); g128 = sb("g128", [96, 8], F32)
                cn = sb("cn", [96, 16, 8], F32); cx = sb("cx", [96, 5, 8], F32); qd = sb("qd", [96, 8, 128], F32)
                mt = sb("mt", [128, 4, 128], BF16); gain = sb("gain", [96, 4, 128], F32)
                dmask = sb("dmask", [128, 4, 128], F32); eq = sb("eq", [128, 2, 128], F32); ek = sb("ek", [128, 2], F32)
                ec = sb("ec", [128, 2, 16], F32); exmx = sb("exmx", [128, 2, 5, 2], F32)
                t1 = sb("t1", [128, 128], F32); t2 = sb("t2", [128, 128], F32)
                kdt = sb("kdt", [128, 8, 96], F32)
                tent = sb("tent", [96, 18, 8, 96], BF16)
                s3 = ExitStack()
                Tst = sb("Tst", [96, 8, 96], F32); Sin = sb("Sin", [96, 8, 96], F32); Sctx = sb("Sctx", [96, 8, 96], F32)
                Tst2 = sb("Tst2", [96, 8, 96], F32)
                TT2 = [Tst, Tst2]
                bTT = [[Buf(f"T{i_}_{d_}") for d_ in range(8)] for i_ in range(2)]
                par = [0, 0]
                kchr = Rot(nc, s2, "kch", [96, 4, 128], BF16, 2); qchr = Rot(nc, s2, "qch", [96, 4, 128], BF16, 2)
                gchr = Rot(nc, s2, "gch", [96, 4, 128], BF16, 3); vchr = Rot(nc, s2, "vch", [128, 384], BF16, 3)
                kvb = s3.enter_context(nc.sbuf_tensor(U("sb_kvb"), [96, 18, 4, 96], BF16))
                Sg = s3.enter_context(nc.sbuf_tensor(U("sb_Sg"), [96, 4, 768], F32))
                kfr = Rot(nc, s3, "kf", [128, 4, 96], BF16, 2); kbr = Rot(nc, s3, "kb", [128, 4, 96], BF16, 2)
                bT = Buf("Tst"); btent = Buf("tent"); bkvb = Buf("kvb"); bSin = Buf("Sin"); bSctx = Buf("Sctx"); bSg = Buf("Sg")
                b_cst = Buf("cst"); b_lg = Buf("lg"); b_kd = Buf("kd"); b_g = Buf("g128"); b_cn = Buf("cn"); b_cx = Buf("cx")
                b_qd = Buf("qd"); b_mt = Buf("mt"); b_t1 = Buf("t1"); b_t2 = Buf("t2"); b_kdt = Buf("kdt"); b_gt = Buf("gt"); b_gain = Buf("gain")
                RT = [b_lg, b_kd, b_g, b_cn, b_cx, b_qd, b_mt, b_kdt, b_gain]
                for dst_, nme in ((dmask, "dmask"), (eq, "eq"), (ek, "ek"), (ec, "ec"), (exmx, "exmx")):
                    k.dma("sp", dst_[:], I[nme], writes=[b_cst])
                k.dma("sp", lg[:], I["logit"][l], writes=[b_lg])
                k.dma("sp", gain[:], I["ret_gain"][l], writes=[b_gain])
                k.op("act", ACT(lg[:], lg[:], AF.Exp, scale=-1.0), reads=[b_lg], writes=[b_lg])
                k.op("dve", TS(lg[:], lg[:], 1.0, ALU.add), reads=[b_lg], writes=[b_lg])
                k.op("act", ACT(lg[:], lg[:], AF.Ln), reads=[b_lg], writes=[b_lg])
                k.op("dve", TS(lg[:], lg[:], -1.0, ALU.mult), reads=[b_lg], writes=[b_lg])
                k.op("act", ACT(g128[:], lg[:96, :], AF.Exp, scale=128.0), reads=[b_lg], writes=[b_g])
                k.op("dve", MS(kdt[:], 1.0), writes=[b_kdt])
                for dh in range(8):
                    d_ = dh // 4
                    sc_ = lg[:, dh:dh + 1]
                    k.op("act", ACT(kd[:, dh:dh + 1], ek[:, d_:d_ + 1], AF.Exp, scale=sc_), reads=[b_lg, b_cst], writes=[b_kd])
                    k.op("act", ACT(cn[:, :, dh], ec[:96, d_, :], AF.Exp, scale=lg[:96, dh:dh + 1]), reads=[b_lg, b_cst], writes=[b_cn])
                    k.op("act", ACT(cx[:, :, dh], exmx[:96, 0, :, d_], AF.Exp, scale=lg[:96, dh:dh + 1]), reads=[b_lg, b_cst], writes=[b_cx])
                    k.op("act", ACT(qd[:, dh, :], eq[:96, d_, :], AF.Exp, scale=lg[:96, dh:dh + 1]), reads=[b_lg, b_cst], writes=[b_qd])
                for dh in range(8):
                    d_ = dh // 4
                    k.op("dve", TTo(cx[:, :, dh], cx[:, :, dh], exmx[:96, 1, :, d_], ALU.mult), reads=[b_cx, b_cst], writes=[b_cx])
                    k.op("dve", TS(kdt[:, dh, :], kdt[:, dh, :], kd[:, dh:dh + 1], ALU.mult), reads=[b_kd, b_kdt], writes=[b_kdt])
                for h in range(4):
                    k.op("act", ACT(t1[:], dmask[:, 0, :], AF.Exp, scale=lg[:, h:h + 1]), reads=[b_lg, b_cst], writes=[b_t1])
                    k.op("act", ACT(t2[:], dmask[:, 2, :], AF.Exp, scale=lg[:, 4 + h:5 + h]), reads=[b_lg, b_cst], writes=[b_t2])
                    k.op("dve", TTo(t1[:], t1[:], dmask[:, 1, :], ALU.mult), reads=[b_t1, b_cst], writes=[b_t1])
                    k.op("dve", TTo(t2[:], t2[:], dmask[:, 3, :], ALU.mult), reads=[b_t2, b_cst], writes=[b_t2])
                    k.op("dve", TTo(mt[:, h, :], t1[:], t2[:], ALU.add), reads=[b_t1, b_t2], writes=[b_mt])
                k.op("dve", MS(Tst[:], 0.0), writes=[bT] + bTT[0])
                k.op("dve", MS(Tst2[:], 0.0), writes=bTT[1])
                if stop == "ret_tab":
                    s3.close()
                    k.flush()
                    return

                def tok0(n):
                    return n * 128

                def load_k(n):
                    kc, bkc = kchr.next()
                    k.dma("sp", kc[:], s_k[:, :, tok0(n):tok0(n) + 128].rearrange("h d t -> d h t"), reads=[B["s_k"]], writes=[bkc])
                    return kc, bkc

                def load_v(n):
                    vc, bvc = vchr.next()
                    k.dma("sp", vc[:], s_v[tok0(n):tok0(n) + 128, :], reads=[B["s_v"]], writes=[bvc])
                    return vc, bvc

                def preA(n):
                    kc, bkc = load_k(n)
                    vc, bvc = load_v(n)
                    pb, bpb = psb.next()
                    for h in range(4):
                        k.op("pe", TR(pb[:, h * 96:(h + 1) * 96], kc[:, h, :], identb[:96, :96]), reads=[bkc, bc], writes=[bpb],
                             inc=(h == 3), pe_accum=True)
                    kf, bkf = kfr.next(); kb, bkb = kbr.next()
                    pv_ = pb[:, 0:384].rearrange("p (h d) -> p h d", h=4)
                    k.op("dve", TTo(kf[:], pv_, kdt[:, 0:4, :], ALU.mult), reads=[bpb, *RT], writes=[bkf])
                    k.op("dve", TTo(kb[:], pv_, kdt[:, 4:8, :], ALU.mult), reads=[bpb, *RT], writes=[bkb])
                    return (vc, bvc, kf, bkf, kb, bkb)

                def preB(n, c_):
                    vc, bvc, kf, bkf, kb, bkb = c_
                    if n == 0:
                        cur_ = TT2[par[0]]
                        k.op("act", ACT(Sctx[:, 0:4, :], cur_[:, 0:4, :], AF.Copy), reads=bTT[par[0]][0:4], writes=[bSctx])
                        k.op("dve", MS(cur_[:, 0:4, :], 0.0), reads=[bSctx], writes=bTT[par[0]][0:4])
                    ps1, bps1 = psf.next(); ps2, bps2 = psf.next()
                    for h in range(4):
                        k.op("pe", MM(ps1[:96, h * 96:(h + 1) * 96], kf[:, h, :], vc[:, h * 96:(h + 1) * 96], True, True),
                             reads=[bkf, bvc], writes=[bps1], inc=(h == 3), pe_accum=True)
                    for h in range(4):
                        k.op("pe", MM(ps2[:96, h * 96:(h + 1) * 96], kb[:, h, :], vc[:, h * 96:(h + 1) * 96], True, True),
                             reads=[bkb, bvc], writes=[bps2], inc=(h == 3), pe_accum=True)
                    k.op("act", ACT(kvb[:, n, :, :], ps2[:96, 0:384].rearrange("p (h e) -> p h e", h=4), AF.Copy), reads=[bps2], writes=[bkvb])
                    p_ = par[0]
                    cur_, nxt_ = TT2[p_], TT2[1 - p_]
                    k.op("act", ACT(tent[:, n, 0:4, :], cur_[:, 0:4, :], AF.Copy), reads=bTT[p_][0:4], writes=[btent])
                    for h in range(4):
                        k.op("dve", STT(nxt_[:, h, :], cur_[:, h, :], g128[:, h:h + 1], ps1[:96, h * 96:(h + 1) * 96], ALU.mult, ALU.add),
                             reads=[bTT[p_][h], b_g, bps1], writes=[bTT[1 - p_][h]])
                    par[0] = 1 - p_

                order = [16, 17] + list(range(16))
                pend = preA(order[0])
                for oi, n in enumerate(order):
                    nxt = preA(order[oi + 1]) if oi + 1 < len(order) else None
                    preB(n, pend)
                    pend = nxt
                for n in [17, 16] + list(range(15, -1, -1)):
                    p_ = par[1]
                    cur_, nxt_ = TT2[p_], TT2[1 - p_]
                    if n == 15:
                        k.op("act", ACT(Sctx[:, 4:8, :], cur_[:, 4:8, :], AF.Copy), reads=bTT[p_][4:8], writes=[bSctx])
                        k.op("dve", MS(cur_[:, 4:8, :], 0.0), reads=[bSctx], writes=bTT[p_][4:8])
                    k.op("act", ACT(tent[:, n, 4:8, :], cur_[:, 4:8, :], AF.Copy), reads=bTT[p_][4:8], writes=[btent])
                    for h in range(4):
                        k.op("dve", STT(nxt_[:, 4 + h, :], cur_[:, 4 + h, :], g128[:, 4 + h:5 + h], kvb[:, n, h, :], ALU.mult, ALU.add),
                             reads=[bTT[p_][4 + h], b_g, bkvb], writes=[bTT[1 - p_][4 + h]])
                    par[1] = 1 - p_
                if stop == "ret_pre":
                    k.flush()
                    s3.close()
                    return
                k.dma("sp", st_c[:, 0:384], TT2[par[0]][:, 0:4, :].rearrange("p a e -> p (a e)"), reads=bTT[par[0]][0:4], writes=[B["st_c"]])
                k.dma("sp", st_c[:, 384:768], TT2[par[1]][:, 4:8, :].rearrange("p a e -> p (a e)"), reads=bTT[par[1]][4:8], writes=[B["st_c"]])
                AG(st_c[:, :], st_g[:, :], B["st_c"], B["st_g"])
                k.dma("sp", Sg[:], st_g.rearrange("(j p) c -> p j c", p=96), reads=[B["st_g"]], writes=[bSg])
                bSinL = [Buf(f"Sin{dh}") for dh in range(8)]
                for dh in range(8):
                    k.op("dve", TS(Sin[:, dh, :], Sctx[:, dh, :], cx[:, 4, dh:dh + 1], ALU.mult), reads=[bSctx, *RT], writes=[bSinL[dh]])
                for j in range(4):
                    for dh in range(8):
                        k.op("dve", STT(Sin[:, dh, :], Sg[:, j, dh * 96:(dh + 1) * 96], cx[:, j, dh:dh + 1], Sin[:, dh, :], ALU.mult, ALU.add),
                             reads=[bSg, *RT, bSinL[dh]], writes=[bSinL[dh]])
                k.op("dve", MS(epsc[:], EPS), reads=bSinL, writes=[bSin, bc])
                k.flush()
                s3.close()
                if stop == "ret_ag":
                    return
                Sfr = Rot(nc, s2, "Sf", [96, 8, 96], BF16, 3)
                qsr = Rot(nc, s2, "qs", [96, 8, 128], BF16, 3)
                Pr = Rot(nc, s2, "Pr", [128, 4, 128], BF16, 3)
                sqr2 = Rot(nc, s2, "sq2", [96, 512], BF16, 2)
                rs2 = Rot(nc, s2, "rs2", [96, 512], F32, 2)
                y1r = Rot(nc, s2, "y1r", [96, 512], F32, 2)
                y2r = Rot(nc, s2, "y2r", [96, 512], F32, 2)
                chunks = list(range(16)) + ([] if last else [16, 17])

                def phaseA(n):
                    kc, bkc = load_k(n)
                    vc, bvc = load_v(n)
                    qc, bqc = qchr.next()
                    k.dma("sp", qc[:], s_q[:, :, tok0(n):tok0(n) + 128].rearrange("h d t -> d h t"), reads=[B["s_q"]], writes=[bqc])
                    gc, bgc = gchr.next()
                    k.dma("sp", gc[:], s_g[:, :, tok0(n):tok0(n) + 128].rearrange("h d t -> d h t"), reads=[B["s_g"]], writes=[bgc])
                    S, bS = Sfr.next()
                    if n < 16:
                        for dh in range(8):
                            k.op("dve", STT(S[:, dh, :], Sin[:, dh, :], cn[:, n, dh:dh + 1], tent[:, n, dh, :], ALU.mult, ALU.add),
                                 reads=[bSin, *RT, btent], writes=[bS])
                    else:
                        k.op("act", ACT(S[:], tent[:, n, :, :], AF.Copy), reads=[btent], writes=[bS])
                    qs, bqs = qsr.next()
                    k.op("dve", TTo(qs[:, 0:4, :], qc[:], qd[:, 0:4, :], ALU.mult), reads=[bqc, *RT], writes=[bqs])
                    k.op("pool", TTo(qs[:, 4:8, :], qc[:], qd[:, 4:8, :], ALU.mult), reads=[bqc, *RT], writes=[bqs])
                    ps, bps = psf.next()
                    for h in range(4):
                        k.op("pe", MM(ps[:, h * 128:(h + 1) * 128], kc[:, h, :], qc[:, h, :], True, True), reads=[bkc, bqc], writes=[bps],
                             inc=(h == 3), pe_accum=True)
                    P, bP = Pr.next()
                    k.op("dve", TTo(P[:], ps[:].rearrange("p (h i) -> p h i", h=4), mt[:], ALU.mult), reads=[bps, *RT], writes=[bP])
                    return (vc, bvc, gc, bgc, S, bS, qs, bqs, P, bP)

                def phaseB(n, ctx_):
                    vc, bvc, gc, bgc, S, bS, qs, bqs, P, bP = ctx_
                    bi = blk_of(tok0(n))
                    po, bpo = psl.next()
                    for h in range(4):
                        o_ = po[:96, h * 128:(h + 1) * 128]
                        k.op("pe", MM(o_, vc[:, h * 96:(h + 1) * 96], P[:, h, :], True, False), reads=[bvc, bP], writes=[bpo], inc=False, pe_accum=True)
                        k.op("pe", MM(o_, S[:, h, :], qs[:, h, :], False, False), reads=[bS, bqs], writes=[bpo], inc=False, pe_accum=True)
                        k.op("pe", MM(o_, S[:, 4 + h, :], qs[:, 4 + h, :], False, True), reads=[bS, bqs], writes=[bpo], inc=(h == 3), pe_accum=True)
                    sq, bsq = sqr2.next()
                    k.op("act", ACT(sq[:], po[:96, :], AF.Square), reads=[bpo], writes=[bsq])
                    pss, bpss = psf.next()
                    k.op("pe", MM(pss[:96, :], onesb[:96, :96], sq[:], True, True), reads=[bsq, bc], writes=[bpss])
                    rs, brs = rs2.next()
                    k.op("act", ACT(rs[:], pss[:96, :], AF.Ln, scale=1.0 / 96, bias=epsc[:96, :]), reads=[bpss, bc], writes=[brs])
                    k.op("act", ACT(rs[:], rs[:], AF.Exp, scale=-0.5), reads=[brs], writes=[brs])
                    y1, by1 = y1r.next()
                    k.op("dve", TTo(y1[:], po[:96, :], rs[:], ALU.mult), reads=[bpo, brs], writes=[by1])
                    y2, by2 = y2r.next()
                    k.op("pool", TTo(y2[:], y1[:], gain[:].rearrange("p h i -> p (h i)"), ALU.mult), reads=[by1, *RT], writes=[by2])
                    k.op("pool", TTo(hy[:96, 2:6, tok0(n):tok0(n) + 128], y2[:].rearrange("p (h i) -> p h i", h=4), gc[:], ALU.mult),
                         reads=[by2, bgc], writes=[bhy[bi]])

                pq = [phaseA(chunks[0]), phaseA(chunks[1])]
                for ci, n in enumerate(chunks):
                    if ci + 2 < len(chunks):
                        pq.append(phaseA(chunks[ci + 2]))
                    phaseB(n, pq.pop(0))
                k.flush()

        def natten(l, last):
            k.stage = "natten"
            with ExitStack() as s2:
                sb = lambda n, shp, dt: s2.enter_context(nc.sbuf_tensor(U("sb_" + n), shp, dt))
                nkf = sb("nkf", [128, 3, 2560], BF16); nvf = sb("nvf", [128, 20, 390], BF16)
                nkc = sb("nkc", [128, 3, 256], BF16); nvc = sb("nvc", [128, 2, 390], BF16)
                bint = sb("bint", [128, 6, 5, 128], BF16); mh = sb("mh", [128, 8], F32)
                bnk = Buf("nkf"); bnv = Buf("nvf"); bctx = Buf("nctx"); bbi = Buf("bint"); bmh = Buf("mh")
                k.dma("sp", mh[:], I["mh"][:, :], writes=[bmh])
                k.dma("pool", bint[:], I["bias_int"][l], writes=[bbi], max_dma_last_dim=2048)
                k.dma("sp", nkf[:, :, 256:2304], s_nk[:, :, 0:TOK].rearrange("h p t -> p h t"), reads=[B["s_nk"]], writes=[bnk])
                k.dma("sp", nvf[:, 2:18, :], s_nv[0:TOK, :].rearrange("(t p) c -> p t c", p=128), reads=[B["s_nv"]], writes=[bnv])
                k.dma("sp", nkc[:], s_nk[:, :, TOK:TT].rearrange("h p t -> p h t"), reads=[B["s_nk"]], writes=[bctx])
                k.dma("sp", nvc[:], s_nv[TOK:TT, :].rearrange("(t p) c -> p t c", p=128), reads=[B["s_nv"]], writes=[bctx])
                with ExitStack() as s3:
                    keg = s3.enter_context(nc.sbuf_tensor(U("sb_keg"), [128, 4, 3, 448], BF16))
                    veg = s3.enter_context(nc.sbuf_tensor(U("sb_veg"), [128, 4, 4, 390], BF16))
                    bkeg = Buf("keg"); bveg = Buf("veg")
                    k.op("pool", MS(veg[:], 0.0), writes=[bveg])
                    k.op("pool", MS(nkf[:, :, 2496:2560], 0.0), writes=[bnk])
                    k.dma("sp", keg[:], ke_g.rearrange("(j h p) t -> p j h t", j=4, h=3), reads=[B["ke_g"]], writes=[bkeg])
                    vg4 = ve_g.rearrange("(j r) c -> j r c", j=4)
                    for j in range(4):
                        k.dma("sp", veg[:, j, 0, :], vg4[j, 0:128, :], reads=[B["ve_g"]], writes=[bveg])
                        k.dma("sp", veg[0:64, j, 1, :], vg4[j, 128:192, :], reads=[B["ve_g"]], writes=[bveg])
                        k.dma("sp", veg[:, j, 2:4, :], vg4[j, 192:448, :].rearrange("(t p) c -> p t c", p=128), reads=[B["ve_g"]], writes=[bveg])
                    for (dstk, srck, dstv, srcv, m0) in (
                            (nkf[:, :, 0:256], lambda j: keg[:, j, :, 192:448], nvf[:, 0:2, :], lambda j: veg[:, j, 2:4, :], 0),
                            (nkf[:, :, 2304:2496], lambda j: keg[:, j, :, 0:192], nvf[:, 18:20, :], lambda j: veg[:, j, 0:2, :], 4)):
                        k.op("dve", TS(dstk, srck(0), mh[:, m0:m0 + 1], ALU.mult), reads=[bkeg, bmh], writes=[bnk])
                        k.op("dve", TS(dstv, srcv(0), mh[:, m0:m0 + 1], ALU.mult), reads=[bveg, bmh], writes=[bnv])
                        for j in range(1, 4):
                            k.op("dve", STT(dstk, srck(j), mh[:, m0 + j:m0 + j + 1], dstk, ALU.mult, ALU.add), reads=[bkeg, bmh, bnk], writes=[bnk])
                            k.op("dve", STT(dstv, srcv(j), mh[:, m0 + j:m0 + j + 1], dstv, ALU.mult, ALU.add), reads=[bveg, bmh, bnv], writes=[bnv])
                    k.flush()
                wo = s2.enter_context(nc.sbuf_tensor(U("sb_wo"), [128, 9, D], BF16))
                bwo = Buf("wo")
                wsrc = I["w_out"][l]
                k.dma("pool", wo[:, 0:2, :], wsrc[0:256, :].rearrange("(s p) c -> p s c", p=128), writes=[bwo])
                k.dma("pool", wo[:96, 2:6, :], wsrc[256:640, :].rearrange("(s p) c -> p s c", p=96), writes=[bwo])
                k.dma("pool", wo[:, 6:9, :], wsrc[640:1024, :].rearrange("(s p) c -> p s c", p=128), writes=[bwo])
                nqr = Rot(nc, s2, "nqt", [128, 3, 128], BF16, 4)
                ber = Rot(nc, s2, "bedge", [128, 2, 6, 128], BF16, 4)
                ssr = Rot(nc, s2, "ssb", [128, 6, 128], F32, 3)
                Pr = Rot(nc, s2, "Pn", [128, 8, 128], BF16, 3)
                otr = Rot(nc, s2, "otok", [128, 384], BF16, 2)
                rdr = Rot(nc, s2, "rden", [128, 6], F32, 2)
                tiles = list(range(16)) + ([] if last else [16, 17])
                units = [(t, hp, hh) for t in tiles for hp in range(3) for hh in range(2)]
                tstate = {}
                sbanks = list(psf.items) + list(psl.items)
                sb_i = [0]
                pso_fix = (psb.items[0][0][:, :].bitcast(F32), psb.items[0][1])
                ptr_fix = psb.items[1]

                def next_bank():
                    it = sbanks[sb_i[0] % 6]
                    sb_i[0] += 1
                    return it

                def tile_cfg(t):
                    if t >= 16:
                        return 0, 0, None
                    if t == 0:
                        return 6, 0, 0
                    if t == 15:
                        return 6, 1792, 3
                    return 5, 128 * t, {1: 1, 14: 2}.get(t)

                def scoresU(u):
                    t, hp, hh = u
                    nkt, kb0, edge = tile_cfg(t)
                    t0_ = t * 128
                    if hp == 0 and hh == 0:
                        nq, bnq = nqr.next()
                        k.dma("sp", nq[:], s_nq[:, :, t0_:t0_ + 128].rearrange("h p t -> p h t"), reads=[B["s_nq"]], writes=[bnq])
                        tstate[t] = {"nq": (nq, bnq)}
                    ts_ = tstate[t]
                    nq, bnq = ts_["nq"]
                    if edge is not None and hh == 0:
                        be, bbe = ber.next()
                        k.dma("pool", be[:], I["bias_edge"][l][edge][:, 2 * hp:2 * hp + 2, :, :], writes=[bbe], max_dma_last_dim=2048)
                        ts_["be"] = (be, bbe)
                    r0 = 64 * hh
                    pA, bpA = next_bank(); pB, bpB = next_bank()
                    for kt in range(nkt):
                        bank, bbank = (pA, bpA) if kt < 4 else (pB, bpB)
                        sl = kt % 4
                        k.op("pe", MM(bank[:, sl * 128:(sl + 1) * 128], nkf[r0:r0 + 64, hp, kb0 + kt * 128:kb0 + (kt + 1) * 128],
                                      nq[r0:r0 + 64, hp, :], True, True), reads=[bnk, bnq], writes=[bbank], pe_accum=True)
                    for c in range(2):
                        k.op("pe", MM(pB[:, (2 + c) * 128:(3 + c) * 128], nkc[r0:r0 + 64, hp, c * 128:(c + 1) * 128], nq[r0:r0 + 64, hp, :], True, True),
                             reads=[bctx, bnq], writes=[bpB], pe_accum=True)
                    return (pA, bpA, pB, bpB, ts_.get("be"))

                def restU(u, sc_):
                    t, hp, hh = u
                    pA, bpA, pB, bpB, be_ = sc_
                    nkt, kb0, edge = tile_cfg(t)
                    t0_ = t * 128
                    bi = blk_of(t0_)
                    h = 2 * hp + hh
                    ts_ = tstate[t]
                    pso, bpso = pso_fix
                    P, bP = Pr.next()
                    if nkt > 0:
                        ss, bss = ssr.next()
                        if edge is not None:
                            bsrc, bbuf = be_[0][:, hh, :, :], be_[1]
                        else:
                            bsrc, bbuf = bint[:, h, :, :], bbi
                        k.op("dve", TTo(ss[:, 0:4, :], pA[:].rearrange("p (s q) -> p s q", s=4), bsrc[:, 0:4, :], ALU.add),
                             reads=[bpA, bbuf], writes=[bss])
                        k.op("dve", TTo(ss[:, 4:nkt, :], pB[:, 0:(nkt - 4) * 128].rearrange("p (s q) -> p s q", q=128), bsrc[:, 4:nkt, :], ALU.add),
                             reads=[bpB, bbuf], writes=[bss])
                        k.op("act", ACT(P[:, 0:nkt, :], ss[:, 0:nkt, :], AF.Exp), reads=[bss], writes=[bP])
                    k.op("act", ACT(P[:, 6:8, :], pB[:, 256:512].rearrange("p (s q) -> p s q", s=2), AF.Exp), reads=[bpB], writes=[bP])
                    o_ = pso[:, h * 65:(h + 1) * 65]
                    for kt in range(nkt):
                        k.op("pe", MM(o_, P[:, kt, :], nvf[:, kb0 // 128 + kt, h * 65:(h + 1) * 65], kt == 0, False),
                             reads=[bP, bnv], writes=[bpso], inc=False, pe_accum=True)
                    for c in range(2):
                        k.op("pe", MM(o_, P[:, 6 + c, :], nvc[:, c, h * 65:(h + 1) * 65], (nkt == 0 and c == 0), c == 1),
                             reads=[bP, bctx], writes=[bpso], inc=(c == 1), pe_accum=True)
                    if hp == 2 and hh == 1:
                        rd, brd = rdr.next()
                        k.op("dve", RCP(rd[:], pso[:, 0:390].rearrange("p (h c) -> p h c", c=65)[:, :, 64]), reads=[bpso], writes=[brd])
                        ot, bot = otr.next()
                        for h2 in range(6):
                            k.op("dve", TS(ot[:, h2 * 64:(h2 + 1) * 64], pso[:, h2 * 65:h2 * 65 + 64], rd[:, h2:h2 + 1], ALU.mult), reads=[bpso, brd], writes=[bot])
                        pb, bpb = ptr_fix
                        for hp2 in range(3):
                            k.op("pe", TR(pb[:, hp2 * 128:(hp2 + 1) * 128], ot[:, hp2 * 128:(hp2 + 1) * 128], identb[:]), reads=[bot, bc], writes=[bpb],
                                 inc=(hp2 == 2), pe_accum=True)
                        k.op("act", ACT(hy[:, 6:9, t0_:t0_ + 128], pb[:, 0:384].rearrange("p (s q) -> p s q", s=3), AF.Copy), reads=[bpb], writes=[bhy[bi]])
                        del tstate[t]

                pendq = [scoresU(units[0])]
                if len(units) > 1:
                    pendq.append(scoresU(units[1]))
                for ui, u in enumerate(units):
                    if ui + 2 < len(units):
                        pendq.append(scoresU(units[ui + 2]))
                    restU(u, pendq.pop(0))
                if stop != "na":
                    k.stage = "wout"
                    wout_body(l, last, wo, bwo)
                k.flush()

        def wout_body(l, last, wo, bwo):
            KS = [128, 128, 96, 96, 96, 96, 128, 128, 128]
            for bi, (c0, wd) in enumerate(BLOCKS):
                if last and bi == 4:
                    continue
                j = 0 if bi < 4 else 1
                for ct in range(8):
                    ps, bps = psf.next()
                    for s_ in range(9):
                        K_ = KS[s_]
                        k.op("pe", MM(ps[:, :wd], wo[:K_, s_, ct * 128:(ct + 1) * 128], hy[:K_, s_, c0:c0 + wd], s_ == 0, s_ == 8),
                             reads=[bwo, bhy[bi]], writes=[bps], inc=(s_ == 8), pe_accum=True)
                    k.op("dve", STT(xT[:, ct, c0:c0 + wd], ps[:, :wd], mod[:, 16 + ct, j:j + 1], xT[:, ct, c0:c0 + wd], ALU.mult, ALU.add),
                         reads=[bps, B["mod"], bx[bi]], writes=[bx[bi]])

        def ffn(l, last):
            k.stage = "ffn"
            blocks = [bi for bi in range(5) if not (last and bi == 4)]
            with ExitStack() as s2:
                w2r = s2.enter_context(nc.sbuf_tensor(U("sb_w2r"), [128, 22, D], BF16))
                bw2 = Buf("w2r")
                w13 = Rot(nc, s2, "w13", [128, 8, 256], BF16, 2)
                slr = Rot(nc, s2, "sil", [128, 512], F32, 2)
                ust = Rot(nc, s2, "ust", [128, 512], BF16, 3)
                ubr = Rot(nc, s2, "ub", [128, 22, 512], BF16, 1)
                nxt_ada = None
                if l + 1 < nlayers:
                    wrot = Rot(nc, s2, "wada", [128, 8, 256], BF16, 2)
                    nxt_ada = ada_steps(l + 1, wrot)
                ai = 0
                for ft in range(22):
                    w, bw = w13.next()
                    k.dma("pool", w[:, :, 0:128], I["w1"][l][:, ft * 128:(ft + 1) * 128].rearrange("(kt p) c -> p kt c", p=128), writes=[bw])
                    k.dma("pool", w[:, :, 128:256], I["w3"][l][:, ft * 128:(ft + 1) * 128].rearrange("(kt p) c -> p kt c", p=128), writes=[bw])
                    if ft in (2, 6, 10, 14):
                        cq = (ft - 2) // 4
                        k.dma("pool", w2r[:, :, cq * 256:(cq + 1) * 256],
                              I["w2"][l][:, cq * 256:(cq + 1) * 256].rearrange("(kt p) c -> p kt c", p=128), writes=[bw2])
                    for bi in blocks:
                        c0, wd = BLOCKS[bi]
                        pa, bpa = psf.next(); pb_, bpb_ = psf.next()
                        for kt in range(8):
                            k.op("pe", MM(pa[:, :wd], w[:, kt, 0:128], hy[:, kt, c0:c0 + wd], kt == 0, kt == 7), reads=[bw, bhy[bi]], writes=[bpa],
                                 inc=(kt == 7), pe_accum=True)
                        for kt in range(8):
                            k.op("pe", MM(pb_[:, :wd], w[:, kt, 128:256], hy[:, kt, c0:c0 + wd], kt == 0, kt == 7), reads=[bw, bhy[bi]], writes=[bpb_],
                                 inc=(kt == 7), pe_accum=True)
                        sl, bsl = slr.next()
                        k.op("act", ACT(sl[:, :wd], pa[:, :wd], AF.Silu), reads=[bpa], writes=[bsl])
                        u, bu = ust.next()
                        k.op("dve", TTo(u[:, :wd], sl[:, :wd], pb_[:, :wd], ALU.mult), reads=[bsl, bpb_], writes=[bu])
                        k.dma("sp", s_u[ft][:, c0:c0 + wd], u[:, :wd], reads=[bu], writes=[B["s_u"]])
                    if nxt_ada is not None and ai < 24:
                        nxt_ada[0][ai](); ai += 1
                ub_a, bub_a = ubr.next()
                ub_h = hy[:].rearrange("p s t -> p (s t)")[:, 0:22 * 512].rearrange("p (f t) -> p f t", f=22)
                for ii_, bi in enumerate(blocks):
                    c0, wd = BLOCKS[bi]
                    j = 0 if bi < 4 else 1
                    if ii_ % 2 == 0:
                        ub, rd_b, wr_b = ub_a, [bub_a], [bub_a]
                    else:
                        ub, rd_b, wr_b = ub_h, list(bhy), list(bhy)
                    k.dma("sp", ub[:, :, :wd], s_u[:, :, c0:c0 + wd].rearrange("f p t -> p f t"), reads=[B["s_u"]], writes=wr_b)
                    for ct in range(8):
                        ps, bps = psf.next()
                        for ft in range(22):
                            k.op("pe", MM(ps[:, :wd], w2r[:, ft, ct * 128:(ct + 1) * 128], ub[:, ft, :wd], ft == 0, ft == 21),
                                 reads=[bw2] + rd_b, writes=[bps], inc=(ft == 21), pe_accum=True)
                        k.op("dve", STT(xT[:, ct, c0:c0 + wd], ps[:, :wd], mod[:, 40 + ct, j:j + 1], xT[:, ct, c0:c0 + wd], ALU.mult, ALU.add),
                             reads=[bps, B["mod"], bx[bi]], writes=[bx[bi]])
                    if nxt_ada is not None and ai < 24:
                        nxt_ada[0][ai](); ai += 1
                if nxt_ada is not None:
                    while ai < 24:
                        nxt_ada[0][ai](); ai += 1
                    nxt_ada[1]()
                k.flush()

        def final_out():
            k.stage = "final_out"
            with ExitStack() as s2:
                norm(0, range(4), final=True, scope=s2)
                otl = Rot(nc, s2, "otile", [128, D], F32, 2)
                for t in range(16):
                    bi = blk_of(t * 128)
                    o, bo = otl.next()
                    for half in range(2):
                        ps, bps = psf.next()
                        for j in range(4):
                            kt = half * 4 + j
                            k.op("pe", TR(ps[:, j * 128:(j + 1) * 128], xT[:, kt, t * 128:(t + 1) * 128], identf[:]), reads=[bx[bi], bc], writes=[bps],
                                 inc=(j == 3), pe_accum=True)
                        if half == 0:
                            k.op("act", ACT(o[:, 0:512], ps[:], AF.Copy), reads=[bps], writes=[bo])
                        else:
                            k.op("dve", CP(o[:, 512:1024], ps[:]), reads=[bps], writes=[bo])
                    k.dma("sp", out_d[t * 128:(t + 1) * 128, :], o[:], reads=[bo], writes=[B["out"]])
                k.flush()

        done = False
        for l in range(nlayers):
            last = (l == DEPTH - 1)
            deferred_ada = None
            if only is None and l == 0:
                deferred_ada = ada(l)
            if debug and l == 0 and only is None:
                k.dma("sp", d_mod[:, :, :], mod[:], reads=[B["mod"]])
            if stop == "norm":
                norm(0, range(5))
                break
            if only is None:
                project(l, last, deferred=deferred_ada)
            if stop == "proj":
                break
            if only is None:
                fourier(l, last)
            if stop == "four":
                break
            if only in (None, "ret"):
                retention(l, last)
            if stop in ("ret", "ret_tab", "ret_pre", "ret_ag"):
                break
            natten(l, last)
            if stop == "na":
                break
            if stop == "wout":
                break
            norm(1, range(4) if last else range(5))
            ffn(l, last)
            if stop == "ffn":
                break
        else:
            if nlayers == DEPTH:
                final_out()
        if debug and only is None:
            k.dma("sp", d_xT[:, :, :], xT[:], reads=bx)
            k.dma("sp", d_hy[:, :, :], hy[:], reads=bhy)
        k.flush(final=True)
    return nc


_NC_CACHE = {}


def kernel(**inputs):
    inp = {k_: np.asarray(v) for k_, v in inputs.items()}
    maps = make_in_maps(inp)
    if "nc" not in _NC_CACHE:
        _NC_CACHE["nc"] = build()
    res = run_bass_kernel_spmd(_NC_CACHE["nc"], maps, core_ids=list(range(8)))
    out = np.empty((2, L, D), np.float32)
    for core in range(8):
        b, q = core // 4, core % 4
        out[b, TOK * q:TOK * (q + 1)] = np.asarray(res.results[core]["out"], np.float32)
    return out
```
